# Optimizing a Trainium2 kernel written in Bass

```python
import jax, jax.numpy as jnp
from jax import lax
import numpy as np

D_MODEL = 2048
BATCH = 2
SEQ = 4096
DEPTH = 2
DEC_BATCH = 128
DEC_SEQ = 8
PAST_LEN = 8192
PAGE_SIZE = 128

HEAD_DIM = 64
N_HEADS = 16
N_KV_HEADS = 4
GROUP = N_HEADS // N_KV_HEADS
D_ATTN = N_HEADS * HEAD_DIM
D_KV = N_KV_HEADS * HEAD_DIM
WINDOW = 128
BLOCK = WINDOW
D_CONV = D_MODEL // 2
CONV_K = 31
D_FF = 3 * D_MODEL
FFN_K = 3
EPS = 1e-6
SPLITS = (D_CONV, 2 * D_CONV, 2 * D_CONV + D_ATTN, 2 * D_CONV + D_ATTN + D_KV,
          2 * D_CONV + D_ATTN + 2 * D_KV, 2 * D_CONV + D_ATTN + 2 * D_KV + D_MODEL)
IN_COLS = SPLITS[-1] + D_MODEL
NEG = -1e30

kernel_name = "hybrid_conformer_conv_swa_sink_gated_decoder_step"


def _rmsnorm(x, g):
    xf = x.astype(jnp.float32)
    xf = xf * lax.rsqrt(jnp.mean(xf * xf, axis=-1, keepdims=True) + EPS)
    return (xf * g.astype(jnp.float32)).astype(x.dtype)


def _layernorm(x, g, b):
    xf = x.astype(jnp.float32)
    mu = jnp.mean(xf, axis=-1, keepdims=True)
    xc = xf - mu
    xf = xc * lax.rsqrt(jnp.mean(xc * xc, axis=-1, keepdims=True) + EPS)
    return (xf * g.astype(jnp.float32) + b.astype(jnp.float32)).astype(x.dtype)


def _causal_dwconv(u, buf, w, b):
    xx = jnp.concatenate([buf.astype(u.dtype), u], axis=1)
    out = lax.conv_general_dilated(
        xx, w[:, None, :].astype(u.dtype), window_strides=(1,), padding='VALID',
        dimension_numbers=('NWC', 'WIO', 'NWC'), feature_group_count=u.shape[-1])
    return out + b.astype(u.dtype), xx[:, -(w.shape[0] - 1):]


def _alibi_slopes():
    return jnp.exp2(-8.0 * jnp.arange(1, N_HEADS + 1, dtype=jnp.float32) / N_HEADS)


def _sink_attend(q, k, v, q_pos, k_pos, sinks):
    lead = q.shape[:-3]
    tq = q.shape[-3]
    qg = q.reshape(*lead, tq, N_KV_HEADS, GROUP, HEAD_DIM)
    s = jnp.einsum('...qkgd,...skd->...kgqs', qg, k).astype(jnp.float32) * (HEAD_DIM ** -0.5)
    dist = q_pos[..., :, None] - k_pos[..., None, :]
    valid = (dist >= 0) & (dist <= WINDOW) & (k_pos[..., None, :] >= 0)
    slopes = _alibi_slopes().reshape(N_KV_HEADS, GROUP, 1, 1)
    s = s - slopes * dist[..., None, None, :, :].astype(jnp.float32)
    s = jnp.where(valid[..., None, None, :, :], s, NEG)
    sink = sinks.astype(jnp.float32).reshape(N_KV_HEADS, GROUP, 1, 1)
    m = jnp.maximum(jnp.max(s, axis=-1, keepdims=True), sink)
    p = jnp.exp(s - m)
    denom = jnp.sum(p, axis=-1, keepdims=True) + jnp.exp(sink - m)
    o = jnp.einsum('...kgqs,...skd->...qkgd', (p / denom).astype(v.dtype), v)
    return o.reshape(*lead, tq, D_ATTN)


def _banded_attention(q, k, v, sinks):
    n, t = q.shape[:2]
    nb = t // BLOCK
    qb = q.reshape(n, nb, BLOCK, N_HEADS, HEAD_DIM)
    kb = k.reshape(n, nb, BLOCK, N_KV_HEADS, HEAD_DIM)
    vb = v.reshape(n, nb, BLOCK, N_KV_HEADS, HEAD_DIM)
    pad = jnp.zeros_like(kb[:, :1])
    kk = jnp.concatenate([jnp.concatenate([pad, kb[:, :-1]], axis=1), kb], axis=2)
    vv = jnp.concatenate([jnp.concatenate([pad, vb[:, :-1]], axis=1), vb], axis=2)
    starts = jnp.arange(nb, dtype=jnp.int32)[:, None] * BLOCK
    q_pos = starts + jnp.arange(BLOCK, dtype=jnp.int32)[None, :]
    k_pos = starts - BLOCK + jnp.arange(2 * BLOCK, dtype=jnp.int32)[None, :]
    o = _sink_attend(qb, kk, vv, q_pos, k_pos, sinks)
    return o.reshape(n, t, D_ATTN)


def _layer(x, conv_buf, k_buf, v_buf, ffn_buf, p, prompt):
    (norm1_g, w_in, conv_w, conv_b, conv_ln_g, conv_ln_b, w_conv_out, attn_sinks,
     w_attn_out, w_out, norm2_g, w_up, ffn_conv_w, ffn_conv_b, w_down) = p
    n, t = x.shape[:2]
    h = _rmsnorm(x, norm1_g)
    proj = h @ w_in
    glu_a, glu_g, q, k, v, gate_c, gate_a = jnp.split(proj, SPLITS, axis=-1)
    u = glu_a * jax.nn.sigmoid(glu_g)
    c, new_conv = _causal_dwconv(u, conv_buf, conv_w, conv_b)
    c = jax.nn.silu(_layernorm(c, conv_ln_g, conv_ln_b))
    branch_c = c @ w_conv_out
    q = q.reshape(n, t, N_HEADS, HEAD_DIM)
    k = k.reshape(n, t, N_KV_HEADS, HEAD_DIM)
    v = v.reshape(n, t, N_KV_HEADS, HEAD_DIM)
    if prompt:
        a = _banded_attention(q, k, v, attn_sinks)
        new_k, new_v = k[:, -WINDOW:], v[:, -WINDOW:]
    else:
        kk = jnp.concatenate([k_buf.astype(k.dtype), k], axis=1)
        vv = jnp.concatenate([v_buf.astype(v.dtype), v], axis=1)
        q_pos = PAST_LEN + jnp.arange(t, dtype=jnp.int32)
        k_pos = PAST_LEN - WINDOW + jnp.arange(WINDOW + t, dtype=jnp.int32)
        a = _sink_attend(q, kk, vv, q_pos, k_pos, attn_sinks)
        new_k, new_v = kk[:, -WINDOW:], vv[:, -WINDOW:]
    branch_a = a @ w_attn_out
    merged = jax.nn.sigmoid(gate_c) * branch_c + jax.nn.sigmoid(gate_a) * branch_a
    x = x + merged @ w_out
    h2 = _rmsnorm(x, norm2_g)
    up, new_ffn = _causal_dwconv(h2 @ w_up, ffn_buf, ffn_conv_w, ffn_conv_b)
    g, val = jnp.split(up, 2, axis=-1)
    x = x + (jax.nn.silu(g) * val) @ w_down
    return x, new_conv, new_k, new_v, new_ffn


def setup_inputs(seed: int = 0) -> dict:
    key = jax.random.key(seed)
    ks = jax.random.split(key, 24)
    f = jnp.float32
    nrm = lambda k, shape, s: jax.random.normal(k, shape, f) * s
    ffn_w = jnp.zeros((DEPTH, FFN_K, 2 * D_FF), f).at[:, -1].set(1.0) + nrm(ks[18], (DEPTH, FFN_K, 2 * D_FF), 0.1)
    return {
        "x_prompt": nrm(ks[0], (BATCH, SEQ, D_MODEL), 1.0),
        "x_sample": nrm(ks[1], (DEC_BATCH, DEC_SEQ, D_MODEL), 1.0),
        "cache_k": nrm(ks[2], (DEPTH, DEC_BATCH, WINDOW, N_KV_HEADS, HEAD_DIM), 1.0),
        "cache_v": nrm(ks[3], (DEPTH, DEC_BATCH, WINDOW, N_KV_HEADS, HEAD_DIM), 1.0),
        "state_conv": nrm(ks[4], (DEPTH, DEC_BATCH, CONV_K - 1, D_CONV), 0.5),
        "state_ffn_conv": nrm(ks[5], (DEPTH, DEC_BATCH, FFN_K - 1, 2 * D_FF), 1.0),
        "norm1_g": 1.0 + nrm(ks[6], (DEPTH, D_MODEL), 0.02),
        "w_in": nrm(ks[7], (DEPTH, D_MODEL, IN_COLS), D_MODEL ** -0.5),
        "conv_w": nrm(ks[8], (DEPTH, CONV_K, D_CONV), CONV_K ** -0.5),
        "conv_b": nrm(ks[9], (DEPTH, D_CONV), 0.02),
        "conv_ln_g": 1.0 + nrm(ks[10], (DEPTH, D_CONV), 0.02),
        "conv_ln_b": nrm(ks[11], (DEPTH, D_CONV), 0.02),
        "w_conv_out": nrm(ks[12], (DEPTH, D_CONV, D_MODEL), D_CONV ** -0.5),
        "attn_sinks": nrm(ks[13], (DEPTH, N_HEADS), 1.0),
        "w_attn_out": nrm(ks[14], (DEPTH, D_ATTN, D_MODEL), D_ATTN ** -0.5),
        "w_out": nrm(ks[15], (DEPTH, D_MODEL, D_MODEL), D_MODEL ** -0.5),
        "norm2_g": 1.0 + nrm(ks[16], (DEPTH, D_MODEL), 0.02),
        "w_up": nrm(ks[17], (DEPTH, D_MODEL, 2 * D_FF), D_MODEL ** -0.5),
        "ffn_conv_w": ffn_w,
        "ffn_conv_b": nrm(ks[19], (DEPTH, 2 * D_FF), 0.02),
        "w_down": nrm(ks[20], (DEPTH, D_FF, D_MODEL), D_FF ** -0.5),
        "final_norm_g": 1.0 + nrm(ks[21], (D_MODEL,), 0.02),
    }


def reference(x_prompt, x_sample, cache_k, cache_v, state_conv, state_ffn_conv,
              norm1_g, w_in, conv_w, conv_b, conv_ln_g, conv_ln_b, w_conv_out,
              attn_sinks, w_attn_out, w_out, norm2_g, w_up, ffn_conv_w, ffn_conv_b,
              w_down, final_norm_g):
    xp, xs = x_prompt, x_sample
    kp_l, vp_l, cp_l, fp_l = [], [], [], []
    ks_l, vs_l, cs_l, fs_l = [], [], [], []
    for l in range(DEPTH):
        p = (norm1_g[l], w_in[l], conv_w[l], conv_b[l], conv_ln_g[l], conv_ln_b[l],
             w_conv_out[l], attn_sinks[l], w_attn_out[l], w_out[l], norm2_g[l],
             w_up[l], ffn_conv_w[l], ffn_conv_b[l], w_down[l])
        zero_conv = jnp.zeros((BATCH, CONV_K - 1, D_CONV), xp.dtype)
        zero_ffn = jnp.zeros((BATCH, FFN_K - 1, 2 * D_FF), xp.dtype)
        xp, c_p, k_p, v_p, f_p = _layer(xp, zero_conv, None, None, zero_ffn, p, True)
        xs, c_s, k_s, v_s, f_s = _layer(xs, state_conv[l], cache_k[l], cache_v[l],
                                        state_ffn_conv[l], p, False)
        kp_l.append(k_p); vp_l.append(v_p); cp_l.append(c_p); fp_l.append(f_p)
        ks_l.append(k_s); vs_l.append(v_s); cs_l.append(c_s); fs_l.append(f_s)
    y_prompt = _rmsnorm(xp, final_norm_g)
    y_sample = _rmsnorm(xs, final_norm_g)
    return (y_prompt, y_sample,
            jnp.stack(kp_l), jnp.stack(vp_l), jnp.stack(cp_l), jnp.stack(fp_l),
            jnp.stack(ks_l), jnp.stack(vs_l), jnp.stack(cs_l), jnp.stack(fs_l))
```

```python
import numpy as np
from contextlib import ExitStack
import concourse.bass as bass
import concourse.mybir as mybir
from concourse.bass_utils import run_bass_kernel_spmd

F32 = mybir.dt.float32
BF16 = mybir.dt.bfloat16
ALU = mybir.AluOpType
AF = mybir.ActivationFunctionType
AX = mybir.AxisListType

L = 2
D = 2048
NU = 118
NW = 3
EPS = 1e-6
PAIRS = [(c, c + 4) if c < 4 else (8 + c - 4, 12 + c - 4) for c in range(8)]
SLOPES = [2.0 ** (-(h + 1) / 2.0) for h in range(16)]
PL = 704
O_G1, O_G2, O_CW, O_CB, O_LG, O_LB, O_SK, O_FW, O_FB = 0, 16, 32, 280, 288, 296, 304, 320, 608
SAME_SYNC = True
DEBUG = False
import os
KSTOP = int(os.environ.get("KSTOP", "0"))
SKIPO = int(os.environ.get("SKIPO", "0"))


class _Stop(Exception):
    pass
BIG = 1.0e9


class Buf:
    __slots__ = ("w", "r", "name")

    def __init__(self, name):
        self.w = None
        self.r = {}
        self.name = name


class DSem:
    def __init__(self, nc, name):
        self.sem = nc.alloc_semaphore(name=name)
        self.val = 0
        self.exact = False


class Eng:
    def __init__(self, nc, eng, name):
        self.eng = eng
        self.name = name
        self.ds = DSem(nc, "c_" + name)
        self.ds.exact = True
        self.known = {}
        self.pend = []
        self.self_sync = SAME_SYNC and name in ("dve", "act")

    def _need(self, need, tk):
        if tk is None:
            return
        ds, val = tk
        if ds is self.ds:
            if not self.self_sync or val is None:
                return
        if val is None:
            raise RuntimeError("dependency on pending (non-inc) op of " + ds.sem.name if hasattr(ds.sem, "name") else "pending dep")
        if not ds.exact:
            val = ds.val
        if need.get(ds, 0) < val:
            need[ds] = val

    def waits(self, reads, writes, raw_same=True):
        need = {}
        for b in reads:
            self._need(need, b.w)
        for b in writes:
            self._need(need, b.w)
            for ds, val in b.r.items():
                if ds is self.ds:
                    continue
                self._need(need, (ds, val))
        for ds, val in need.items():
            if self.known.get(ds, 0) < val:
                self.eng.wait_ge(ds.sem, val)
                self.known[ds] = val

    def op(self, fn, reads=(), writes=(), inc=True):
        self.waits(reads, writes)
        ins = fn()
        if inc:
            self.ds.val += 1
            ins.then_inc(self.ds.sem, 1)
            tk = (self.ds, self.ds.val)
            for kind, b in self.pend:
                if kind == "r":
                    b.r[self.ds] = self.ds.val
                else:
                    if b.w is not None and b.w[0] is self.ds and b.w[1] is None:
                        b.w = tk
            self.pend = []
            for b in reads:
                b.r[self.ds] = self.ds.val
            for b in writes:
                b.w = tk
                b.r = {}
        else:
            for b in reads:
                b.r[self.ds] = None
                self.pend.append(("r", b))
            for b in writes:
                b.w = (self.ds, None)
                b.r = {}
                self.pend.append(("w", b))
        return ins

    def dma(self, dsem, out, in_, reads=(), writes=(), **kw):
        self.waits(reads, writes)
        dsem.val += 16
        self.eng.dma_start(out=out, in_=in_, **kw).then_inc(dsem.sem, 16)
        tk = (dsem, dsem.val)
        for b in reads:
            b.r[dsem] = dsem.val
        for b in writes:
            b.w = tk
            b.r = {}


def build_program():
    nc = bass.Bass("TRN2", target_bir_lowering=False)
    PE = Eng(nc, nc.tensor, "pe")
    DV = Eng(nc, nc.vector, "dve")
    AC = Eng(nc, nc.scalar, "act")
    SP = Eng(nc, nc.sync, "sp")
    GP = Eng(nc, nc.gpsimd, "pool")
    engs = [PE, DV, AC, SP, GP]

    def dram(name, shape, kind):
        return nc.dram_tensor(name, list(shape), F32, kind=kind).ap()

    xT_d = dram("xT", [128, 16, 1536], "ExternalInput")
    vmask_d = dram("vmask", [128, 1], "ExternalInput")
    dtab_d = dram("dtab", [128, 3 * 256], "ExternalInput")
    par_d = dram("par", [128, 2 * PL + 16], "ExternalInput")
    wp_d = [dram(f"wp{l}", [NU, 128, 16 * 256], "ExternalInput") for l in range(L)]
    sconv_d = dram("sconvT", [L, 128, 8, 608], "ExternalInput")
    sffn_d = dram("sffnT", [L, 128, 96 * 32], "ExternalInput")
    ckT_d = dram("ckT", [L, 128, 2, 2048], "ExternalInput")
    cv_d = dram("cv", [L, 16, 128, 256], "ExternalInput")
    ck_d = dram("ck", [L, 16, 128, 256], "ExternalInput")
    sconvn_d = dram("sconvn", [L, 16, 30, 1024], "ExternalInput")
    ident_d = dram("ident", [128, 128], "ExternalInput")

    yT_o = dram("yT", [128, 16, 1152], "ExternalOutput")
    kvp_o = dram("kvp", [L, 2, 128, 256], "ExternalOutput")
    convp_o = dram("convp", [L, 128, 8 * 30], "ExternalOutput")
    ffnp_o = dram("ffnp", [L, 128, 96 * 2], "ExternalOutput")
    kvsn_o = dram("kvsn", [L, 2, 128, 256], "ExternalOutput")
    kso_o = dram("kso", [L, 16, 120, 256], "ExternalOutput")
    vso_o = dram("vso", [L, 16, 120, 256], "ExternalOutput")
    convsn_o = dram("convsn", [L, 128, 8 * 128], "ExternalOutput")
    convso_o = dram("convso", [L, 16, 22, 1024], "ExternalOutput")
    ffns_o = dram("ffns", [L, 128, 96 * 32], "ExternalOutput")

    dbg_o = {}
    if DEBUG:
        for nm, kk in (("dbg_c", 8), ("dbg_a", 8), ("dbg_m", 16), ("dbg_x", 16), ("dbg_h2", 16), ("dbg_q", 8), ("dbg_k", 2)):
            dbg_o[nm] = dram(nm, [128, kk, 640], "ExternalOutput")
    ds_dbg = DSem(nc, "d_dbg")
    ds_const = DSem(nc, "d_const")
    ds_x = DSem(nc, "d_x")
    ds_samp = DSem(nc, "d_samp")
    ds_sampg = DSem(nc, "d_sampg")
    ds_out = DSem(nc, "d_out")
    ds_copy = DSem(nc, "d_copy")
    ds_w = [DSem(nc, f"d_w{i}") for i in range(NW)]
    for d in ds_w:
        d.exact = True

    def sb(name, shape, dt):
        return nc.alloc_sbuf_tensor("s_" + name, list(shape), dt)

    par = sb("par", [128, 2 * PL + 16], F32)
    dtab = sb("dtab", [128, 768], F32)
    vmask = sb("vmaskt", [128, 1], F32)
    identf = sb("identf", [128, 128], F32)
    identb = sb("identb", [128, 128], BF16)
    ones = sb("ones", [128, 128], BF16)
    wring = [sb(f"wr{i}", [128, 4096], BF16) for i in range(NW)]
    kcar = [sb(f"kcar{l}", [128, 256], BF16) for l in range(L)]
    vcar = [sb(f"vcar{l}", [128, 256], BF16) for l in range(L)]
    ucar = [sb(f"ucar{l}", [128, 240], BF16) for l in range(L)]
    fcar = [sb(f"fcar{l}", [128, 192], F32) for l in range(L)]
    ps = nc.alloc_psum_tensor("ps", [128, 7 * 512], F32)
    pt = nc.alloc_psum_tensor("pt", [128, 1024], BF16)

    b_par, b_dtab, b_vmask, b_id = Buf("par"), Buf("dtab"), Buf("vmask"), Buf("ident")
    b_wr = [Buf(f"wr{i}") for i in range(NW)]
    b_bank = [Buf(f"bank{i}") for i in range(7)]
    b_pt = Buf("pt")
    b_kcar = [Buf("kcar") for _ in range(L)]
    b_vcar = [Buf("vcar") for _ in range(L)]
    b_ucar = [Buf("ucar") for _ in range(L)]
    b_fcar = [Buf("fcar") for _ in range(L)]

    def pcol(off, n=1):
        return par[:, off:off + n]

    SP.dma(ds_const, par[:], par_d, writes=[b_par])
    SP.dma(ds_const, dtab[:], dtab_d, writes=[b_dtab])
    SP.dma(ds_const, vmask[:], vmask_d, writes=[b_vmask])
    SP.dma(ds_const, identf[:], ident_d, writes=[b_id])
    DV.op(lambda: nc.vector.tensor_copy(out=identb[:], in_=identf[:]), reads=[b_id], writes=[b_id])
    DV.op(lambda: nc.vector.memset(ones[:], 1.0), writes=[b_id])
    for l in range(L):
        DV.op(lambda: nc.vector.memset(fcar[l][:], 0.0), writes=[b_fcar[l]])
    for l in range(L):
        SP.dma(ds_copy, kso_o[l], ck_d[l, :, 8:128, :])
        SP.dma(ds_copy, vso_o[l], cv_d[l, :, 8:128, :])
        SP.dma(ds_copy, convso_o[l], sconvn_d[l, :, 8:30, :])

    stream = []
    for tile in range(2):
        for l in range(L):
            for u in range(NU):
                stream.append((l, u))
    wstate = {"issued": 0, "pos": 0}

    def w_issue_upto(n):
        while wstate["issued"] < min(n, len(stream)):
            i = wstate["issued"]
            l, u = stream[i]
            s = i % NW
            GP.dma(ds_w[s], wring[s][:], wp_d[l][u], writes=[b_wr[s]])
            wstate["issued"] += 1

    def w_next(keep=0):
        i = wstate["pos"]
        w_issue_upto(i + NW - keep)
        wstate["pos"] += 1
        s = i % NW
        return wring[s], b_wr[s]

    def barrier():
        for e in engs:
            for o in engs:
                if o is e or o.ds.val == 0:
                    continue
                if e.known.get(o.ds, 0) < o.ds.val:
                    e.eng.wait_ge(o.ds.sem, o.ds.val)
                    e.known[o.ds] = o.ds.val
            for d in [ds_const, ds_x, ds_samp, ds_sampg, ds_out, ds_dbg] + ds_w:
                if d.val and e.known.get(d, 0) < d.val:
                    e.eng.wait_ge(d.sem, d.val)
                    e.known[d] = d.val

    slot_rr = {"i": 0}

    def next_slot():
        i = slot_rr["i"] % 3
        slot_rr["i"] += 1
        return i

    def run_tile(tile):
        NC = 896 if tile == 0 else 640
        NCP = 896 if tile == 0 else 512
        NBLK = NCP // 128
        has_s = tile == 1
        xoff = 0 if tile == 0 else 896
        with ExitStack() as es:
            def tb(name, shape, dt):
                return es.enter_context(nc.sbuf_tensor(f"t{tile}_{name}", list(shape), dt))
            xT = tb("xT", [128, 16 * NC], F32)
            hT = tb("hT", [128, 16 * NC], BF16)
            UW = 30 + NCP
            MW = 16 * NC
            uq_cols = max(MW, 8 * UW + 8 * NC + (8 * 16 * 38 if has_s else 0))
            uq = tb("uq", [128, uq_cols], BF16)
            cT = tb("cT", [128, 8 * NC], BF16)
            aT = tb("aT", [128, 8 * NC], BF16)
            kT = tb("kT", [128, 2 * (128 + NC)], BF16)
            Vt = tb("V", [128, (NC // 128 + 1) * 256], BF16)
            TS = [tb(f"T{i}", [128, 928], F32) for i in range(3)]
            sbs = [tb(f"sb{i}", [128, 512], F32) for i in range(2)]
            pball = tb("pb", [128, 1024], BF16)
            pTall = tb("pTt", [128, 1024], BF16)
            pbs = [pball[:, 0:512], pball[:, 512:1024]]
            pTs = [pTall[:, 0:512], pTall[:, 512:1024]]
            sqs = [pball, pTall]
            sm = [tb(f"sm{i}", [128, 16], F32) for i in range(2)]
            if has_s:
                sffn = tb("sffn", [128, 96 * 32], F32)
                ckT = tb("ckTs", [128, 4096], BF16)
                cvt = tb("cvs", [128, 16 * 256], BF16)
                stf = tb("stf", [128, 128], F32)
                b_sffn, b_ckT, b_cv, b_stf = Buf("sffn"), Buf("ckT"), Buf("cv"), Buf("stf")

            b_x = [Buf(f"x{k}") for k in range(16)]
            b_h = Buf("hT")
            b_u = [Buf(f"u{k}") for k in range(8)]
            b_q = [Buf(f"q{k}") for k in range(8)]
            b_c = [Buf(f"c{k}") for k in range(8)]
            b_a = [Buf(f"a{k}") for k in range(8)]
            b_k = [Buf(f"k{k}") for k in range(2)]
            b_v = [Buf(f"v{k}") for k in range(NC // 128 + 1)]
            b_m = [Buf(f"m{k}") for k in range(16)]
            b_T = [Buf(f"T{k}") for k in range(3)]
            b_sb = [Buf("sb0"), Buf("sb1")]
            b_p = [Buf("p0"), Buf("p1")]
            b_pT = [Buf("pT0"), Buf("pT1")]
            b_sq = [b_p, b_pT]
            b_sm = [Buf("sm0"), Buf("sm1")]
            rr = {"T": 0, "sq": 0, "at": 0, "ost": 0}

            def nextT():
                i = rr["T"] % 3
                rr["T"] += 1
                return TS[i], b_T[i]

            xv = xT[:].rearrange("p (k n) -> p k n", k=16)
            hv = hT[:].rearrange("p (k n) -> p k n", k=16)
            uv = uq[:, 0:8 * UW].rearrange("p (k n) -> p k n", k=8)
            qv = uq[:, 8 * UW:8 * UW + 8 * NC].rearrange("p (k n) -> p k n", k=8)
            if has_s:
                so = 8 * UW + 8 * NC
                usv = uq[:, so:so + 8 * 608].rearrange("p (k s r) -> p k s r", k=8, s=16)
            mv = uq[:, 0:MW].rearrange("p (k n) -> p k n", k=16)
            cv_ = cT[:].rearrange("p (k n) -> p k n", k=8)
            av = aT[:].rearrange("p (k n) -> p k n", k=8)
            kv_ = kT[:].rearrange("p (k n) -> p k n", k=2)
            vv = Vt[:].rearrange("p (b d) -> p b d", d=256)

            def splits(c0, c1):
                n = c1 - c0
                if n <= 512:
                    return [(c0, c1)]
                h = (n // 2 + 1) // 2 * 2
                return [(c0, c0 + h), (c0 + h, c1)]

            def slot_ap(slot, sp, n):
                return ps[:, (2 * slot + sp) * 512:(2 * slot + sp) * 512 + n]

            def slot_view(slot, spl):
                n = spl[0][1] - spl[0][0]
                if len(spl) == 1:
                    return ps[:, 2 * slot * 512:2 * slot * 512 + n]
                assert spl[1][1] - spl[1][0] == n
                return ps[:, 2 * slot * 512:(2 * slot + 2) * 512].rearrange("p (s n) -> p s n", s=2)[:, :, 0:n]

            def cols_view(ap2d, spl):
                if len(spl) == 1:
                    return ap2d
                return ap2d.rearrange("p (s n) -> p s n", s=2)

            def mm_chunk(slot, wt, bw, wcol, nk, src, bsrc, spl, kslot0=0):
                wv = wt[:].rearrange("p (k n) -> p k n", k=16)
                banks = [b_bank[2 * slot], b_bank[2 * slot + 1]][:len(spl)]
                for k in range(nk):
                    for si, (a, b) in enumerate(spl):
                        last = (k == nk - 1) and (si == len(spl) - 1)
                        PE.op(lambda k=k, si=si, a=a, b=b: nc.tensor.matmul(
                            out=slot_ap(slot, si, b - a), lhsT=wv[:, kslot0 + k, wcol:wcol + 128],
                            rhs=src[:, k, a:b], start=(k == 0), stop=(k == nk - 1)),
                            reads=[bw] + list(bsrc), writes=banks, inc=last)
                return banks

            SP.dma(ds_x, xv, xT_d[:, :, xoff:xoff + NC], writes=b_x)

            def rmsnorm(goff, r0, dst, bdst, is_final=False, ycol0=0, c_out0=0):
                spl = splits(r0, NC)
                slot = next_slot()
                banks = [b_bank[2 * slot], b_bank[2 * slot + 1]][:len(spl)]
                for k in range(16):
                    i = rr["sq"] % 2
                    rr["sq"] += 1
                    AC.op(lambda k=k, i=i: nc.scalar.activation(out=sqs[i][:, 0:NC - r0], in_=xv[:, k, r0:NC], func=AF.Square),
                          reads=[b_x[k]], writes=b_sq[i])
                    for si, (a, b) in enumerate(spl):
                        last = (k == 15) and (si == len(spl) - 1)
                        PE.op(lambda k=k, i=i, si=si, a=a, b=b: nc.tensor.matmul(
                            out=slot_ap(slot, si, b - a), lhsT=ones[:], rhs=sqs[i][:, a - r0:b - r0],
                            start=(k == 0), stop=(k == 15)), reads=b_sq[i] + [b_id], writes=banks, inc=True)
                rt, brt = nextT()
                n = NC - r0
                for si, (a, b) in enumerate(spl):
                    AC.op(lambda si=si, a=a, b=b: nc.scalar.activation(out=rt[:, a - r0:b - r0], in_=slot_ap(slot, si, b - a),
                                                                     func=AF.Sqrt, bias=epsb[:, 0:1], scale=1.0 / D),
                          reads=banks + [b_id], writes=[brt])
                DV.op(lambda: nc.vector.reciprocal(out=rt[:, 0:n], in_=rt[:, 0:n]), reads=[brt], writes=[brt])
                if not is_final:
                    for k in range(16):
                        DV.op(lambda k=k: nc.vector.scalar_tensor_tensor(
                            out=dst[:, k, r0:NC], in0=xv[:, k, r0:NC], scalar=pcol(goff + k), in1=rt[:, 0:n],
                            op0=ALU.mult, op1=ALU.mult), reads=[b_x[k], brt, b_par], writes=[bdst])
                else:
                    for k in range(16):
                        ot, bot = nextT()
                        if ot is rt:
                            ot, bot = nextT()
                        DV.op(lambda k=k, ot=ot: nc.vector.scalar_tensor_tensor(
                            out=ot[:, 0:n], in0=xv[:, k, r0:NC], scalar=pcol(goff + k), in1=rt[:, 0:n],
                            op0=ALU.mult, op1=ALU.mult), reads=[b_x[k], brt, b_par], writes=[bot])
                        SP.dma(ds_out, yT_o[:, k, ycol0:ycol0 + n], ot[:, 0:n], reads=[bot])

            def alias(dst, srcs):
                acc = {}
                for bb in srcs:
                    items = list(bb.r.items())
                    if bb.w is not None:
                        items.append(bb.w)
                    for dsr, vr in items:
                        if vr is None:
                            raise RuntimeError("alias on pending op")
                        acc[dsr] = max(acc.get(dsr, 0), vr)
                for bd in dst:
                    for dsr, vr in acc.items():
                        bd.r[dsr] = max(bd.r.get(dsr, 0) or 0, vr)

            for l in range(L):
                po = l * PL
                alias(b_u + b_q, b_m)
                if tile == 0:
                    p1_0, b_0, c0 = (0, 1, 248) if l == 0 else (128, 2, 376)
                else:
                    p1_0, b_0, c0 = 0, 0, 0
                if has_s:
                    GP.dma(ds_sampg, uq[:, so:so + 8 * 608].rearrange("p (k n) -> p k n", k=8), sconv_d[l], writes=b_u)
                    SP.dma(ds_samp, sffn[:], sffn_d[l], writes=[b_sffn])
                    GP.dma(ds_sampg, ckT[:].rearrange("p (k n) -> p k n", k=2), ckT_d[l], writes=[b_ckT])
                    GP.dma(ds_sampg, cvt[:].rearrange("p (s d) -> p s d", s=16), cv_d[l].rearrange("s k d -> k s d"), writes=[b_cv])
                if tile == 1:
                    DV.op(lambda: nc.vector.tensor_copy(out=kv_[:, :, 0:128], in_=kcar[l][:].rearrange("p (k n) -> p k n", k=2)),
                          reads=[b_kcar[l]], writes=b_k)
                    DV.op(lambda: nc.vector.tensor_copy(out=vv[:, 0, :], in_=vcar[l][:]), reads=[b_vcar[l]], writes=[b_v[0]])
                    DV.op(lambda: nc.vector.tensor_copy(out=uv[:, :, 0:30], in_=ucar[l][:].rearrange("p (k n) -> p k n", k=8)),
                          reads=[b_ucar[l]], writes=b_u)
                else:
                    DV.op(lambda: nc.vector.memset(uv[:, :, 0:30], 0.0), writes=b_u)

                rmsnorm(po + O_G1, p1_0, hv, b_h)

                if KSTOP == 1 + 10 * tile and l == 0:
                    raise _Stop()
                spl1 = splits(p1_0, NC)
                n1 = NC - p1_0
                for i in range(8):
                    wt, bw = w_next()
                    sa = next_slot()
                    ba = mm_chunk(sa, wt, bw, 0, 16, hv, [b_h], spl1)
                    sg = next_slot()
                    bg = mm_chunk(sg, wt, bw, 128, 16, hv, [b_h], spl1)
                    tt, btt = nextT()
                    AC.op(lambda: nc.scalar.activation(out=cols_view(tt[:, 0:n1], spl1), in_=slot_view(sg, spl1), func=AF.Sigmoid),
                          reads=bg, writes=[btt])
                    for si, (a, b) in enumerate(spl1):
                        pa, pb = a, min(b, NCP)
                        if pb > pa:
                            DV.op(lambda si=si, a=a, pa=pa, pb=pb: nc.vector.tensor_tensor(
                                out=uv[:, i, 30 + pa:30 + pb], in0=slot_ap(sa, si, b - a)[:, pa - a:pb - a],
                                in1=tt[:, pa - p1_0:pb - p1_0], op=ALU.mult), reads=ba + [btt], writes=[b_u[i]])
                        if has_s and b > NCP:
                            DV.op(lambda si=si, a=a, b=b: nc.vector.tensor_tensor(
                                out=usv[:, i, :, 30:38],
                                in0=slot_ap(sa, si, b - a)[:, NCP - a:NCP - a + 128].rearrange("p (s t) -> p s t", s=16),
                                in1=tt[:, NCP - p1_0:NCP - p1_0 + 128].rearrange("p (s t) -> p s t", s=16),
                                op=ALU.mult), reads=ba + [btt], writes=[b_u[i]])
                for i in range(4):
                    wt, bw = w_next()
                    for cj in range(2):
                        c = 2 * i + cj
                        s_ = next_slot()
                        bq_ = mm_chunk(s_, wt, bw, cj * 128, 16, hv, [b_h], spl1)
                        AC.op(lambda c=c, s_=s_: nc.scalar.activation(out=cols_view(qv[:, c, p1_0:NC], spl1), in_=slot_view(s_, spl1),
                                                                    func=AF.Copy, scale=0.125), reads=bq_, writes=[b_q[c]])
                wt, bw = w_next()
                wv_ = wt[:].rearrange("p (k n) -> p k n", k=16)
                for cj in range(2):
                    s_ = next_slot()
                    bk_ = mm_chunk(s_, wt, bw, cj * 128, 16, hv, [b_h], spl1)
                    AC.op(lambda cj=cj, s_=s_: nc.scalar.activation(out=cols_view(kv_[:, cj, 128 + p1_0:128 + NC], spl1),
                                                                  in_=slot_view(s_, spl1), func=AF.Copy), reads=bk_, writes=[b_k[cj]])

                def tokmajor(blk, wv_, bw, out_d, also_v=None):
                    for k in range(16):
                        PE.op(lambda k=k: nc.tensor.matmul(out=ps[:, 6 * 512:6 * 512 + 256], lhsT=hv[:, k, blk * 128:(blk + 1) * 128],
                                                           rhs=wv_[:, k, :], start=(k == 0), stop=(k == 15)),
                              reads=[b_h, bw], writes=[b_bank[6]], inc=(k == 15))
                    if out_d is None:
                        AC.op(lambda: nc.scalar.activation(out=vv[:, also_v, :], in_=ps[:, 6 * 512:6 * 512 + 256], func=AF.Copy),
                              reads=[b_bank[6]], writes=[b_v[also_v]])
                    else:
                        o_, bo_ = nextT()
                        AC.op(lambda: nc.scalar.activation(out=o_[:, 0:256], in_=ps[:, 6 * 512:6 * 512 + 256], func=AF.Copy),
                              reads=[b_bank[6]], writes=[bo_])
                        if also_v is not None:
                            DV.op(lambda: nc.vector.tensor_copy(out=vv[:, also_v, :], in_=o_[:, 0:256]), reads=[bo_], writes=[b_v[also_v]])
                        SP.dma(ds_out, out_d, o_[:, 0:256], reads=[bo_])

                if tile == 1 and not (SKIPO & 1):
                    tokmajor(3, wv_, bw, kvp_o[l, 0])
                    tokmajor(4, wv_, bw, kvsn_o[l, 0])
                wt, bw = w_next()
                wv_ = wt[:].rearrange("p (k n) -> p k n", k=16)
                for blk in range(p1_0 // 128, NC // 128):
                    od = None
                    if tile == 1 and blk == 3:
                        od = kvp_o[l, 1]
                    if tile == 1 and blk == 4:
                        od = kvsn_o[l, 1]
                    if SKIPO & 2:
                        od = None
                    tokmajor(blk, wv_, bw, od, also_v=blk + 1)

                if tile == 1 and (SKIPO & 4):
                    pass
                elif tile == 1:
                    t_, bt_ = nextT()
                    DV.op(lambda: nc.vector.tensor_copy(out=t_[:, 0:240].rearrange("p (k n) -> p k n", k=8), in_=uv[:, :, 30 + 482:30 + 512]),
                          reads=b_u, writes=[bt_])
                    SP.dma(ds_out, convp_o[l], t_[:, 0:240], reads=[bt_])
                    for hh in range(2):
                        t_, bt_ = nextT()
                        DV.op(lambda hh=hh, t_=t_: nc.vector.tensor_copy(
                            out=t_[:, 0:512].rearrange("p (k s t) -> p k s t", k=4, s=16), in_=usv[:, 4 * hh:4 * hh + 4, :, 30:38]),
                            reads=b_u, writes=[bt_])
                        SP.dma(ds_out, convsn_o[l][:, 512 * hh:512 * hh + 512], t_[:, 0:512], reads=[bt_])
                else:
                    DV.op(lambda: nc.vector.tensor_copy(out=ucar[l][:].rearrange("p (k n) -> p k n", k=8), in_=uv[:, :, 30 + 866:30 + 896]),
                          reads=b_u, writes=[b_ucar[l]])
                    DV.op(lambda: nc.vector.tensor_copy(out=kcar[l][:].rearrange("p (k n) -> p k n", k=2), in_=kv_[:, :, 128 + 768:128 + 896]),
                          reads=b_k, writes=[b_kcar[l]])
                    DV.op(lambda: nc.vector.tensor_copy(out=vcar[l][:], in_=vv[:, 7, :]), reads=[b_v[7]], writes=[b_vcar[l]])

                if KSTOP == 2 + 10 * tile and l == 0:
                    raise _Stop()
                cc0 = 128 * b_0
                ncv = NC - cc0
                npc = NCP - cc0
                splc = splits(cc0, NC)
                s1 = next_slot()
                s2 = next_slot()
                s3 = next_slot()
                bk1 = [b_bank[2 * s1], b_bank[2 * s1 + 1]][:len(splc)]
                bk2 = [b_bank[2 * s2], b_bank[2 * s2 + 1]][:len(splc)]
                bk3 = [b_bank[2 * s3], b_bank[2 * s3 + 1]]
                acc = ps[:, 2 * s3 * 512:2 * s3 * 512 + 1024]
                for ch in range(8):
                    cwo = po + O_CW + ch * 31
                    DV.op(lambda: nc.vector.tensor_scalar(out=acc[:, 0:npc], in0=uv[:, ch, cc0:cc0 + npc], scalar1=pcol(cwo),
                                                         scalar2=pcol(po + O_CB + ch), op0=ALU.mult, op1=ALU.add),
                          reads=[b_u[ch], b_par], writes=bk3)
                    for j in range(1, 31):
                        DV.op(lambda j=j: nc.vector.scalar_tensor_tensor(out=acc[:, 0:npc], in0=uv[:, ch, cc0 + j:cc0 + j + npc], scalar=pcol(cwo + j),
                                                                         in1=acc[:, 0:npc], op0=ALU.mult, op1=ALU.add),
                              reads=[b_u[ch]] + bk3, writes=bk3, inc=(j == 30) or SAME_SYNC)
                    if has_s:
                        accs = acc[:, npc:npc + 128].rearrange("p (s t) -> p s t", s=16)
                        DV.op(lambda: nc.vector.tensor_scalar(out=accs, in0=usv[:, ch, :, 0:8], scalar1=pcol(cwo),
                                                             scalar2=pcol(po + O_CB + ch), op0=ALU.mult, op1=ALU.add),
                              reads=[b_u[ch], b_par], writes=bk3)
                        for j in range(1, 31):
                            DV.op(lambda j=j: nc.vector.scalar_tensor_tensor(out=accs, in0=usv[:, ch, :, j:j + 8], scalar=pcol(cwo + j),
                                                                             in1=accs, op0=ALU.mult, op1=ALU.add),
                                  reads=[b_u[ch]] + bk3, writes=bk3, inc=(j == 30) or SAME_SYNC)
                    AC.op(lambda: nc.scalar.activation(out=cv_[:, ch, cc0:NC], in_=acc[:, 0:ncv], func=AF.Copy), reads=bk3, writes=[b_c[ch]])
                    i = rr["sq"] % 2
                    rr["sq"] += 1
                    AC.op(lambda i=i: nc.scalar.activation(out=sqs[i][:, 0:ncv], in_=acc[:, 0:ncv], func=AF.Square), reads=bk3, writes=b_sq[i])
                    for si, (a, b) in enumerate(splc):
                        PE.op(lambda si=si, a=a, b=b: nc.tensor.matmul(out=slot_ap(s1, si, b - a), lhsT=ones[:], rhs=cv_[:, ch, a:b],
                                                                     start=(ch == 0), stop=(ch == 7)), reads=[b_c[ch], b_id], writes=bk1, inc=True)
                        PE.op(lambda si=si, a=a, b=b, i=i: nc.tensor.matmul(out=slot_ap(s2, si, b - a), lhsT=ones[:], rhs=sqs[i][:, a - cc0:b - cc0],
                                                                          start=(ch == 0), stop=(ch == 7)), reads=b_sq[i] + [b_id], writes=bk2, inc=True)
                mu, bmu = nextT()
                rs_, brs = nextT()
                for si, (a, b) in enumerate(splc):
                    AC.op(lambda si=si, a=a, b=b: nc.scalar.activation(out=mu[:, a - cc0:b - cc0], in_=slot_ap(s1, si, b - a), func=AF.Copy, scale=1.0 / 1024),
                          reads=bk1, writes=[bmu])
                    DV.op(lambda si=si, a=a, b=b: nc.vector.tensor_tensor(out=rs_[:, a - cc0:b - cc0], in0=mu[:, a - cc0:b - cc0], in1=mu[:, a - cc0:b - cc0], op=ALU.mult),
                          reads=[bmu], writes=[brs])
                    DV.op(lambda si=si, a=a, b=b: nc.vector.scalar_tensor_tensor(out=rs_[:, a - cc0:b - cc0], in0=slot_ap(s2, si, b - a), scalar=1.0 / 1024,
                                                                                 in1=rs_[:, a - cc0:b - cc0], op0=ALU.mult, op1=ALU.subtract),
                          reads=bk2 + [brs], writes=[brs])
                AC.op(lambda: nc.scalar.activation(out=rs_[:, 0:ncv], in_=rs_[:, 0:ncv], func=AF.Sqrt, bias=epsb[:, 0:1], scale=1.0), reads=[brs, b_id], writes=[brs])
                DV.op(lambda: nc.vector.reciprocal(out=rs_[:, 0:ncv], in_=rs_[:, 0:ncv]), reads=[brs], writes=[brs])
                for ch in range(8):
                    t_, bt_ = nextT()
                    while t_ is mu or t_ is rs_:
                        t_, bt_ = nextT()
                    DV.op(lambda: nc.vector.tensor_tensor(out=t_[:, 0:ncv], in0=cv_[:, ch, cc0:NC], in1=mu[:, 0:ncv], op=ALU.subtract),
                          reads=[b_c[ch], bmu], writes=[bt_])
                    DV.op(lambda: nc.vector.tensor_tensor(out=t_[:, 0:ncv], in0=t_[:, 0:ncv], in1=rs_[:, 0:ncv], op=ALU.mult),
                          reads=[bt_, brs], writes=[bt_])
                    AC.op(lambda: nc.scalar.activation(out=cv_[:, ch, cc0:NC], in_=t_[:, 0:ncv], func=AF.Silu,
                                                       bias=pcol(po + O_LB + ch), scale=pcol(po + O_LG + ch)),
                          reads=[bt_, b_par], writes=[b_c[ch]])

                if KSTOP == 3 + 10 * tile and l == 0:
                    raise _Stop()
                def attention(blk, is_samp):
                    q0 = blk * 128
                    tabi = 2 if is_samp else (1 if (tile == 0 and blk == 3) else 0)
                    dt_ = dtab[:, tabi * 256:(tabi + 1) * 256]
                    for c in range(8):
                        kc = 0 if c < 4 else 1
                        par_ = rr["at"] % 2
                        rr["at"] += 1
                        sbk = [b_S[0][par_], b_S[1][par_]]
                        sofs = [0 * 512 + par_ * 256, 1 * 512 + par_ * 256]
                        for hf in range(2):
                            pr = slice(hf * 64, hf * 64 + 64)
                            if not is_samp:
                                PE.op(lambda hf=hf, pr=pr: nc.tensor.matmul(out=ps[:, sofs[hf]:sofs[hf] + 256], lhsT=qv[pr, c, q0:q0 + 128],
                                                                          rhs=kv_[pr, kc, q0:q0 + 256], start=True, stop=True),
                                      reads=[b_q[c], b_k[kc]], writes=[sbk[hf]])
                            else:
                                ckv = ckT[:].rearrange("p (k s n) -> p k s n", k=2, s=16)
                                for s in range(16):
                                    PE.op(lambda s=s, pr=pr: nc.tensor.matmul(out=ps[:, 6 * 512 + s * 8:6 * 512 + s * 8 + 8], lhsT=ckv[pr, kc, s, :],
                                                                            rhs=qv[pr, c, q0 + s * 8:q0 + s * 8 + 8], start=True, stop=True),
                                          reads=[b_q[c], b_ckT], writes=[b_bank[6]], inc=(s == 15))
                                DV.op(lambda: nc.vector.tensor_copy(out=stf[:], in_=ps[:, 6 * 512:6 * 512 + 128]), reads=[b_bank[6]], writes=[b_stf])
                                PE.op(lambda hf=hf: nc.tensor.transpose(out=ps[:, sofs[hf]:sofs[hf] + 128], in_=stf[:], identity=identf[:]),
                                      reads=[b_stf, b_id], writes=[sbk[hf]])
                                PE.op(lambda hf=hf, pr=pr: nc.tensor.matmul(out=ps[:, sofs[hf] + 128:sofs[hf] + 256], lhsT=qv[pr, c, q0:q0 + 128],
                                                                          rhs=kv_[pr, kc, 128 + q0:128 + q0 + 128], start=True, stop=True),
                                      reads=[b_q[c], b_k[kc]], writes=[sbk[hf]])
                        sbt, bsb = sbs[par_], b_sb[par_]
                        smt, bsm = sm[par_], b_sm[par_]
                        for hf in range(2):
                            h = PAIRS[c][hf]
                            DV.op(lambda hf=hf, h=h: nc.vector.scalar_tensor_tensor(out=sbt[:, hf * 256:hf * 256 + 256], in0=dt_, scalar=-SLOPES[h],
                                                                                    in1=ps[:, sofs[hf]:sofs[hf] + 256], op0=ALU.mult, op1=ALU.add),
                                  reads=[sbk[hf], b_dtab], writes=[bsb])
                        sk = pcol(po + O_SK + 2 * c, 2)
                        DV.op(lambda: nc.vector.tensor_reduce(out=smt[:, 0:2], in_=sbt[:].rearrange("p (h n) -> p h n", h=2), axis=AX.X, op=ALU.max),
                              reads=[bsb], writes=[bsm])
                        DV.op(lambda: nc.vector.tensor_tensor(out=smt[:, 0:2], in0=smt[:, 0:2], in1=sk, op=ALU.max), reads=[bsm, b_par], writes=[bsm])
                        DV.op(lambda: nc.vector.tensor_scalar(out=smt[:, 2:4], in0=smt[:, 0:2], scalar1=-1.0, scalar2=None, op0=ALU.mult), reads=[bsm], writes=[bsm])
                        DV.op(lambda: nc.vector.tensor_tensor(out=smt[:, 4:6], in0=sk, in1=smt[:, 0:2], op=ALU.subtract), reads=[bsm, b_par], writes=[bsm])
                        pbt, bpb = pbs[par_], b_p[par_]
                        for hf in range(2):
                            AC.op(lambda hf=hf: nc.scalar.activation(out=sbt[:, hf * 256:hf * 256 + 256], in_=sbt[:, hf * 256:hf * 256 + 256], func=AF.Exp,
                                                                     bias=smt[:, 2 + hf:3 + hf], scale=1.0, accum_out=smt[:, 8 + hf:9 + hf]),
                                  reads=[bsb, bsm], writes=[bsb, bsm])
                        AC.op(lambda: nc.scalar.activation(out=smt[:, 6:8], in_=smt[:, 4:6], func=AF.Exp), reads=[bsm], writes=[bsm])
                        DV.op(lambda: nc.vector.tensor_tensor(out=smt[:, 10:12], in0=smt[:, 8:10], in1=smt[:, 6:8], op=ALU.add), reads=[bsm], writes=[bsm])
                        DV.op(lambda: nc.vector.reciprocal(out=smt[:, 12:14], in_=smt[:, 10:12]), reads=[bsm], writes=[bsm])
                        for hf in range(2):
                            DV.op(lambda hf=hf: nc.vector.tensor_scalar(out=pbt[:, hf * 256:hf * 256 + 256], in0=sbt[:, hf * 256:hf * 256 + 256],
                                                                        scalar1=smt[:, 12 + hf:13 + hf], scalar2=None, op0=ALU.mult),
                                  reads=[bsb, bsm], writes=[bpb])
                        for j in range(4):
                            PE.op(lambda j=j: nc.tensor.transpose(out=pt[:, j * 128:(j + 1) * 128], in_=pbt[:, j * 128:(j + 1) * 128], identity=identb[:]),
                                  reads=[bpb, b_id], writes=[b_pt], inc=(j == 3))
                        pTt, bpT = pTs[par_], b_pT[par_]
                        AC.op(lambda: nc.scalar.activation(out=pTt, in_=pt[:, 0:512], func=AF.Copy), reads=[b_pt], writes=[bpT])
                        abank = 2 + (c % 4)
                        aof = abank * 512
                        for hf in range(2):
                            kvh = (0 if c < 4 else 2) + hf
                            pr = slice(hf * 64, hf * 64 + 64)
                            if not is_samp:
                                PE.op(lambda hf=hf, pr=pr, kvh=kvh: nc.tensor.matmul(out=ps[pr, aof:aof + 128], lhsT=vv[:, blk, kvh * 64:kvh * 64 + 64],
                                                                                   rhs=pTt[:, (2 * hf) * 128:(2 * hf + 1) * 128], start=True, stop=False),
                                      reads=[b_v[blk], bpT], writes=[b_bank[abank]], inc=False)
                                PE.op(lambda hf=hf, pr=pr, kvh=kvh: nc.tensor.matmul(out=ps[pr, aof:aof + 128], lhsT=vv[:, blk + 1, kvh * 64:kvh * 64 + 64],
                                                                                   rhs=pTt[:, (2 * hf + 1) * 128:(2 * hf + 2) * 128], start=False, stop=True),
                                      reads=[b_v[blk + 1], bpT], writes=[b_bank[abank]], inc=True)
                            else:
                                cvv = cvt[:].rearrange("p (s d) -> p s d", s=16)
                                PE.op(lambda hf=hf, pr=pr, kvh=kvh: nc.tensor.matmul(out=ps[pr, aof:aof + 128], lhsT=vv[:, blk + 1, kvh * 64:kvh * 64 + 64],
                                                                                   rhs=pTt[:, (2 * hf + 1) * 128:(2 * hf + 2) * 128], start=True, stop=False),
                                      reads=[b_v[blk + 1], bpT], writes=[b_bank[abank]], inc=False)
                                for s in range(16):
                                    PE.op(lambda hf=hf, pr=pr, kvh=kvh, s=s: nc.tensor.matmul(
                                        out=ps[pr, aof + s * 8:aof + s * 8 + 8], lhsT=cvv[:, s, kvh * 64:kvh * 64 + 64],
                                        rhs=pTt[:, (2 * hf) * 128 + s * 8:(2 * hf) * 128 + s * 8 + 8], start=False, stop=(s == 15)),
                                        reads=[b_cv, bpT], writes=[b_bank[abank]], inc=(s == 15))
                        AC.op(lambda: nc.scalar.activation(out=av[:, c, q0:q0 + 128], in_=ps[:, aof:aof + 128], func=AF.Copy),
                              reads=[b_bank[abank]], writes=[b_a[c]])

                b_S = [[Buf("S00"), Buf("S01")], [Buf("S10"), Buf("S11")]]
                alias(b_S[0] + b_S[1], [b_bank[0], b_bank[1]])
                for blk in range(b_0, NBLK):
                    attention(blk, False)
                if has_s:
                    attention(4, True)
                alias([b_bank[0], b_bank[1]], b_S[0] + b_S[1])

                if KSTOP == 4 + 10 * tile and l == 0:
                    raise _Stop()
                if DEBUG and tile == 1 and l == 0:
                    GP.dma(ds_dbg, dbg_o["dbg_c"], cv_, reads=b_c)
                    GP.dma(ds_dbg, dbg_o["dbg_a"], av, reads=b_a)
                    GP.dma(ds_dbg, dbg_o["dbg_q"], qv, reads=b_q)
                    GP.dma(ds_dbg, dbg_o["dbg_k"], kv_[:, :, 128:128 + NC], reads=b_k)
                alias(b_m, b_u + b_q)

                splf = splits(c0, NC)
                nf = NC - c0
                for jj in range(8):
                    for cj in range(2):
                        j = 2 * jj + cj
                        wgc, bwgc = w_next(keep=cj)
                        wga, bwga = wgc, bwgc
                        s_ = next_slot()
                        bb_ = mm_chunk(s_, wgc, bwgc, 0, 16, hv, [b_h], splf)
                        t1, bt1 = nextT()
                        AC.op(lambda: nc.scalar.activation(out=cols_view(t1[:, 0:nf], splf), in_=slot_view(s_, splf), func=AF.Sigmoid), reads=bb_, writes=[bt1])
                        s_ = next_slot()
                        bb_ = mm_chunk(s_, wga, bwga, 128, 16, hv, [b_h], splf)
                        t2, bt2 = nextT()
                        AC.op(lambda: nc.scalar.activation(out=cols_view(t2[:, 0:nf], splf), in_=slot_view(s_, splf), func=AF.Sigmoid), reads=bb_, writes=[bt2])
                        if cj == 0:
                            wco, bwco = w_next()
                        s_ = next_slot()
                        bb_ = mm_chunk(s_, wco, bwco, cj * 128, 8, cv_, b_c, splf, kslot0=0)
                        DV.op(lambda: nc.vector.tensor_tensor(out=cols_view(t1[:, 0:nf], splf), in0=slot_view(s_, splf), in1=cols_view(t1[:, 0:nf], splf), op=ALU.mult),
                              reads=bb_ + [bt1], writes=[bt1])
                        s_ = next_slot()
                        bb_ = mm_chunk(s_, wco, bwco, cj * 128, 8, av, b_a, splf, kslot0=8)
                        DV.op(lambda: nc.vector.tensor_tensor(out=cols_view(t2[:, 0:nf], splf), in0=slot_view(s_, splf), in1=cols_view(t2[:, 0:nf], splf), op=ALU.mult),
                              reads=bb_ + [bt2], writes=[bt2])
                        DV.op(lambda: nc.vector.tensor_tensor(out=mv[:, j, c0:NC], in0=t1[:, 0:nf], in1=t2[:, 0:nf], op=ALU.add),
                              reads=[bt1, bt2], writes=[b_m[j]])

                def resid_add(j, s_, bb_):
                    DV.op(lambda: nc.vector.tensor_tensor(out=cols_view(xv[:, j, c0:NC], splf), in0=slot_view(s_, splf), in1=cols_view(xv[:, j, c0:NC], splf), op=ALU.add),
                          reads=bb_ + [b_x[j]], writes=[b_x[j]])

                def warm_mask(j):
                    if tile == 0 and c0 < 384:
                        DV.op(lambda: nc.vector.tensor_scalar(out=xv[:, j, c0:384], in0=xv[:, j, c0:384], scalar1=vmask[:, 0:1], scalar2=None, op0=ALU.mult),
                              reads=[b_x[j], b_vmask], writes=[b_x[j]])

                for jj in range(8):
                    wt, bw = w_next()
                    for cj in range(2):
                        j = 2 * jj + cj
                        s_ = next_slot()
                        bb_ = mm_chunk(s_, wt, bw, cj * 128, 16, mv, b_m, splf)
                        resid_add(j, s_, bb_)
                        warm_mask(j)

                if DEBUG and tile == 1 and l == 0:
                    GP.dma(ds_dbg, dbg_o["dbg_m"], mv, reads=b_m)
                    GP.dma(ds_dbg, dbg_o["dbg_x"], xv, reads=b_x)
                if KSTOP == 5 + 10 * tile and l == 0:
                    raise _Stop()
                rmsnorm(po + O_G2, c0, hv, b_h)
                if DEBUG and tile == 1 and l == 0:
                    GP.dma(ds_dbg, dbg_o["dbg_h2"], hv, reads=[b_h])

                npf = NCP - c0
                for gi in range(3):
                    for i in range(16):
                        wt, bw = w_next()
                        accs_ = []
                        for cj in range(2):
                            chn = (gi * 16 + i) + 48 * cj
                            s_ = next_slot()
                            bb_ = mm_chunk(s_, wt, bw, cj * 128, 16, hv, [b_h], splf)
                            acc, bacc = nextT()
                            fwo = po + O_FW + chn * 3
                            AC.op(lambda: nc.scalar.activation(out=cols_view(acc[:, 0:nf], splf), in_=slot_view(s_, splf), func=AF.Identity,
                                                               bias=pcol(po + O_FB + chn), scale=pcol(fwo + 2)), reads=bb_ + [b_par], writes=[bacc])

                            def pcols(a, b):
                                out = []
                                for si, (sa_, sb_) in enumerate(splf):
                                    lo, hi = max(a, sa_), min(b, sb_)
                                    if hi > lo:
                                        out.append((si, sa_, lo, hi))
                                return out
                            for tap, wi in ((1, 1), (2, 0)):
                                for (si, sa_, lo, hi) in pcols(c0, NCP - tap):
                                    DV.op(lambda si=si, sa_=sa_, lo=lo, hi=hi, tap=tap, wi=wi: nc.vector.scalar_tensor_tensor(
                                        out=acc[:, lo + tap - c0:hi + tap - c0], in0=slot_ap(s_, si, 512)[:, lo - sa_:hi - sa_], scalar=pcol(fwo + wi),
                                        in1=acc[:, lo + tap - c0:hi + tap - c0], op0=ALU.mult, op1=ALU.add), reads=bb_ + [bacc, b_par], writes=[bacc])
                            if tile == 1:
                                fc = fcar[l][:].rearrange("p (c t) -> p c t", t=2)
                                DV.op(lambda: nc.vector.scalar_tensor_tensor(out=acc[:, 0:1], in0=fc[:, chn, 1:2], scalar=pcol(fwo + 1), in1=acc[:, 0:1],
                                                                             op0=ALU.mult, op1=ALU.add), reads=[bacc, b_fcar[l], b_par], writes=[bacc])
                                DV.op(lambda: nc.vector.scalar_tensor_tensor(out=acc[:, 0:2], in0=fc[:, chn, 0:2], scalar=pcol(fwo + 0), in1=acc[:, 0:2],
                                                                             op0=ALU.mult, op1=ALU.add), reads=[bacc, b_fcar[l], b_par], writes=[bacc])
                            if has_s:
                                si_s = len(splf) - 1
                                sa_ = splf[si_s][0]
                                assert sa_ <= NCP
                                pss = slot_ap(s_, si_s, 512)[:, NCP - sa_:NCP - sa_ + 128].rearrange("p (s t) -> p s t", s=16)
                                acs = acc[:, NCP - c0:NCP - c0 + 128].rearrange("p (s t) -> p s t", s=16)
                                sfv = sffn[:].rearrange("p (c s t) -> p c s t", c=96, s=16)
                                DV.op(lambda: nc.vector.scalar_tensor_tensor(out=acs[:, :, 1:8], in0=pss[:, :, 0:7], scalar=pcol(fwo + 1), in1=acs[:, :, 1:8],
                                                                             op0=ALU.mult, op1=ALU.add), reads=bb_ + [bacc, b_par], writes=[bacc])
                                DV.op(lambda: nc.vector.scalar_tensor_tensor(out=acs[:, :, 2:8], in0=pss[:, :, 0:6], scalar=pcol(fwo + 0), in1=acs[:, :, 2:8],
                                                                             op0=ALU.mult, op1=ALU.add), reads=bb_ + [bacc, b_par], writes=[bacc])
                                DV.op(lambda: nc.vector.scalar_tensor_tensor(out=acs[:, :, 0:1], in0=sfv[:, chn, :, 1:2], scalar=pcol(fwo + 1), in1=acs[:, :, 0:1],
                                                                             op0=ALU.mult, op1=ALU.add), reads=[bacc, b_sffn, b_par], writes=[bacc])
                                DV.op(lambda: nc.vector.scalar_tensor_tensor(out=acs[:, :, 0:2], in0=sfv[:, chn, :, 0:2], scalar=pcol(fwo + 0), in1=acs[:, :, 0:2],
                                                                             op0=ALU.mult, op1=ALU.add), reads=[bacc, b_sffn, b_par], writes=[bacc])
                                AC.op(lambda: nc.scalar.activation(out=sfv[:, chn, :, :], in_=pss[:, :, 6:8], func=AF.Copy), reads=bb_ + [bacc], writes=[b_sffn])
                            si_p = [si for si, (sa2, sb2) in enumerate(splf) if sa2 < NCP][-1]
                            sa2 = splf[si_p][0]
                            fc = fcar[l][:].rearrange("p (c t) -> p c t", t=2)
                            AC.op(lambda: nc.scalar.activation(out=fc[:, chn, :], in_=slot_ap(s_, si_p, 512)[:, NCP - 2 - sa2:NCP - sa2], func=AF.Copy),
                                  reads=bb_ + [bacc], writes=[b_fcar[l]])
                            accs_.append((acc, bacc))
                        (ag, bag), (avl, bavl) = accs_
                        AC.op(lambda: nc.scalar.activation(out=ag[:, 0:nf], in_=ag[:, 0:nf], func=AF.Silu), reads=[bag], writes=[bag])
                        DV.op(lambda: nc.vector.tensor_tensor(out=mv[:, i, c0:NC], in0=ag[:, 0:nf], in1=avl[:, 0:nf], op=ALU.mult),
                              reads=[bag, bavl], writes=[b_m[i]])
                    for jj in range(8):
                        wt, bw = w_next()
                        for cj in range(2):
                            j = 2 * jj + cj
                            s_ = next_slot()
                            bb_ = mm_chunk(s_, wt, bw, cj * 128, 16, mv, b_m, splf)
                            resid_add(j, s_, bb_)
                            if gi == 2:
                                warm_mask(j)
                if KSTOP == 6 + 10 * tile and l == 0:
                    raise _Stop()
                if KSTOP == 7 + 10 * tile and l == 1:
                    raise _Stop()
                if tile == 1:
                    SP.dma(ds_out, ffnp_o[l], fcar[l][:], reads=[b_fcar[l]])
                    SP.dma(ds_out, ffns_o[l], sffn[:], reads=[b_sffn])

            if tile == 0:
                rmsnorm(2 * PL, 384, None, None, is_final=True, ycol0=0)
            else:
                rmsnorm(2 * PL, 0, None, None, is_final=True, ycol0=512)
            barrier()

    epsb = sb("epsb", [128, 1], F32)
    DV.op(lambda: nc.vector.memset(epsb[:], EPS), writes=[b_id])
    try:
        run_tile(0)
        if KSTOP == 8:
            raise _Stop()
        run_tile(1)
    except _Stop:
        barrier()
    for d in (ds_out, ds_copy):
        nc.sync.wait_ge(d.sem, d.val)
    return nc


def _unit(W, rows, cols):
    U = np.zeros((128, 16, 256), np.float32)
    for k, r in enumerate(rows):
        if r is None:
            continue
        if isinstance(r, np.ndarray):
            U[:, k, :] = W[r][:, cols]
        else:
            U[:, k, :] = W[r:r + 128][:, cols]
    return U.reshape(128, 4096)


def _pack_layer(w_in, w_co, w_ao, w_out, w_up, w_down):
    units = []
    r16 = [k * 128 for k in range(16)]
    ar = np.arange
    for i in range(8):
        units.append(_unit(w_in, r16, np.concatenate([ar(i * 128, i * 128 + 128), ar(1024 + i * 128, 1024 + i * 128 + 128)])))
    for i in range(4):
        cols = []
        for c in (2 * i, 2 * i + 1):
            for h in PAIRS[c]:
                cols.append(ar(2048 + h * 64, 2048 + h * 64 + 64))
        units.append(_unit(w_in, r16, np.concatenate(cols)))
    units.append(_unit(w_in, r16, ar(3072, 3328)))
    units.append(_unit(w_in, r16, ar(3328, 3584)))
    arow = []
    for c in range(8):
        arow.append(np.concatenate([ar(h * 64, h * 64 + 64) for h in PAIRS[c]]))
    for jj in range(8):
        j0, j1 = 2 * jj, 2 * jj + 1
        units.append(_unit(w_in, r16, np.concatenate([ar(3584 + j0 * 128, 3584 + j0 * 128 + 128), ar(5632 + j0 * 128, 5632 + j0 * 128 + 128)])))
        U = np.zeros((128, 16, 256), np.float32)
        cols = ar(jj * 256, jj * 256 + 256)
        for k in range(8):
            U[:, k, :] = w_co[k * 128:(k + 1) * 128][:, cols]
            U[:, 8 + k, :] = w_ao[arow[k]][:, cols]
        units.append(U.reshape(128, 4096))
        units.append(_unit(w_in, r16, np.concatenate([ar(3584 + j1 * 128, 3584 + j1 * 128 + 128), ar(5632 + j1 * 128, 5632 + j1 * 128 + 128)])))
    for j in range(8):
        units.append(_unit(w_out, r16, ar(j * 256, j * 256 + 256)))
    for gi in range(3):
        for i in range(16):
            c = gi * 16 + i
            units.append(_unit(w_up, r16, np.concatenate([ar(c * 128, c * 128 + 128), ar(6144 + c * 128, 6144 + c * 128 + 128)])))
        for j in range(8):
            units.append(_unit(w_down, [(gi * 16 + k) * 128 for k in range(16)], ar(j * 256, j * 256 + 256)))
    assert len(units) == NU
    return np.stack(units)


def _fm(v):
    return np.ascontiguousarray(v.reshape(-1, 128).T)


_NC_CACHE = {}
_PREP_ONLY = False


def kernel(x_prompt, x_sample, cache_k, cache_v, state_conv, state_ffn_conv,
           norm1_g, w_in, conv_w, conv_b, conv_ln_g, conv_ln_b, w_conv_out,
           attn_sinks, w_attn_out, w_out, norm2_g, w_up, ffn_conv_w, ffn_conv_b,
           w_down, final_norm_g):
    f = np.float32
    A = lambda a: np.asarray(a, dtype=f)
    x_prompt, x_sample = A(x_prompt), A(x_sample)
    cache_k, cache_v, state_conv, state_ffn_conv = A(cache_k), A(cache_v), A(state_conv), A(state_ffn_conv)
    wp = [_pack_layer(A(w_in[l]), A(w_conv_out[l]), A(w_attn_out[l]), A(w_out[l]), A(w_up[l]), A(w_down[l])) for l in range(L)]
    par = np.zeros((128, 2 * PL + 16), f)
    for l in range(L):
        o = l * PL
        par[:, o + O_G1:o + O_G1 + 16] = _fm(A(norm1_g[l]))
        par[:, o + O_G2:o + O_G2 + 16] = _fm(A(norm2_g[l]))
        cw = A(conv_w[l])
        par[:, o + O_CW:o + O_CW + 248] = cw.T.reshape(8, 128, 31).transpose(1, 0, 2).reshape(128, 248)
        par[:, o + O_CB:o + O_CB + 8] = _fm(A(conv_b[l]))
        par[:, o + O_LG:o + O_LG + 8] = _fm(A(conv_ln_g[l]))
        par[:, o + O_LB:o + O_LB + 8] = _fm(A(conv_ln_b[l]))
        sk = A(attn_sinks[l])
        par[:, o + O_SK:o + O_SK + 16] = np.array([sk[PAIRS[c][hf]] for c in range(8) for hf in range(2)], f)[None, :]
        fw = A(ffn_conv_w[l])
        par[:, o + O_FW:o + O_FW + 288] = fw.T.reshape(96, 128, 3).transpose(1, 0, 2).reshape(128, 288)
        par[:, o + O_FB:o + O_FB + 96] = _fm(A(ffn_conv_b[l]))
    par[:, 2 * PL:2 * PL + 16] = _fm(A(final_norm_g))
    ident = np.eye(128, dtype=f)
    a_ = np.arange(128)[:, None]
    j_ = np.arange(256)[None, :]
    dist = (128 + a_ - j_).astype(f)
    gen = np.where((j_ >= a_) & (j_ <= 128 + a_), dist, BIG).astype(f)
    s_r, t_r = np.arange(128)[:, None] // 8, np.arange(128)[:, None] % 8
    jc = np.arange(128)[None, :]
    dc = np.where(jc >= t_r, (128 + t_r - jc), BIG)
    s_c, t_c = jc // 8, jc % 8
    dn = np.where((s_c == s_r) & (t_c <= t_r), (t_r - t_c), BIG)
    samp = np.concatenate([dc, dn], axis=1).astype(f)

    in_maps = []
    for c in range(8):
        b, seg = c // 4, c % 4
        s0 = 16 * c
        xx = np.zeros((1536, D), f)
        p0 = seg * 1024 - 384
        lo = max(p0, 0)
        xx[lo - p0:1408] = x_prompt[b, lo:seg * 1024 + 1024]
        xx[1408:] = x_sample[s0:s0 + 16].reshape(128, D)
        xT = np.ascontiguousarray(xx.T.reshape(16, 128, 1536).transpose(1, 0, 2))
        first = gen.copy()
        if seg == 0:
            first[:, :128] = BIG
        dt = np.concatenate([gen, first, samp], axis=1)
        vm = np.full((128, 1), 0.0 if seg == 0 else 1.0, f)
        sc = state_conv[:, s0:s0 + 16]
        scT = np.zeros((L, 128, 8, 16, 38), f)
        scT[..., :30] = sc.reshape(L, 16, 30, 8, 128).transpose(0, 4, 3, 1, 2)
        sf = state_ffn_conv[:, s0:s0 + 16]
        sfT = sf.reshape(L, 16, 2, 96, 128).transpose(0, 4, 3, 1, 2)
        ck = cache_k[:, s0:s0 + 16].reshape(L, 16, 128, 256)
        cvv = cache_v[:, s0:s0 + 16].reshape(L, 16, 128, 256)
        ckT = ck.reshape(L, 16, 128, 2, 2, 64).transpose(0, 4, 5, 3, 1, 2).reshape(L, 128, 2, 2048)
        m = {
            "xT": xT, "vmask": vm, "dtab": np.ascontiguousarray(dt), "par": par, "wp0": wp[0], "wp1": wp[1],
            "sconvT": np.ascontiguousarray(scT.reshape(L, 128, 8, 608)),
            "sffnT": np.ascontiguousarray(sfT.reshape(L, 128, 96 * 32)),
            "ckT": np.ascontiguousarray(ckT), "cv": np.ascontiguousarray(cvv), "ck": np.ascontiguousarray(ck),
            "sconvn": np.ascontiguousarray(sc), "ident": ident,
        }
        in_maps.append(m)

    if _PREP_ONLY:
        return in_maps
    if "nc" not in _NC_CACHE:
        _NC_CACHE["nc"] = build_program()
    nc = _NC_CACHE["nc"]
    res = run_bass_kernel_spmd(nc, in_maps, core_ids=list(range(8)))
    return _post(res.results)


def _post(R):
    f = np.float32
    y_prompt = np.zeros((2, 4096, D), f)
    y_sample = np.zeros((128, 8, D), f)
    k_p = np.zeros((L, 2, 128, 4, 64), f)
    v_p = np.zeros((L, 2, 128, 4, 64), f)
    conv_p = np.zeros((L, 2, 30, 1024), f)
    ffn_p = np.zeros((L, 2, 2, 12288), f)
    k_s = np.zeros((L, 128, 128, 4, 64), f)
    v_s = np.zeros((L, 128, 128, 4, 64), f)
    conv_s = np.zeros((L, 128, 30, 1024), f)
    ffn_s = np.zeros((L, 128, 2, 12288), f)
    for c in range(8):
        b, seg = c // 4, c % 4
        s0 = 16 * c
        r = R[c]
        y = np.asarray(r["yT"]).transpose(1, 0, 2).reshape(D, 1152).T
        y_prompt[b, seg * 1024:(seg + 1) * 1024] = y[:1024]
        y_sample[s0:s0 + 16] = y[1024:].reshape(16, 8, D)
        kvsn = np.asarray(r["kvsn"])
        k_s[:, s0:s0 + 16, :120] = np.asarray(r["kso"]).reshape(L, 16, 120, 4, 64)
        v_s[:, s0:s0 + 16, :120] = np.asarray(r["vso"]).reshape(L, 16, 120, 4, 64)
        k_s[:, s0:s0 + 16, 120:] = kvsn[:, 0].reshape(L, 16, 8, 4, 64)
        v_s[:, s0:s0 + 16, 120:] = kvsn[:, 1].reshape(L, 16, 8, 4, 64)
        conv_s[:, s0:s0 + 16, :22] = np.asarray(r["convso"])
        csn = np.asarray(r["convsn"]).reshape(L, 128, 8, 16, 8)
        conv_s[:, s0:s0 + 16, 22:] = csn.transpose(0, 3, 4, 2, 1).reshape(L, 16, 8, 1024)
        fs = np.asarray(r["ffns"]).reshape(L, 128, 96, 16, 2)
        ffn_s[:, s0:s0 + 16] = fs.transpose(0, 3, 4, 2, 1).reshape(L, 16, 2, 12288)
        if seg == 3:
            kvp = np.asarray(r["kvp"])
            k_p[:, b] = kvp[:, 0].reshape(L, 128, 4, 64)
            v_p[:, b] = kvp[:, 1].reshape(L, 128, 4, 64)
            cp = np.asarray(r["convp"]).reshape(L, 128, 8, 30)
            conv_p[:, b] = cp.transpose(0, 3, 2, 1).reshape(L, 30, 1024)
            fp = np.asarray(r["ffnp"]).reshape(L, 128, 96, 2)
            ffn_p[:, b] = fp.transpose(0, 3, 2, 1).reshape(L, 2, 12288)
    return (y_prompt, y_sample, k_p, v_p, conv_p, ffn_p, k_s, v_s, conv_s, ffn_s)
```

```python
import numpy as np
from contextlib import ExitStack
import concourse.bass as bass
import concourse.mybir as mybir
from concourse.bass_utils import run_bass_kernel_spmd

F32 = mybir.dt.float32
BF16 = mybir.dt.bfloat16
ALU = mybir.AluOpType
AF = mybir.ActivationFunctionType
AX = mybir.AxisListType

L = 2
D = 2048
NU = 118
NW = 3
EPS = 1e-6
PAIRS = [(c, c + 4) if c < 4 else (8 + c - 4, 12 + c - 4) for c in range(8)]
SLOPES = [2.0 ** (-(h + 1) / 2.0) for h in range(16)]
PL = 704
O_G1, O_G2, O_CW, O_CB, O_LG, O_LB, O_SK, O_FW, O_FB = 0, 16, 32, 280, 288, 296, 304, 320, 608
SAME_SYNC = True
DEBUG = False
import os
KSTOP = int(os.environ.get("KSTOP", "0"))
SKIPO = int(os.environ.get("SKIPO", "0"))


class _Stop(Exception):
    pass
BIG = 1.0e9


class Buf:
    __slots__ = ("w", "r", "name")

    def __init__(self, name):
        self.w = None
        self.r = {}
        self.name = name


class DSem:
    def __init__(self, nc, name):
        self.sem = nc.alloc_semaphore(name=name)
        self.val = 0
        self.exact = False


class Eng:
    def __init__(self, nc, eng, name):
        self.eng = eng
        self.name = name
        self.ds = DSem(nc, "c_" + name)
        self.ds.exact = True
        self.known = {}
        self.pend = []
        self.self_sync = SAME_SYNC and name in ("dve", "act")

    def _need(self, need, tk):
        if tk is None:
            return
        ds, val = tk
        if ds is self.ds:
            if not self.self_sync or val is None:
                return
        if val is None:
            raise RuntimeError("dependency on pending (non-inc) op of " + ds.sem.name if hasattr(ds.sem, "name") else "pending dep")
        if not ds.exact:
            val = ds.val
        if need.get(ds, 0) < val:
            need[ds] = val

    def waits(self, reads, writes, raw_same=True):
        need = {}
        for b in reads:
            self._need(need, b.w)
        for b in writes:
            self._need(need, b.w)
            for ds, val in b.r.items():
                if ds is self.ds:
                    continue
                self._need(need, (ds, val))
        for ds, val in need.items():
            if self.known.get(ds, 0) < val:
                self.eng.wait_ge(ds.sem, val)
                self.known[ds] = val

    def op(self, fn, reads=(), writes=(), inc=True):
        self.waits(reads, writes)
        ins = fn()
        if inc:
            self.ds.val += 1
            ins.then_inc(self.ds.sem, 1)
            tk = (self.ds, self.ds.val)
            for kind, b in self.pend:
                if kind == "r":
                    b.r[self.ds] = self.ds.val
                else:
                    if b.w is not None and b.w[0] is self.ds and b.w[1] is None:
                        b.w = tk
            self.pend = []
            for b in reads:
                b.r[self.ds] = self.ds.val
            for b in writes:
                b.w = tk
                b.r = {}
        else:
            for b in reads:
                b.r[self.ds] = None
                self.pend.append(("r", b))
            for b in writes:
                b.w = (self.ds, None)
                b.r = {}
                self.pend.append(("w", b))
        return ins

    def dma(self, dsem, out, in_, reads=(), writes=(), **kw):
        self.waits(reads, writes)
        dsem.val += 16
        self.eng.dma_start(out=out, in_=in_, **kw).then_inc(dsem.sem, 16)
        tk = (dsem, dsem.val)
        for b in reads:
            b.r[dsem] = dsem.val
        for b in writes:
            b.w = tk
            b.r = {}


def build_program():
    nc = bass.Bass("TRN2", target_bir_lowering=False)
    PE = Eng(nc, nc.tensor, "pe")
    DV = Eng(nc, nc.vector, "dve")
    AC = Eng(nc, nc.scalar, "act")
    SP = Eng(nc, nc.sync, "sp")
    GP = Eng(nc, nc.gpsimd, "pool")
    engs = [PE, DV, AC, SP, GP]

    def dram(name, shape, kind):
        return nc.dram_tensor(name, list(shape), F32, kind=kind).ap()

    xT_d = dram("xT", [128, 16, 1536], "ExternalInput")
    vmask_d = dram("vmask", [128, 1], "ExternalInput")
    dtab_d = dram("dtab", [128, 3 * 256], "ExternalInput")
    par_d = dram("par", [128, 2 * PL + 16], "ExternalInput")
    wp_d = [dram(f"wp{l}", [NU, 128, 16 * 256], "ExternalInput") for l in range(L)]
    sconv_d = dram("sconvT", [L, 128, 8, 608], "ExternalInput")
    sffn_d = dram("sffnT", [L, 128, 96 * 32], "ExternalInput")
    ckT_d = dram("ckT", [L, 128, 2, 2048], "ExternalInput")
    cv_d = dram("cv", [L, 16, 128, 256], "ExternalInput")
    ck_d = dram("ck", [L, 16, 128, 256], "ExternalInput")
    sconvn_d = dram("sconvn", [L, 16, 30, 1024], "ExternalInput")
    ident_d = dram("ident", [128, 128], "ExternalInput")

    yT_o = dram("yT", [128, 16, 1152], "ExternalOutput")
    kvp_o = dram("kvp", [L, 2, 128, 256], "ExternalOutput")
    convp_o = dram("convp", [L, 128, 8 * 30], "ExternalOutput")
    ffnp_o = dram("ffnp", [L, 128, 96 * 2], "ExternalOutput")
    kvsn_o = dram("kvsn", [L, 2, 128, 256], "ExternalOutput")
    kso_o = dram("kso", [L, 16, 120, 256], "ExternalOutput")
    vso_o = dram("vso", [L, 16, 120, 256], "ExternalOutput")
    convsn_o = dram("convsn", [L, 128, 8 * 128], "ExternalOutput")
    convso_o = dram("convso", [L, 16, 22, 1024], "ExternalOutput")
    ffns_o = dram("ffns", [L, 128, 96 * 32], "ExternalOutput")

    dbg_o = {}
    if DEBUG:
        for nm, kk in (("dbg_c", 8), ("dbg_a", 8), ("dbg_m", 16), ("dbg_x", 16), ("dbg_h2", 16), ("dbg_q", 8), ("dbg_k", 2)):
            dbg_o[nm] = dram(nm, [128, kk, 640], "ExternalOutput")
    ds_dbg = DSem(nc, "d_dbg")
    ds_const = DSem(nc, "d_const")
    ds_x = DSem(nc, "d_x")
    ds_samp = DSem(nc, "d_samp")
    ds_sampg = DSem(nc, "d_sampg")
    ds_out = DSem(nc, "d_out")
    ds_copy = DSem(nc, "d_copy")
    ds_w = [DSem(nc, f"d_w{i}") for i in range(NW)]
    for d in ds_w:
        d.exact = True

    def sb(name, shape, dt):
        return nc.alloc_sbuf_tensor("s_" + name, list(shape), dt)

    par = sb("par", [128, 2 * PL + 16], F32)
    dtab = sb("dtab", [128, 768], F32)
    vmask = sb("vmaskt", [128, 1], F32)
    identf = sb("identf", [128, 128], F32)
    identb = sb("identb", [128, 128], BF16)
    ones = sb("ones", [128, 128], BF16)
    wring = [sb(f"wr{i}", [128, 4096], BF16) for i in range(NW)]
    kcar = [sb(f"kcar{l}", [128, 256], BF16) for l in range(L)]
    vcar = [sb(f"vcar{l}", [128, 256], BF16) for l in range(L)]
    ucar = [sb(f"ucar{l}", [128, 240], BF16) for l in range(L)]
    fcar = [sb(f"fcar{l}", [128, 192], F32) for l in range(L)]
    ps = nc.alloc_psum_tensor("ps", [128, 7 * 512], F32)
    pt = nc.alloc_psum_tensor("pt", [128, 1024], BF16)

    b_par, b_dtab, b_vmask, b_id = Buf("par"), Buf("dtab"), Buf("vmask"), Buf("ident")
    b_wr = [Buf(f"wr{i}") for i in range(NW)]
    b_bank = [Buf(f"bank{i}") for i in range(7)]
    b_pt = Buf("pt")
    b_kcar = [Buf("kcar") for _ in range(L)]
    b_vcar = [Buf("vcar") for _ in range(L)]
    b_ucar = [Buf("ucar") for _ in range(L)]
    b_fcar = [Buf("fcar") for _ in range(L)]

    def pcol(off, n=1):
        return par[:, off:off + n]

    SP.dma(ds_const, par[:], par_d, writes=[b_par])
    SP.dma(ds_const, dtab[:], dtab_d, writes=[b_dtab])
    SP.dma(ds_const, vmask[:], vmask_d, writes=[b_vmask])
    SP.dma(ds_const, identf[:], ident_d, writes=[b_id])
    DV.op(lambda: nc.vector.tensor_copy(out=identb[:], in_=identf[:]), reads=[b_id], writes=[b_id])
    DV.op(lambda: nc.vector.memset(ones[:], 1.0), writes=[b_id])
    for l in range(L):
        DV.op(lambda: nc.vector.memset(fcar[l][:], 0.0), writes=[b_fcar[l]])
    for l in range(L):
        SP.dma(ds_copy, kso_o[l], ck_d[l, :, 8:128, :])
        SP.dma(ds_copy, vso_o[l], cv_d[l, :, 8:128, :])
        SP.dma(ds_copy, convso_o[l], sconvn_d[l, :, 8:30, :])

    stream = []
    for tile in range(2):
        for l in range(L):
            for u in range(NU):
                stream.append((l, u))
    wstate = {"issued": 0, "pos": 0}

    def w_issue_upto(n):
        while wstate["issued"] < min(n, len(stream)):
            i = wstate["issued"]
            l, u = stream[i]
            s = i % NW
            GP.dma(ds_w[s], wring[s][:], wp_d[l][u], writes=[b_wr[s]])
            wstate["issued"] += 1

    def w_next(keep=0):
        i = wstate["pos"]
        w_issue_upto(i + NW - keep)
        wstate["pos"] += 1
        s = i % NW
        return wring[s], b_wr[s]

    def barrier():
        for e in engs:
            for o in engs:
                if o is e or o.ds.val == 0:
                    continue
                if e.known.get(o.ds, 0) < o.ds.val:
                    e.eng.wait_ge(o.ds.sem, o.ds.val)
                    e.known[o.ds] = o.ds.val
            for d in [ds_const, ds_x, ds_samp, ds_sampg, ds_out, ds_dbg] + ds_w:
                if d.val and e.known.get(d, 0) < d.val:
                    e.eng.wait_ge(d.sem, d.val)
                    e.known[d] = d.val

    slot_rr = {"i": 0}

    def next_slot():
        i = slot_rr["i"] % 3
        slot_rr["i"] += 1
        return i

    def run_tile(tile):
        NC = 896 if tile == 0 else 640
        NCP = 896 if tile == 0 else 512
        NBLK = NCP // 128
        has_s = tile == 1
        xoff = 0 if tile == 0 else 896
        with ExitStack() as es:
            def tb(name, shape, dt):
                return es.enter_context(nc.sbuf_tensor(f"t{tile}_{name}", list(shape), dt))
            xT = tb("xT", [128, 16 * NC], F32)
            hT = tb("hT", [128, 16 * NC], BF16)
            UW = 30 + NCP
            MW = 16 * NC
            uq_cols = max(MW, 8 * UW + 8 * NC + (8 * 16 * 38 if has_s else 0))
            uq = tb("uq", [128, uq_cols], BF16)
            cT = tb("cT", [128, 8 * NC], BF16)
            aT = tb("aT", [128, 8 * NC], BF16)
            kT = tb("kT", [128, 2 * (128 + NC)], BF16)
            Vt = tb("V", [128, (NC // 128 + 1) * 256], BF16)
            TS = [tb(f"T{i}", [128, 928], F32) for i in range(3)]
            sbs = [tb(f"sb{i}", [128, 512], F32) for i in range(2)]
            pball = tb("pb", [128, 1024], BF16)
            pTall = tb("pTt", [128, 1024], BF16)
            pbs = [pball[:, 0:512], pball[:, 512:1024]]
            pTs = [pTall[:, 0:512], pTall[:, 512:1024]]
            sqs = [pball, pTall]
            sm = [tb(f"sm{i}", [128, 16], F32) for i in range(2)]
            if has_s:
                sffn = tb("sffn", [128, 96 * 32], F32)
                ckT = tb("ckTs", [128, 4096], BF16)
                cvt = tb("cvs", [128, 16 * 256], BF16)
                stf = tb("stf", [128, 128], F32)
                b_sffn, b_ckT, b_cv, b_stf = Buf("sffn"), Buf("ckT"), Buf("cv"), Buf("stf")

            b_x = [Buf(f"x{k}") for k in range(16)]
            b_h = Buf("hT")
            b_u = [Buf(f"u{k}") for k in range(8)]
            b_q = [Buf(f"q{k}") for k in range(8)]
            b_c = [Buf(f"c{k}") for k in range(8)]
            b_a = [Buf(f"a{k}") for k in range(8)]
            b_k = [Buf(f"k{k}") for k in range(2)]
            b_v = [Buf(f"v{k}") for k in range(NC // 128 + 1)]
            b_m = [Buf(f"m{k}") for k in range(16)]
            b_T = [Buf(f"T{k}") for k in range(3)]
            b_sb = [Buf("sb0"), Buf("sb1")]
            b_p = [Buf("p0"), Buf("p1")]
            b_pT = [Buf("pT0"), Buf("pT1")]
            b_sq = [b_p, b_pT]
            b_sm = [Buf("sm0"), Buf("sm1")]
            rr = {"T": 0, "sq": 0, "at": 0, "ost": 0}

            def nextT():
                i = rr["T"] % 3
                rr["T"] += 1
                return TS[i], b_T[i]

            xv = xT[:].rearrange("p (k n) -> p k n", k=16)
            hv = hT[:].rearrange("p (k n) -> p k n", k=16)
            uv = uq[:, 0:8 * UW].rearrange("p (k n) -> p k n", k=8)
            qv = uq[:, 8 * UW:8 * UW + 8 * NC].rearrange("p (k n) -> p k n", k=8)
            if has_s:
                so = 8 * UW + 8 * NC
                usv = uq[:, so:so + 8 * 608].rearrange("p (k s r) -> p k s r", k=8, s=16)
            mv = uq[:, 0:MW].rearrange("p (k n) -> p k n", k=16)
            cv_ = cT[:].rearrange("p (k n) -> p k n", k=8)
            av = aT[:].rearrange("p (k n) -> p k n", k=8)
            kv_ = kT[:].rearrange("p (k n) -> p k n", k=2)
            vv = Vt[:].rearrange("p (b d) -> p b d", d=256)

            def splits(c0, c1):
                n = c1 - c0
                if n <= 512:
                    return [(c0, c1)]
                h = (n // 2 + 1) // 2 * 2
                return [(c0, c0 + h), (c0 + h, c1)]

            def slot_ap(slot, sp, n):
                return ps[:, (2 * slot + sp) * 512:(2 * slot + sp) * 512 + n]

            def slot_view(slot, spl):
                n = spl[0][1] - spl[0][0]
                if len(spl) == 1:
                    return ps[:, 2 * slot * 512:2 * slot * 512 + n]
                assert spl[1][1] - spl[1][0] == n
                return ps[:, 2 * slot * 512:(2 * slot + 2) * 512].rearrange("p (s n) -> p s n", s=2)[:, :, 0:n]

            def cols_view(ap2d, spl):
                if len(spl) == 1:
                    return ap2d
                return ap2d.rearrange("p (s n) -> p s n", s=2)

            def mm_chunk(slot, wt, bw, wcol, nk, src, bsrc, spl, kslot0=0):
                wv = wt[:].rearrange("p (k n) -> p k n", k=16)
                banks = [b_bank[2 * slot], b_bank[2 * slot + 1]][:len(spl)]
                for k in range(nk):
                    for si, (a, b) in enumerate(spl):
                        last = (k == nk - 1) and (si == len(spl) - 1)
                        PE.op(lambda k=k, si=si, a=a, b=b: nc.tensor.matmul(
                            out=slot_ap(slot, si, b - a), lhsT=wv[:, kslot0 + k, wcol:wcol + 128],
                            rhs=src[:, k, a:b], start=(k == 0), stop=(k == nk - 1)),
                            reads=[bw] + list(bsrc), writes=banks, inc=last)
                return banks

            SP.dma(ds_x, xv, xT_d[:, :, xoff:xoff + NC], writes=b_x)

            def rmsnorm(goff, r0, dst, bdst, is_final=False, ycol0=0, c_out0=0):
                spl = splits(r0, NC)
                slot = next_slot()
                banks = [b_bank[2 * slot], b_bank[2 * slot + 1]][:len(spl)]
                for k in range(16):
                    i = rr["sq"] % 2
                    rr["sq"] += 1
                    AC.op(lambda k=k, i=i: nc.scalar.activation(out=sqs[i][:, 0:NC - r0], in_=xv[:, k, r0:NC], func=AF.Square),
                          reads=[b_x[k]], writes=b_sq[i])
                    for si, (a, b) in enumerate(spl):
                        last = (k == 15) and (si == len(spl) - 1)
                        PE.op(lambda k=k, i=i, si=si, a=a, b=b: nc.tensor.matmul(
                            out=slot_ap(slot, si, b - a), lhsT=ones[:], rhs=sqs[i][:, a - r0:b - r0],
                            start=(k == 0), stop=(k == 15)), reads=b_sq[i] + [b_id], writes=banks, inc=True)
                rt, brt = nextT()
                n = NC - r0
                for si, (a, b) in enumerate(spl):
                    AC.op(lambda si=si, a=a, b=b: nc.scalar.activation(out=rt[:, a - r0:b - r0], in_=slot_ap(slot, si, b - a),
                                                                     func=AF.Sqrt, bias=epsb[:, 0:1], scale=1.0 / D),
                          reads=banks + [b_id], writes=[brt])
                DV.op(lambda: nc.vector.reciprocal(out=rt[:, 0:n], in_=rt[:, 0:n]), reads=[brt], writes=[brt])
                if not is_final:
                    for k in range(16):
                        DV.op(lambda k=k: nc.vector.scalar_tensor_tensor(
                            out=dst[:, k, r0:NC], in0=xv[:, k, r0:NC], scalar=pcol(goff + k), in1=rt[:, 0:n],
                            op0=ALU.mult, op1=ALU.mult), reads=[b_x[k], brt, b_par], writes=[bdst])
                else:
                    for k in range(16):
                        ot, bot = nextT()
                        if ot is rt:
                            ot, bot = nextT()
                        DV.op(lambda k=k, ot=ot: nc.vector.scalar_tensor_tensor(
                            out=ot[:, 0:n], in0=xv[:, k, r0:NC], scalar=pcol(goff + k), in1=rt[:, 0:n],
                            op0=ALU.mult, op1=ALU.mult), reads=[b_x[k], brt, b_par], writes=[bot])
                        SP.dma(ds_out, yT_o[:, k, ycol0:ycol0 + n], ot[:, 0:n], reads=[bot])

            def alias(dst, srcs):
                acc = {}
                for bb in srcs:
                    items = list(bb.r.items())
                    if bb.w is not None:
                        items.append(bb.w)
                    for dsr, vr in items:
                        if vr is None:
                            raise RuntimeError("alias on pending op")
                        acc[dsr] = max(acc.get(dsr, 0), vr)
                for bd in dst:
                    for dsr, vr in acc.items():
                        bd.r[dsr] = max(bd.r.get(dsr, 0) or 0, vr)

            for l in range(L):
                po = l * PL
                alias(b_u + b_q, b_m)
                if tile == 0:
                    p1_0, b_0, c0 = (0, 1, 248) if l == 0 else (128, 2, 376)
                else:
                    p1_0, b_0, c0 = 0, 0, 0
                if has_s:
                    GP.dma(ds_sampg, uq[:, so:so + 8 * 608].rearrange("p (k n) -> p k n", k=8), sconv_d[l], writes=b_u)
                    SP.dma(ds_samp, sffn[:], sffn_d[l], writes=[b_sffn])
                    GP.dma(ds_sampg, ckT[:].rearrange("p (k n) -> p k n", k=2), ckT_d[l], writes=[b_ckT])
                    GP.dma(ds_sampg, cvt[:].rearrange("p (s d) -> p s d", s=16), cv_d[l].rearrange("s k d -> k s d"), writes=[b_cv])
                if tile == 1:
                    DV.op(lambda: nc.vector.tensor_copy(out=kv_[:, :, 0:128], in_=kcar[l][:].rearrange("p (k n) -> p k n", k=2)),
                          reads=[b_kcar[l]], writes=b_k)
                    DV.op(lambda: nc.vector.tensor_copy(out=vv[:, 0, :], in_=vcar[l][:]), reads=[b_vcar[l]], writes=[b_v[0]])
                    DV.op(lambda: nc.vector.tensor_copy(out=uv[:, :, 0:30], in_=ucar[l][:].rearrange("p (k n) -> p k n", k=8)),
                          reads=[b_ucar[l]], writes=b_u)
                else:
                    DV.op(lambda: nc.vector.memset(uv[:, :, 0:30], 0.0), writes=b_u)

                rmsnorm(po + O_G1, p1_0, hv, b_h)

                if KSTOP == 1 + 10 * tile and l == 0:
                    raise _Stop()
                spl1 = splits(p1_0, NC)
                n1 = NC - p1_0
                for i in range(8):
                    wt, bw = w_next()
                    sa = next_slot()
                    ba = mm_chunk(sa, wt, bw, 0, 16, hv, [b_h], spl1)
                    sg = next_slot()
                    bg = mm_chunk(sg, wt, bw, 128, 16, hv, [b_h], spl1)
                    tt, btt = nextT()
                    AC.op(lambda: nc.scalar.activation(out=cols_view(tt[:, 0:n1], spl1), in_=slot_view(sg, spl1), func=AF.Sigmoid),
                          reads=bg, writes=[btt])
                    for si, (a, b) in enumerate(spl1):
                        pa, pb = a, min(b, NCP)
                        if pb > pa:
                            DV.op(lambda si=si, a=a, pa=pa, pb=pb: nc.vector.tensor_tensor(
                                out=uv[:, i, 30 + pa:30 + pb], in0=slot_ap(sa, si, b - a)[:, pa - a:pb - a],
                                in1=tt[:, pa - p1_0:pb - p1_0], op=ALU.mult), reads=ba + [btt], writes=[b_u[i]])
                        if has_s and b > NCP:
                            DV.op(lambda si=si, a=a, b=b: nc.vector.tensor_tensor(
                                out=usv[:, i, :, 30:38],
                                in0=slot_ap(sa, si, b - a)[:, NCP - a:NCP - a + 128].rearrange("p (s t) -> p s t", s=16),
                                in1=tt[:, NCP - p1_0:NCP - p1_0 + 128].rearrange("p (s t) -> p s t", s=16),
                                op=ALU.mult), reads=ba + [btt], writes=[b_u[i]])
                for i in range(4):
                    wt, bw = w_next()
                    for cj in range(2):
                        c = 2 * i + cj
                        s_ = next_slot()
                        bq_ = mm_chunk(s_, wt, bw, cj * 128, 16, hv, [b_h], spl1)
                        AC.op(lambda c=c, s_=s_: nc.scalar.activation(out=cols_view(qv[:, c, p1_0:NC], spl1), in_=slot_view(s_, spl1),
                                                                    func=AF.Copy, scale=0.125), reads=bq_, writes=[b_q[c]])
                wt, bw = w_next()
                wv_ = wt[:].rearrange("p (k n) -> p k n", k=16)
                for cj in range(2):
                    s_ = next_slot()
                    bk_ = mm_chunk(s_, wt, bw, cj * 128, 16, hv, [b_h], spl1)
                    AC.op(lambda cj=cj, s_=s_: nc.scalar.activation(out=cols_view(kv_[:, cj, 128 + p1_0:128 + NC], spl1),
                                                                  in_=slot_view(s_, spl1), func=AF.Copy), reads=bk_, writes=[b_k[cj]])

                def tokmajor(blk, wv_, bw, out_d, also_v=None):
                    for k in range(16):
                        PE.op(lambda k=k: nc.tensor.matmul(out=ps[:, 6 * 512:6 * 512 + 256], lhsT=hv[:, k, blk * 128:(blk + 1) * 128],
                                                           rhs=wv_[:, k, :], start=(k == 0), stop=(k == 15)),
                              reads=[b_h, bw], writes=[b_bank[6]], inc=(k == 15))
                    if out_d is None:
                        AC.op(lambda: nc.scalar.activation(out=vv[:, also_v, :], in_=ps[:, 6 * 512:6 * 512 + 256], func=AF.Copy),
                              reads=[b_bank[6]], writes=[b_v[also_v]])
                    else:
                        o_, bo_ = nextT()
                        AC.op(lambda: nc.scalar.activation(out=o_[:, 0:256], in_=ps[:, 6 * 512:6 * 512 + 256], func=AF.Copy),
                              reads=[b_bank[6]], writes=[bo_])
                        if also_v is not None:
                            DV.op(lambda: nc.vector.tensor_copy(out=vv[:, also_v, :], in_=o_[:, 0:256]), reads=[bo_], writes=[b_v[also_v]])
                        SP.dma(ds_out, out_d, o_[:, 0:256], reads=[bo_])

                if tile == 1 and not (SKIPO & 1):
                    tokmajor(3, wv_, bw, kvp_o[l, 0])
                    tokmajor(4, wv_, bw, kvsn_o[l, 0])
                wt, bw = w_next()
                wv_ = wt[:].rearrange("p (k n) -> p k n", k=16)
                for blk in range(p1_0 // 128, NC // 128):
                    od = None
                    if tile == 1 and blk == 3:
                        od = kvp_o[l, 1]
                    if tile == 1 and blk == 4:
                        od = kvsn_o[l, 1]
                    if SKIPO & 2:
                        od = None
                    tokmajor(blk, wv_, bw, od, also_v=blk + 1)

                if tile == 1 and (SKIPO & 4):
                    pass
                elif tile == 1:
                    t_, bt_ = nextT()
                    DV.op(lambda: nc.vector.tensor_copy(out=t_[:, 0:240].rearrange("p (k n) -> p k n", k=8), in_=uv[:, :, 30 + 482:30 + 512]),
                          reads=b_u, writes=[bt_])
                    SP.dma(ds_out, convp_o[l], t_[:, 0:240], reads=[bt_])
                    for hh in range(2):
                        t_, bt_ = nextT()
                        DV.op(lambda hh=hh, t_=t_: nc.vector.tensor_copy(
                            out=t_[:, 0:512].rearrange("p (k s t) -> p k s t", k=4, s=16), in_=usv[:, 4 * hh:4 * hh + 4, :, 30:38]),
                            reads=b_u, writes=[bt_])
                        SP.dma(ds_out, convsn_o[l][:, 512 * hh:512 * hh + 512], t_[:, 0:512], reads=[bt_])
                else:
                    DV.op(lambda: nc.vector.tensor_copy(out=ucar[l][:].rearrange("p (k n) -> p k n", k=8), in_=uv[:, :, 30 + 866:30 + 896]),
                          reads=b_u, writes=[b_ucar[l]])
                    DV.op(lambda: nc.vector.tensor_copy(out=kcar[l][:].rearrange("p (k n) -> p k n", k=2), in_=kv_[:, :, 128 + 768:128 + 896]),
                          reads=b_k, writes=[b_kcar[l]])
                    DV.op(lambda: nc.vector.tensor_copy(out=vcar[l][:], in_=vv[:, 7, :]), reads=[b_v[7]], writes=[b_vcar[l]])

                if KSTOP == 2 + 10 * tile and l == 0:
                    raise _Stop()
                cc0 = 128 * b_0
                ncv = NC - cc0
                npc = NCP - cc0
                splc = splits(cc0, NC)
                s1 = next_slot()
                s2 = next_slot()
                bk1 = [b_bank[2 * s1], b_bank[2 * s1 + 1]][:len(splc)]
                bk2 = [b_bank[2 * s2], b_bank[2 * s2 + 1]][:len(splc)]
                for cp in range(4):
                    chs = (2 * cp, 2 * cp + 1)
                    accl = [nextT(), nextT()]
                    for ai, ch in enumerate(chs):
                        (acc, bacc), cwo = accl[ai], po + O_CW + ch * 31
                        DV.op(lambda: nc.vector.tensor_scalar(out=acc[:, 0:npc], in0=uv[:, ch, cc0:cc0 + npc], scalar1=pcol(cwo),
                                                             scalar2=pcol(po + O_CB + ch), op0=ALU.mult, op1=ALU.add),
                              reads=[b_u[ch], b_par], writes=[bacc])
                    for j in range(1, 31):
                        for ai, ch in enumerate(chs):
                            (acc, bacc), cwo = accl[ai], po + O_CW + ch * 31
                            DV.op(lambda: nc.vector.scalar_tensor_tensor(out=acc[:, 0:npc], in0=uv[:, ch, cc0 + j:cc0 + j + npc], scalar=pcol(cwo + j),
                                                                         in1=acc[:, 0:npc], op0=ALU.mult, op1=ALU.add),
                                  reads=[b_u[ch], bacc], writes=[bacc])
                    if has_s:
                        for ai, ch in enumerate(chs):
                            (acc, bacc), cwo = accl[ai], po + O_CW + ch * 31
                            accs = acc[:, npc:npc + 128].rearrange("p (s t) -> p s t", s=16)
                            DV.op(lambda: nc.vector.tensor_scalar(out=accs, in0=usv[:, ch, :, 0:8], scalar1=pcol(cwo),
                                                                 scalar2=pcol(po + O_CB + ch), op0=ALU.mult, op1=ALU.add),
                                  reads=[b_u[ch], b_par], writes=[bacc])
                        for j in range(1, 31):
                            for ai, ch in enumerate(chs):
                                (acc, bacc), cwo = accl[ai], po + O_CW + ch * 31
                                accs = acc[:, npc:npc + 128].rearrange("p (s t) -> p s t", s=16)
                                DV.op(lambda: nc.vector.scalar_tensor_tensor(out=accs, in0=usv[:, ch, :, j:j + 8], scalar=pcol(cwo + j),
                                                                             in1=accs, op0=ALU.mult, op1=ALU.add),
                                      reads=[b_u[ch], bacc], writes=[bacc])
                    for ai, ch in enumerate(chs):
                        acc, bacc = accl[ai]
                        AC.op(lambda: nc.scalar.activation(out=cv_[:, ch, cc0:NC], in_=acc[:, 0:ncv], func=AF.Copy), reads=[bacc], writes=[b_c[ch]])
                        i = rr["sq"] % 2
                        rr["sq"] += 1
                        AC.op(lambda: nc.scalar.activation(out=sqs[i][:, 0:ncv], in_=acc[:, 0:ncv], func=AF.Square), reads=[bacc], writes=b_sq[i])
                        for si, (a_, b_) in enumerate(splc):
                            PE.op(lambda: nc.tensor.matmul(out=slot_ap(s1, si, b_ - a_), lhsT=ones[:], rhs=cv_[:, ch, a_:b_],
                                                           start=(ch == 0), stop=(ch == 7)), reads=[b_c[ch], b_id], writes=bk1, inc=True)
                            PE.op(lambda: nc.tensor.matmul(out=slot_ap(s2, si, b_ - a_), lhsT=ones[:], rhs=sqs[i][:, a_ - cc0:b_ - cc0],
                                                           start=(ch == 0), stop=(ch == 7)), reads=b_sq[i] + [b_id], writes=bk2, inc=True)
                mu, bmu = nextT()
                rs_, brs = nextT()
                for si, (a, b) in enumerate(splc):
                    AC.op(lambda si=si, a=a, b=b: nc.scalar.activation(out=mu[:, a - cc0:b - cc0], in_=slot_ap(s1, si, b - a), func=AF.Copy, scale=1.0 / 1024),
                          reads=bk1, writes=[bmu])
                    DV.op(lambda si=si, a=a, b=b: nc.vector.tensor_tensor(out=rs_[:, a - cc0:b - cc0], in0=mu[:, a - cc0:b - cc0], in1=mu[:, a - cc0:b - cc0], op=ALU.mult),
                          reads=[bmu], writes=[brs])
                    DV.op(lambda si=si, a=a, b=b: nc.vector.scalar_tensor_tensor(out=rs_[:, a - cc0:b - cc0], in0=slot_ap(s2, si, b - a), scalar=1.0 / 1024,
                                                                                 in1=rs_[:, a - cc0:b - cc0], op0=ALU.mult, op1=ALU.subtract),
                          reads=bk2 + [brs], writes=[brs])
                AC.op(lambda: nc.scalar.activation(out=rs_[:, 0:ncv], in_=rs_[:, 0:ncv], func=AF.Sqrt, bias=epsb[:, 0:1], scale=1.0), reads=[brs, b_id], writes=[brs])
                DV.op(lambda: nc.vector.reciprocal(out=rs_[:, 0:ncv], in_=rs_[:, 0:ncv]), reads=[brs], writes=[brs])
                for ch in range(8):
                    t_, bt_ = nextT()
                    while t_ is mu or t_ is rs_:
                        t_, bt_ = nextT()
                    DV.op(lambda: nc.vector.tensor_tensor(out=t_[:, 0:ncv], in0=cv_[:, ch, cc0:NC], in1=mu[:, 0:ncv], op=ALU.subtract),
                          reads=[b_c[ch], bmu], writes=[bt_])
                    DV.op(lambda: nc.vector.tensor_tensor(out=t_[:, 0:ncv], in0=t_[:, 0:ncv], in1=rs_[:, 0:ncv], op=ALU.mult),
                          reads=[bt_, brs], writes=[bt_])
                    AC.op(lambda: nc.scalar.activation(out=cv_[:, ch, cc0:NC], in_=t_[:, 0:ncv], func=AF.Silu,
                                                       bias=pcol(po + O_LB + ch), scale=pcol(po + O_LG + ch)),
                          reads=[bt_, b_par], writes=[b_c[ch]])

                if KSTOP == 3 + 10 * tile and l == 0:
                    raise _Stop()
                def attention(blk, is_samp):
                    q0 = blk * 128
                    tabi = 2 if is_samp else (1 if (tile == 0 and blk == 3) else 0)
                    dt_ = dtab[:, tabi * 256:(tabi + 1) * 256]
                    for c in range(8):
                        kc = 0 if c < 4 else 1
                        par_ = rr["at"] % 2
                        rr["at"] += 1
                        sbk = [b_S[0][par_], b_S[1][par_]]
                        sofs = [0 * 512 + par_ * 256, 1 * 512 + par_ * 256]
                        for hf in range(2):
                            pr = slice(hf * 64, hf * 64 + 64)
                            if not is_samp:
                                PE.op(lambda hf=hf, pr=pr: nc.tensor.matmul(out=ps[:, sofs[hf]:sofs[hf] + 256], lhsT=qv[pr, c, q0:q0 + 128],
                                                                          rhs=kv_[pr, kc, q0:q0 + 256], start=True, stop=True),
                                      reads=[b_q[c], b_k[kc]], writes=[sbk[hf]])
                            else:
                                ckv = ckT[:].rearrange("p (k s n) -> p k s n", k=2, s=16)
                                for s in range(16):
                                    PE.op(lambda s=s, pr=pr: nc.tensor.matmul(out=ps[:, 6 * 512 + s * 8:6 * 512 + s * 8 + 8], lhsT=ckv[pr, kc, s, :],
                                                                            rhs=qv[pr, c, q0 + s * 8:q0 + s * 8 + 8], start=True, stop=True),
                                          reads=[b_q[c], b_ckT], writes=[b_bank[6]], inc=(s == 15))
                                DV.op(lambda: nc.vector.tensor_copy(out=stf[:], in_=ps[:, 6 * 512:6 * 512 + 128]), reads=[b_bank[6]], writes=[b_stf])
                                PE.op(lambda hf=hf: nc.tensor.transpose(out=ps[:, sofs[hf]:sofs[hf] + 128], in_=stf[:], identity=identf[:]),
                                      reads=[b_stf, b_id], writes=[sbk[hf]])
                                PE.op(lambda hf=hf, pr=pr: nc.tensor.matmul(out=ps[:, sofs[hf] + 128:sofs[hf] + 256], lhsT=qv[pr, c, q0:q0 + 128],
                                                                          rhs=kv_[pr, kc, 128 + q0:128 + q0 + 128], start=True, stop=True),
                                      reads=[b_q[c], b_k[kc]], writes=[sbk[hf]])
                        sbt, bsb = sbs[par_], b_sb[par_]
                        smt, bsm = sm[par_], b_sm[par_]
                        for hf in range(2):
                            h = PAIRS[c][hf]
                            DV.op(lambda hf=hf, h=h: nc.vector.scalar_tensor_tensor(out=sbt[:, hf * 256:hf * 256 + 256], in0=dt_, scalar=-SLOPES[h],
                                                                                    in1=ps[:, sofs[hf]:sofs[hf] + 256], op0=ALU.mult, op1=ALU.add),
                                  reads=[sbk[hf], b_dtab], writes=[bsb])
                        sk = pcol(po + O_SK + 2 * c, 2)
                        DV.op(lambda: nc.vector.tensor_reduce(out=smt[:, 0:2], in_=sbt[:].rearrange("p (h n) -> p h n", h=2), axis=AX.X, op=ALU.max),
                              reads=[bsb], writes=[bsm])
                        DV.op(lambda: nc.vector.tensor_tensor(out=smt[:, 0:2], in0=smt[:, 0:2], in1=sk, op=ALU.max), reads=[bsm, b_par], writes=[bsm])
                        DV.op(lambda: nc.vector.tensor_scalar(out=smt[:, 2:4], in0=smt[:, 0:2], scalar1=-1.0, scalar2=None, op0=ALU.mult), reads=[bsm], writes=[bsm])
                        DV.op(lambda: nc.vector.tensor_tensor(out=smt[:, 4:6], in0=sk, in1=smt[:, 0:2], op=ALU.subtract), reads=[bsm, b_par], writes=[bsm])
                        pbt, bpb = pbs[par_], b_p[par_]
                        for hf in range(2):
                            AC.op(lambda hf=hf: nc.scalar.activation(out=sbt[:, hf * 256:hf * 256 + 256], in_=sbt[:, hf * 256:hf * 256 + 256], func=AF.Exp,
                                                                     bias=smt[:, 2 + hf:3 + hf], scale=1.0, accum_out=smt[:, 8 + hf:9 + hf]),
                                  reads=[bsb, bsm], writes=[bsb, bsm])
                        AC.op(lambda: nc.scalar.activation(out=smt[:, 6:8], in_=smt[:, 4:6], func=AF.Exp), reads=[bsm], writes=[bsm])
                        DV.op(lambda: nc.vector.tensor_tensor(out=smt[:, 10:12], in0=smt[:, 8:10], in1=smt[:, 6:8], op=ALU.add), reads=[bsm], writes=[bsm])
                        DV.op(lambda: nc.vector.reciprocal(out=smt[:, 12:14], in_=smt[:, 10:12]), reads=[bsm], writes=[bsm])
                        for hf in range(2):
                            DV.op(lambda hf=hf: nc.vector.tensor_scalar(out=pbt[:, hf * 256:hf * 256 + 256], in0=sbt[:, hf * 256:hf * 256 + 256],
                                                                        scalar1=smt[:, 12 + hf:13 + hf], scalar2=None, op0=ALU.mult),
                                  reads=[bsb, bsm], writes=[bpb])
                        for j in range(4):
                            PE.op(lambda j=j: nc.tensor.transpose(out=pt[:, j * 128:(j + 1) * 128], in_=pbt[:, j * 128:(j + 1) * 128], identity=identb[:]),
                                  reads=[bpb, b_id], writes=[b_pt], inc=(j == 3))
                        pTt, bpT = pTs[par_], b_pT[par_]
                        AC.op(lambda: nc.scalar.activation(out=pTt, in_=pt[:, 0:512], func=AF.Copy), reads=[b_pt], writes=[bpT])
                        abank = 2 + (c % 4)
                        aof = abank * 512
                        for hf in range(2):
                            kvh = (0 if c < 4 else 2) + hf
                            pr = slice(hf * 64, hf * 64 + 64)
                            if not is_samp:
                                PE.op(lambda hf=hf, pr=pr, kvh=kvh: nc.tensor.matmul(out=ps[pr, aof:aof + 128], lhsT=vv[:, blk, kvh * 64:kvh * 64 + 64],
                                                                                   rhs=pTt[:, (2 * hf) * 128:(2 * hf + 1) * 128], start=True, stop=False),
                                      reads=[b_v[blk], bpT], writes=[b_bank[abank]], inc=False)
                                PE.op(lambda hf=hf, pr=pr, kvh=kvh: nc.tensor.matmul(out=ps[pr, aof:aof + 128], lhsT=vv[:, blk + 1, kvh * 64:kvh * 64 + 64],
                                                                                   rhs=pTt[:, (2 * hf + 1) * 128:(2 * hf + 2) * 128], start=False, stop=True),
                                      reads=[b_v[blk + 1], bpT], writes=[b_bank[abank]], inc=True)
                            else:
                                cvv = cvt[:].rearrange("p (s d) -> p s d", s=16)
                                PE.op(lambda hf=hf, pr=pr, kvh=kvh: nc.tensor.matmul(out=ps[pr, aof:aof + 128], lhsT=vv[:, blk + 1, kvh * 64:kvh * 64 + 64],
                                                                                   rhs=pTt[:, (2 * hf + 1) * 128:(2 * hf + 2) * 128], start=True, stop=False),
                                      reads=[b_v[blk + 1], bpT], writes=[b_bank[abank]], inc=False)
                                for s in range(16):
                                    PE.op(lambda hf=hf, pr=pr, kvh=kvh, s=s: nc.tensor.matmul(
                                        out=ps[pr, aof + s * 8:aof + s * 8 + 8], lhsT=cvv[:, s, kvh * 64:kvh * 64 + 64],
                                        rhs=pTt[:, (2 * hf) * 128 + s * 8:(2 * hf) * 128 + s * 8 + 8], start=False, stop=(s == 15)),
                                        reads=[b_cv, bpT], writes=[b_bank[abank]], inc=(s == 15))
                        AC.op(lambda: nc.scalar.activation(out=av[:, c, q0:q0 + 128], in_=ps[:, aof:aof + 128], func=AF.Copy),
                              reads=[b_bank[abank]], writes=[b_a[c]])

                b_S = [[Buf("S00"), Buf("S01")], [Buf("S10"), Buf("S11")]]
                alias(b_S[0] + b_S[1], [b_bank[0], b_bank[1]])
                for blk in range(b_0, NBLK):
                    attention(blk, False)
                if has_s:
                    attention(4, True)
                alias([b_bank[0], b_bank[1]], b_S[0] + b_S[1])

                if KSTOP == 4 + 10 * tile and l == 0:
                    raise _Stop()
                if DEBUG and tile == 1 and l == 0:
                    GP.dma(ds_dbg, dbg_o["dbg_c"], cv_, reads=b_c)
                    GP.dma(ds_dbg, dbg_o["dbg_a"], av, reads=b_a)
                    GP.dma(ds_dbg, dbg_o["dbg_q"], qv, reads=b_q)
                    GP.dma(ds_dbg, dbg_o["dbg_k"], kv_[:, :, 128:128 + NC], reads=b_k)
                alias(b_m, b_u + b_q)

                splf = splits(c0, NC)
                nf = NC - c0
                for jj in range(8):
                    for cj in range(2):
                        j = 2 * jj + cj
                        wgc, bwgc = w_next(keep=cj)
                        wga, bwga = wgc, bwgc
                        s_ = next_slot()
                        bb_ = mm_chunk(s_, wgc, bwgc, 0, 16, hv, [b_h], splf)
                        t1, bt1 = nextT()
                        AC.op(lambda: nc.scalar.activation(out=cols_view(t1[:, 0:nf], splf), in_=slot_view(s_, splf), func=AF.Sigmoid), reads=bb_, writes=[bt1])
                        s_ = next_slot()
                        bb_ = mm_chunk(s_, wga, bwga, 128, 16, hv, [b_h], splf)
                        t2, bt2 = nextT()
                        AC.op(lambda: nc.scalar.activation(out=cols_view(t2[:, 0:nf], splf), in_=slot_view(s_, splf), func=AF.Sigmoid), reads=bb_, writes=[bt2])
                        if cj == 0:
                            wco, bwco = w_next()
                        s_ = next_slot()
                        bb_ = mm_chunk(s_, wco, bwco, cj * 128, 8, cv_, b_c, splf, kslot0=0)
                        DV.op(lambda: nc.vector.tensor_tensor(out=cols_view(t1[:, 0:nf], splf), in0=slot_view(s_, splf), in1=cols_view(t1[:, 0:nf], splf), op=ALU.mult),
                              reads=bb_ + [bt1], writes=[bt1])
                        s_ = next_slot()
                        bb_ = mm_chunk(s_, wco, bwco, cj * 128, 8, av, b_a, splf, kslot0=8)
                        DV.op(lambda: nc.vector.tensor_tensor(out=cols_view(t2[:, 0:nf], splf), in0=slot_view(s_, splf), in1=cols_view(t2[:, 0:nf], splf), op=ALU.mult),
                              reads=bb_ + [bt2], writes=[bt2])
                        DV.op(lambda: nc.vector.tensor_tensor(out=mv[:, j, c0:NC], in0=t1[:, 0:nf], in1=t2[:, 0:nf], op=ALU.add),
                              reads=[bt1, bt2], writes=[b_m[j]])

                def resid_add(j, s_, bb_):
                    DV.op(lambda: nc.vector.tensor_tensor(out=cols_view(xv[:, j, c0:NC], splf), in0=slot_view(s_, splf), in1=cols_view(xv[:, j, c0:NC], splf), op=ALU.add),
                          reads=bb_ + [b_x[j]], writes=[b_x[j]])

                def warm_mask(j):
                    if tile == 0 and c0 < 384:
                        DV.op(lambda: nc.vector.tensor_scalar(out=xv[:, j, c0:384], in0=xv[:, j, c0:384], scalar1=vmask[:, 0:1], scalar2=None, op0=ALU.mult),
                              reads=[b_x[j], b_vmask], writes=[b_x[j]])

                for jj in range(8):
                    wt, bw = w_next()
                    for cj in range(2):
                        j = 2 * jj + cj
                        s_ = next_slot()
                        bb_ = mm_chunk(s_, wt, bw, cj * 128, 16, mv, b_m, splf)
                        resid_add(j, s_, bb_)
                        warm_mask(j)

                if DEBUG and tile == 1 and l == 0:
                    GP.dma(ds_dbg, dbg_o["dbg_m"], mv, reads=b_m)
                    GP.dma(ds_dbg, dbg_o["dbg_x"], xv, reads=b_x)
                if KSTOP == 5 + 10 * tile and l == 0:
                    raise _Stop()
                rmsnorm(po + O_G2, c0, hv, b_h)
                if DEBUG and tile == 1 and l == 0:
                    GP.dma(ds_dbg, dbg_o["dbg_h2"], hv, reads=[b_h])

                npf = NCP - c0
                for gi in range(3):
                    for i in range(16):
                        wt, bw = w_next()
                        accs_ = []
                        for cj in range(2):
                            chn = (gi * 16 + i) + 48 * cj
                            s_ = next_slot()
                            bb_ = mm_chunk(s_, wt, bw, cj * 128, 16, hv, [b_h], splf)
                            acc, bacc = nextT()
                            fwo = po + O_FW + chn * 3
                            AC.op(lambda: nc.scalar.activation(out=cols_view(acc[:, 0:nf], splf), in_=slot_view(s_, splf), func=AF.Identity,
                                                               bias=pcol(po + O_FB + chn), scale=pcol(fwo + 2)), reads=bb_ + [b_par], writes=[bacc])

                            def pcols(a, b):
                                out = []
                                for si, (sa_, sb_) in enumerate(splf):
                                    lo, hi = max(a, sa_), min(b, sb_)
                                    if hi > lo:
                                        out.append((si, sa_, lo, hi))
                                return out
                            for tap, wi in ((1, 1), (2, 0)):
                                for (si, sa_, lo, hi) in pcols(c0, NCP - tap):
                                    DV.op(lambda si=si, sa_=sa_, lo=lo, hi=hi, tap=tap, wi=wi: nc.vector.scalar_tensor_tensor(
                                        out=acc[:, lo + tap - c0:hi + tap - c0], in0=slot_ap(s_, si, 512)[:, lo - sa_:hi - sa_], scalar=pcol(fwo + wi),
                                        in1=acc[:, lo + tap - c0:hi + tap - c0], op0=ALU.mult, op1=ALU.add), reads=bb_ + [bacc, b_par], writes=[bacc])
                            if tile == 1:
                                fc = fcar[l][:].rearrange("p (c t) -> p c t", t=2)
                                DV.op(lambda: nc.vector.scalar_tensor_tensor(out=acc[:, 0:1], in0=fc[:, chn, 1:2], scalar=pcol(fwo + 1), in1=acc[:, 0:1],
                                                                             op0=ALU.mult, op1=ALU.add), reads=[bacc, b_fcar[l], b_par], writes=[bacc])
                                DV.op(lambda: nc.vector.scalar_tensor_tensor(out=acc[:, 0:2], in0=fc[:, chn, 0:2], scalar=pcol(fwo + 0), in1=acc[:, 0:2],
                                                                             op0=ALU.mult, op1=ALU.add), reads=[bacc, b_fcar[l], b_par], writes=[bacc])
                            if has_s:
                                si_s = len(splf) - 1
                                sa_ = splf[si_s][0]
                                assert sa_ <= NCP
                                pss = slot_ap(s_, si_s, 512)[:, NCP - sa_:NCP - sa_ + 128].rearrange("p (s t) -> p s t", s=16)
                                acs = acc[:, NCP - c0:NCP - c0 + 128].rearrange("p (s t) -> p s t", s=16)
                                sfv = sffn[:].rearrange("p (c s t) -> p c s t", c=96, s=16)
                                DV.op(lambda: nc.vector.scalar_tensor_tensor(out=acs[:, :, 1:8], in0=pss[:, :, 0:7], scalar=pcol(fwo + 1), in1=acs[:, :, 1:8],
                                                                             op0=ALU.mult, op1=ALU.add), reads=bb_ + [bacc, b_par], writes=[bacc])
                                DV.op(lambda: nc.vector.scalar_tensor_tensor(out=acs[:, :, 2:8], in0=pss[:, :, 0:6], scalar=pcol(fwo + 0), in1=acs[:, :, 2:8],
                                                                             op0=ALU.mult, op1=ALU.add), reads=bb_ + [bacc, b_par], writes=[bacc])
                                DV.op(lambda: nc.vector.scalar_tensor_tensor(out=acs[:, :, 0:1], in0=sfv[:, chn, :, 1:2], scalar=pcol(fwo + 1), in1=acs[:, :, 0:1],
                                                                             op0=ALU.mult, op1=ALU.add), reads=[bacc, b_sffn, b_par], writes=[bacc])
                                DV.op(lambda: nc.vector.scalar_tensor_tensor(out=acs[:, :, 0:2], in0=sfv[:, chn, :, 0:2], scalar=pcol(fwo + 0), in1=acs[:, :, 0:2],
                                                                             op0=ALU.mult, op1=ALU.add), reads=[bacc, b_sffn, b_par], writes=[bacc])
                                AC.op(lambda: nc.scalar.activation(out=sfv[:, chn, :, :], in_=pss[:, :, 6:8], func=AF.Copy), reads=bb_ + [bacc], writes=[b_sffn])
                            si_p = [si for si, (sa2, sb2) in enumerate(splf) if sa2 < NCP][-1]
                            sa2 = splf[si_p][0]
                            fc = fcar[l][:].rearrange("p (c t) -> p c t", t=2)
                            AC.op(lambda: nc.scalar.activation(out=fc[:, chn, :], in_=slot_ap(s_, si_p, 512)[:, NCP - 2 - sa2:NCP - sa2], func=AF.Copy),
                                  reads=bb_ + [bacc], writes=[b_fcar[l]])
                            accs_.append((acc, bacc))
                        (ag, bag), (avl, bavl) = accs_
                        AC.op(lambda: nc.scalar.activation(out=ag[:, 0:nf], in_=ag[:, 0:nf], func=AF.Silu), reads=[bag], writes=[bag])
                        DV.op(lambda: nc.vector.tensor_tensor(out=mv[:, i, c0:NC], in0=ag[:, 0:nf], in1=avl[:, 0:nf], op=ALU.mult),
                              reads=[bag, bavl], writes=[b_m[i]])
                    for jj in range(8):
                        wt, bw = w_next()
                        for cj in range(2):
                            j = 2 * jj + cj
                            s_ = next_slot()
                            bb_ = mm_chunk(s_, wt, bw, cj * 128, 16, mv, b_m, splf)
                            resid_add(j, s_, bb_)
                            if gi == 2:
                                warm_mask(j)
                if KSTOP == 6 + 10 * tile and l == 0:
                    raise _Stop()
                if KSTOP == 7 + 10 * tile and l == 1:
                    raise _Stop()
                if tile == 1:
                    SP.dma(ds_out, ffnp_o[l], fcar[l][:], reads=[b_fcar[l]])
                    SP.dma(ds_out, ffns_o[l], sffn[:], reads=[b_sffn])

            if tile == 0:
                rmsnorm(2 * PL, 384, None, None, is_final=True, ycol0=0)
            else:
                rmsnorm(2 * PL, 0, None, None, is_final=True, ycol0=512)
            barrier()

    epsb = sb("epsb", [128, 1], F32)
    DV.op(lambda: nc.vector.memset(epsb[:], EPS), writes=[b_id])
    try:
        run_tile(0)
        if KSTOP == 8:
            raise _Stop()
        run_tile(1)
    except _Stop:
        barrier()
    for d in (ds_out, ds_copy):
        nc.sync.wait_ge(d.sem, d.val)
    return nc


def _unit(W, rows, cols):
    U = np.zeros((128, 16, 256), np.float32)
    for k, r in enumerate(rows):
        if r is None:
            continue
        if isinstance(r, np.ndarray):
            U[:, k, :] = W[r][:, cols]
        else:
            U[:, k, :] = W[r:r + 128][:, cols]
    return U.reshape(128, 4096)


def _pack_layer(w_in, w_co, w_ao, w_out, w_up, w_down):
    units = []
    r16 = [k * 128 for k in range(16)]
    ar = np.arange
    for i in range(8):
        units.append(_unit(w_in, r16, np.concatenate([ar(i * 128, i * 128 + 128), ar(1024 + i * 128, 1024 + i * 128 + 128)])))
    for i in range(4):
        cols = []
        for c in (2 * i, 2 * i + 1):
            for h in PAIRS[c]:
                cols.append(ar(2048 + h * 64, 2048 + h * 64 + 64))
        units.append(_unit(w_in, r16, np.concatenate(cols)))
    units.append(_unit(w_in, r16, ar(3072, 3328)))
    units.append(_unit(w_in, r16, ar(3328, 3584)))
    arow = []
    for c in range(8):
        arow.append(np.concatenate([ar(h * 64, h * 64 + 64) for h in PAIRS[c]]))
    for jj in range(8):
        j0, j1 = 2 * jj, 2 * jj + 1
        units.append(_unit(w_in, r16, np.concatenate([ar(3584 + j0 * 128, 3584 + j0 * 128 + 128), ar(5632 + j0 * 128, 5632 + j0 * 128 + 128)])))
        U = np.zeros((128, 16, 256), np.float32)
        cols = ar(jj * 256, jj * 256 + 256)
        for k in range(8):
            U[:, k, :] = w_co[k * 128:(k + 1) * 128][:, cols]
            U[:, 8 + k, :] = w_ao[arow[k]][:, cols]
        units.append(U.reshape(128, 4096))
        units.append(_unit(w_in, r16, np.concatenate([ar(3584 + j1 * 128, 3584 + j1 * 128 + 128), ar(5632 + j1 * 128, 5632 + j1 * 128 + 128)])))
    for j in range(8):
        units.append(_unit(w_out, r16, ar(j * 256, j * 256 + 256)))
    for gi in range(3):
        for i in range(16):
            c = gi * 16 + i
            units.append(_unit(w_up, r16, np.concatenate([ar(c * 128, c * 128 + 128), ar(6144 + c * 128, 6144 + c * 128 + 128)])))
        for j in range(8):
            units.append(_unit(w_down, [(gi * 16 + k) * 128 for k in range(16)], ar(j * 256, j * 256 + 256)))
    assert len(units) == NU
    return np.stack(units)


def _fm(v):
    return np.ascontiguousarray(v.reshape(-1, 128).T)


_NC_CACHE = {}
_PREP_ONLY = False


def kernel(x_prompt, x_sample, cache_k, cache_v, state_conv, state_ffn_conv,
           norm1_g, w_in, conv_w, conv_b, conv_ln_g, conv_ln_b, w_conv_out,
           attn_sinks, w_attn_out, w_out, norm2_g, w_up, ffn_conv_w, ffn_conv_b,
           w_down, final_norm_g):
    f = np.float32
    A = lambda a: np.asarray(a, dtype=f)
    x_prompt, x_sample = A(x_prompt), A(x_sample)
    cache_k, cache_v, state_conv, state_ffn_conv = A(cache_k), A(cache_v), A(state_conv), A(state_ffn_conv)
    wp = [_pack_layer(A(w_in[l]), A(w_conv_out[l]), A(w_attn_out[l]), A(w_out[l]), A(w_up[l]), A(w_down[l])) for l in range(L)]
    par = np.zeros((128, 2 * PL + 16), f)
    for l in range(L):
        o = l * PL
        par[:, o + O_G1:o + O_G1 + 16] = _fm(A(norm1_g[l]))
        par[:, o + O_G2:o + O_G2 + 16] = _fm(A(norm2_g[l]))
        cw = A(conv_w[l])
        par[:, o + O_CW:o + O_CW + 248] = cw.T.reshape(8, 128, 31).transpose(1, 0, 2).reshape(128, 248)
        par[:, o + O_CB:o + O_CB + 8] = _fm(A(conv_b[l]))
        par[:, o + O_LG:o + O_LG + 8] = _fm(A(conv_ln_g[l]))
        par[:, o + O_LB:o + O_LB + 8] = _fm(A(conv_ln_b[l]))
        sk = A(attn_sinks[l])
        par[:, o + O_SK:o + O_SK + 16] = np.array([sk[PAIRS[c][hf]] for c in range(8) for hf in range(2)], f)[None, :]
        fw = A(ffn_conv_w[l])
        par[:, o + O_FW:o + O_FW + 288] = fw.T.reshape(96, 128, 3).transpose(1, 0, 2).reshape(128, 288)
        par[:, o + O_FB:o + O_FB + 96] = _fm(A(ffn_conv_b[l]))
    par[:, 2 * PL:2 * PL + 16] = _fm(A(final_norm_g))
    ident = np.eye(128, dtype=f)
    a_ = np.arange(128)[:, None]
    j_ = np.arange(256)[None, :]
    dist = (128 + a_ - j_).astype(f)
    gen = np.where((j_ >= a_) & (j_ <= 128 + a_), dist, BIG).astype(f)
    s_r, t_r = np.arange(128)[:, None] // 8, np.arange(128)[:, None] % 8
    jc = np.arange(128)[None, :]
    dc = np.where(jc >= t_r, (128 + t_r - jc), BIG)
    s_c, t_c = jc // 8, jc % 8
    dn = np.where((s_c == s_r) & (t_c <= t_r), (t_r - t_c), BIG)
    samp = np.concatenate([dc, dn], axis=1).astype(f)

    in_maps = []
    for c in range(8):
        b, seg = c // 4, c % 4
        s0 = 16 * c
        xx = np.zeros((1536, D), f)
        p0 = seg * 1024 - 384
        lo = max(p0, 0)
        xx[lo - p0:1408] = x_prompt[b, lo:seg * 1024 + 1024]
        xx[1408:] = x_sample[s0:s0 + 16].reshape(128, D)
        xT = np.ascontiguousarray(xx.T.reshape(16, 128, 1536).transpose(1, 0, 2))
        first = gen.copy()
        if seg == 0:
            first[:, :128] = BIG
        dt = np.concatenate([gen, first, samp], axis=1)
        vm = np.full((128, 1), 0.0 if seg == 0 else 1.0, f)
        sc = state_conv[:, s0:s0 + 16]
        scT = np.zeros((L, 128, 8, 16, 38), f)
        scT[..., :30] = sc.reshape(L, 16, 30, 8, 128).transpose(0, 4, 3, 1, 2)
        sf = state_ffn_conv[:, s0:s0 + 16]
        sfT = sf.reshape(L, 16, 2, 96, 128).transpose(0, 4, 3, 1, 2)
        ck = cache_k[:, s0:s0 + 16].reshape(L, 16, 128, 256)
        cvv = cache_v[:, s0:s0 + 16].reshape(L, 16, 128, 256)
        ckT = ck.reshape(L, 16, 128, 2, 2, 64).transpose(0, 4, 5, 3, 1, 2).reshape(L, 128, 2, 2048)
        m = {
            "xT": xT, "vmask": vm, "dtab": np.ascontiguousarray(dt), "par": par, "wp0": wp[0], "wp1": wp[1],
            "sconvT": np.ascontiguousarray(scT.reshape(L, 128, 8, 608)),
            "sffnT": np.ascontiguousarray(sfT.reshape(L, 128, 96 * 32)),
            "ckT": np.ascontiguousarray(ckT), "cv": np.ascontiguousarray(cvv), "ck": np.ascontiguousarray(ck),
            "sconvn": np.ascontiguousarray(sc), "ident": ident,
        }
        in_maps.append(m)

    if _PREP_ONLY:
        return in_maps
    if "nc" not in _NC_CACHE:
        _NC_CACHE["nc"] = build_program()
    nc = _NC_CACHE["nc"]
    res = run_bass_kernel_spmd(nc, in_maps, core_ids=list(range(8)))
    return _post(res.results)


def _post(R):
    f = np.float32
    y_prompt = np.zeros((2, 4096, D), f)
    y_sample = np.zeros((128, 8, D), f)
    k_p = np.zeros((L, 2, 128, 4, 64), f)
    v_p = np.zeros((L, 2, 128, 4, 64), f)
    conv_p = np.zeros((L, 2, 30, 1024), f)
    ffn_p = np.zeros((L, 2, 2, 12288), f)
    k_s = np.zeros((L, 128, 128, 4, 64), f)
    v_s = np.zeros((L, 128, 128, 4, 64), f)
    conv_s = np.zeros((L, 128, 30, 1024), f)
    ffn_s = np.zeros((L, 128, 2, 12288), f)
    for c in range(8):
        b, seg = c // 4, c % 4
        s0 = 16 * c
        r = R[c]
        y = np.asarray(r["yT"]).transpose(1, 0, 2).reshape(D, 1152).T
        y_prompt[b, seg * 1024:(seg + 1) * 1024] = y[:1024]
        y_sample[s0:s0 + 16] = y[1024:].reshape(16, 8, D)
        kvsn = np.asarray(r["kvsn"])
        k_s[:, s0:s0 + 16, :120] = np.asarray(r["kso"]).reshape(L, 16, 120, 4, 64)
        v_s[:, s0:s0 + 16, :120] = np.asarray(r["vso"]).reshape(L, 16, 120, 4, 64)
        k_s[:, s0:s0 + 16, 120:] = kvsn[:, 0].reshape(L, 16, 8, 4, 64)
        v_s[:, s0:s0 + 16, 120:] = kvsn[:, 1].reshape(L, 16, 8, 4, 64)
        conv_s[:, s0:s0 + 16, :22] = np.asarray(r["convso"])
        csn = np.asarray(r["convsn"]).reshape(L, 128, 8, 16, 8)
        conv_s[:, s0:s0 + 16, 22:] = csn.transpose(0, 3, 4, 2, 1).reshape(L, 16, 8, 1024)
        fs = np.asarray(r["ffns"]).reshape(L, 128, 96, 16, 2)
        ffn_s[:, s0:s0 + 16] = fs.transpose(0, 3, 4, 2, 1).reshape(L, 16, 2, 12288)
        if seg == 3:
            kvp = np.asarray(r["kvp"])
            k_p[:, b] = kvp[:, 0].reshape(L, 128, 4, 64)
            v_p[:, b] = kvp[:, 1].reshape(L, 128, 4, 64)
            cp = np.asarray(r["convp"]).reshape(L, 128, 8, 30)
            conv_p[:, b] = cp.transpose(0, 3, 2, 1).reshape(L, 30, 1024)
            fp = np.asarray(r["ffnp"]).reshape(L, 128, 96, 2)
            ffn_p[:, b] = fp.transpose(0, 3, 2, 1).reshape(L, 2, 12288)
    return (y_prompt, y_sample, k_p, v_p, conv_p, ffn_p, k_s, v_s, conv_s, ffn_s)
```

```python
import numpy as np
from contextlib import ExitStack
import concourse.bass as bass
import concourse.mybir as mybir
from concourse.bass_utils import run_bass_kernel_spmd

F32 = mybir.dt.float32
BF16 = mybir.dt.bfloat16
ALU = mybir.AluOpType
AF = mybir.ActivationFunctionType
AX = mybir.AxisListType

L = 2
D = 2048
NU = 118
NW = 3
EPS = 1e-6
PAIRS = [(c, c + 4) if c < 4 else (8 + c - 4, 12 + c - 4) for c in range(8)]
SLOPES = [2.0 ** (-(h + 1) / 2.0) for h in range(16)]
PL = 704
O_G1, O_G2, O_CW, O_CB, O_LG, O_LB, O_SK, O_FW, O_FB = 0, 16, 32, 280, 288, 296, 304, 320, 608
SAME_SYNC = True
DEBUG = False
import os
KSTOP = int(os.environ.get("KSTOP", "0"))
SKIPO = int(os.environ.get("SKIPO", "0"))


class _Stop(Exception):
    pass
BIG = 1.0e9


class Buf:
    __slots__ = ("w", "r", "name")

    def __init__(self, name):
        self.w = None
        self.r = {}
        self.name = name


class DSem:
    def __init__(self, nc, name):
        self.sem = nc.alloc_semaphore(name=name)
        self.val = 0
        self.exact = False


class Eng:
    def __init__(self, nc, eng, name):
        self.eng = eng
        self.name = name
        self.ds = DSem(nc, "c_" + name)
        self.ds.exact = True
        self.known = {}
        self.pend = []
        self.self_sync = SAME_SYNC and name in ("dve", "act")

    def _need(self, need, tk):
        if tk is None:
            return
        ds, val = tk
        if ds is self.ds:
            if not self.self_sync or val is None:
                return
        if val is None:
            raise RuntimeError("dependency on pending (non-inc) op of " + ds.sem.name if hasattr(ds.sem, "name") else "pending dep")
        if not ds.exact:
            val = ds.val
        if need.get(ds, 0) < val:
            need[ds] = val

    def waits(self, reads, writes, raw_same=True):
        need = {}
        for b in reads:
            self._need(need, b.w)
        for b in writes:
            self._need(need, b.w)
            for ds, val in b.r.items():
                if ds is self.ds:
                    continue
                self._need(need, (ds, val))
        for ds, val in need.items():
            if self.known.get(ds, 0) < val:
                self.eng.wait_ge(ds.sem, val)
                self.known[ds] = val

    def op(self, fn, reads=(), writes=(), inc=True):
        self.waits(reads, writes)
        ins = fn()
        if inc:
            self.ds.val += 1
            ins.then_inc(self.ds.sem, 1)
            tk = (self.ds, self.ds.val)
            for kind, b in self.pend:
                if kind == "r":
                    b.r[self.ds] = self.ds.val
                else:
                    if b.w is not None and b.w[0] is self.ds and b.w[1] is None:
                        b.w = tk
            self.pend = []
            for b in reads:
                b.r[self.ds] = self.ds.val
            for b in writes:
                b.w = tk
                b.r = {}
        else:
            for b in reads:
                b.r[self.ds] = None
                self.pend.append(("r", b))
            for b in writes:
                b.w = (self.ds, None)
                b.r = {}
                self.pend.append(("w", b))
        return ins

    def dma(self, dsem, out, in_, reads=(), writes=(), **kw):
        self.waits(reads, writes)
        dsem.val += 16
        self.eng.dma_start(out=out, in_=in_, **kw).then_inc(dsem.sem, 16)
        tk = (dsem, dsem.val)
        for b in reads:
            b.r[dsem] = dsem.val
        for b in writes:
            b.w = tk
            b.r = {}


def build_program():
    nc = bass.Bass("TRN2", target_bir_lowering=False)
    PE = Eng(nc, nc.tensor, "pe")
    DV = Eng(nc, nc.vector, "dve")
    AC = Eng(nc, nc.scalar, "act")
    SP = Eng(nc, nc.sync, "sp")
    GP = Eng(nc, nc.gpsimd, "pool")
    engs = [PE, DV, AC, SP, GP]

    def dram(name, shape, kind):
        return nc.dram_tensor(name, list(shape), F32, kind=kind).ap()

    xT_d = dram("xT", [128, 16, 1536], "ExternalInput")
    vmask_d = dram("vmask", [128, 1], "ExternalInput")
    dtab_d = dram("dtab", [128, 3 * 256], "ExternalInput")
    par_d = dram("par", [128, 2 * PL + 16], "ExternalInput")
    wp_d = [dram(f"wp{l}", [NU, 128, 16 * 256], "ExternalInput") for l in range(L)]
    sconv_d = dram("sconvT", [L, 128, 8, 608], "ExternalInput")
    sffn_d = dram("sffnT", [L, 128, 96 * 32], "ExternalInput")
    ckT_d = dram("ckT", [L, 128, 2, 2048], "ExternalInput")
    cv_d = dram("cv", [L, 16, 128, 256], "ExternalInput")
    ck_d = dram("ck", [L, 16, 128, 256], "ExternalInput")
    sconvn_d = dram("sconvn", [L, 16, 30, 1024], "ExternalInput")
    ident_d = dram("ident", [128, 128], "ExternalInput")

    yT_o = dram("yT", [128, 16, 1152], "ExternalOutput")
    kvp_o = dram("kvp", [L, 2, 128, 256], "ExternalOutput")
    convp_o = dram("convp", [L, 128, 8 * 30], "ExternalOutput")
    ffnp_o = dram("ffnp", [L, 128, 96 * 2], "ExternalOutput")
    kvsn_o = dram("kvsn", [L, 2, 128, 256], "ExternalOutput")
    kso_o = dram("kso", [L, 16, 120, 256], "ExternalOutput")
    vso_o = dram("vso", [L, 16, 120, 256], "ExternalOutput")
    convsn_o = dram("convsn", [L, 128, 8 * 128], "ExternalOutput")
    convso_o = dram("convso", [L, 16, 22, 1024], "ExternalOutput")
    ffns_o = dram("ffns", [L, 128, 96 * 32], "ExternalOutput")

    dbg_o = {}
    if DEBUG:
        for nm, kk in (("dbg_c", 8), ("dbg_a", 8), ("dbg_m", 16), ("dbg_x", 16), ("dbg_h2", 16), ("dbg_q", 8), ("dbg_k", 2)):
            dbg_o[nm] = dram(nm, [128, kk, 640], "ExternalOutput")
    ds_dbg = DSem(nc, "d_dbg")
    ds_const = DSem(nc, "d_const")
    ds_x = DSem(nc, "d_x")
    ds_samp = DSem(nc, "d_samp")
    ds_sampg = DSem(nc, "d_sampg")
    ds_out = DSem(nc, "d_out")
    ds_copy = DSem(nc, "d_copy")
    ds_w = [DSem(nc, f"d_w{i}") for i in range(NW)]
    for d in ds_w:
        d.exact = True

    def sb(name, shape, dt):
        return nc.alloc_sbuf_tensor("s_" + name, list(shape), dt)

    par = sb("par", [128, 2 * PL + 16], F32)
    dtab = sb("dtab", [128, 768], F32)
    vmask = sb("vmaskt", [128, 1], F32)
    identf = sb("identf", [128, 128], F32)
    identb = sb("identb", [128, 128], BF16)
    ones = sb("ones", [128, 128], BF16)
    wring = [sb(f"wr{i}", [128, 4096], BF16) for i in range(NW)]
    kcar = [sb(f"kcar{l}", [128, 256], BF16) for l in range(L)]
    vcar = [sb(f"vcar{l}", [128, 256], BF16) for l in range(L)]
    ucar = [sb(f"ucar{l}", [128, 240], BF16) for l in range(L)]
    fcar = [sb(f"fcar{l}", [128, 192], F32) for l in range(L)]
    ps = nc.alloc_psum_tensor("ps", [128, 7 * 512], F32)
    pt = nc.alloc_psum_tensor("pt", [128, 1024], BF16)

    b_par, b_dtab, b_vmask, b_id = Buf("par"), Buf("dtab"), Buf("vmask"), Buf("ident")
    b_wr = [Buf(f"wr{i}") for i in range(NW)]
    b_bank = [Buf(f"bank{i}") for i in range(7)]
    b_pt = Buf("pt")
    b_kcar = [Buf("kcar") for _ in range(L)]
    b_vcar = [Buf("vcar") for _ in range(L)]
    b_ucar = [Buf("ucar") for _ in range(L)]
    b_fcar = [Buf("fcar") for _ in range(L)]

    def pcol(off, n=1):
        return par[:, off:off + n]

    SP.dma(ds_const, par[:], par_d, writes=[b_par])
    SP.dma(ds_const, dtab[:], dtab_d, writes=[b_dtab])
    SP.dma(ds_const, vmask[:], vmask_d, writes=[b_vmask])
    SP.dma(ds_const, identf[:], ident_d, writes=[b_id])
    DV.op(lambda: nc.vector.tensor_copy(out=identb[:], in_=identf[:]), reads=[b_id], writes=[b_id])
    DV.op(lambda: nc.vector.memset(ones[:], 1.0), writes=[b_id])
    for l in range(L):
        DV.op(lambda: nc.vector.memset(fcar[l][:], 0.0), writes=[b_fcar[l]])
    for l in range(L):
        SP.dma(ds_copy, kso_o[l], ck_d[l, :, 8:128, :])
        SP.dma(ds_copy, vso_o[l], cv_d[l, :, 8:128, :])
        SP.dma(ds_copy, convso_o[l], sconvn_d[l, :, 8:30, :])

    stream = []
    for tile in range(2):
        for l in range(L):
            for u in range(NU):
                stream.append((l, u))
    wstate = {"issued": 0, "pos": 0}

    def w_issue_upto(n):
        while wstate["issued"] < min(n, len(stream)):
            i = wstate["issued"]
            l, u = stream[i]
            s = i % NW
            GP.dma(ds_w[s], wring[s][:], wp_d[l][u], writes=[b_wr[s]])
            wstate["issued"] += 1

    def w_next(keep=0):
        i = wstate["pos"]
        w_issue_upto(i + NW - keep)
        wstate["pos"] += 1
        s = i % NW
        return wring[s], b_wr[s]

    def barrier():
        for e in engs:
            for o in engs:
                if o is e or o.ds.val == 0:
                    continue
                if e.known.get(o.ds, 0) < o.ds.val:
                    e.eng.wait_ge(o.ds.sem, o.ds.val)
                    e.known[o.ds] = o.ds.val
            for d in [ds_const, ds_x, ds_samp, ds_sampg, ds_out, ds_dbg] + ds_w:
                if d.val and e.known.get(d, 0) < d.val:
                    e.eng.wait_ge(d.sem, d.val)
                    e.known[d] = d.val

    slot_rr = {"i": 0}

    def next_slot():
        i = slot_rr["i"] % 3
        slot_rr["i"] += 1
        return i

    def run_tile(tile):
        NC = 896 if tile == 0 else 640
        NCP = 896 if tile == 0 else 512
        NBLK = NCP // 128
        has_s = tile == 1
        xoff = 0 if tile == 0 else 896
        with ExitStack() as es:
            def tb(name, shape, dt):
                return es.enter_context(nc.sbuf_tensor(f"t{tile}_{name}", list(shape), dt))
            xT = tb("xT", [128, 16 * NC], F32)
            hT = tb("hT", [128, 16 * NC], BF16)
            UW = 30 + NCP
            MW = 16 * NC
            uq_cols = max(MW, 8 * UW + 8 * NC + (8 * 16 * 38 if has_s else 0))
            uq = tb("uq", [128, uq_cols], BF16)
            cT = tb("cT", [128, 8 * NC], BF16)
            aT = tb("aT", [128, 8 * NC], BF16)
            kT = tb("kT", [128, 2 * (128 + NC)], BF16)
            Vt = tb("V", [128, (NC // 128 + 1) * 256], BF16)
            TS = [tb(f"T{i}", [128, 928], F32) for i in range(3)]
            sbs = [tb(f"sb{i}", [128, 512], F32) for i in range(2)]
            pball = tb("pb", [128, 1024], BF16)
            pTall = tb("pTt", [128, 1024], BF16)
            pbs = [pball[:, 0:512], pball[:, 512:1024]]
            pTs = [pTall[:, 0:512], pTall[:, 512:1024]]
            sqs = [pball, pTall]
            sm = [tb(f"sm{i}", [128, 16], F32) for i in range(2)]
            if has_s:
                sffn = tb("sffn", [128, 96 * 32], F32)
                ckT = tb("ckTs", [128, 4096], BF16)
                cvt = tb("cvs", [128, 16 * 256], BF16)
                stf = tb("stf", [128, 128], F32)
                b_sffn, b_ckT, b_cv, b_stf = Buf("sffn"), Buf("ckT"), Buf("cv"), Buf("stf")

            b_x = [Buf(f"x{k}") for k in range(16)]
            b_h = Buf("hT")
            b_u = [Buf(f"u{k}") for k in range(8)]
            b_q = [Buf(f"q{k}") for k in range(8)]
            b_c = [Buf(f"c{k}") for k in range(8)]
            b_a = [Buf(f"a{k}") for k in range(8)]
            b_k = [Buf(f"k{k}") for k in range(2)]
            b_v = [Buf(f"v{k}") for k in range(NC // 128 + 1)]
            b_m = [Buf(f"m{k}") for k in range(16)]
            b_T = [Buf(f"T{k}") for k in range(3)]
            b_sb = [Buf("sb0"), Buf("sb1")]
            b_p = [Buf("p0"), Buf("p1")]
            b_pT = [Buf("pT0"), Buf("pT1")]
            b_sq = [b_p, b_pT]
            b_sm = [Buf("sm0"), Buf("sm1")]
            rr = {"T": 0, "sq": 0, "at": 0, "ost": 0}

            def nextT():
                i = rr["T"] % 3
                rr["T"] += 1
                return TS[i], b_T[i]

            xv = xT[:].rearrange("p (k n) -> p k n", k=16)
            hv = hT[:].rearrange("p (k n) -> p k n", k=16)
            uv = uq[:, 0:8 * UW].rearrange("p (k n) -> p k n", k=8)
            qv = uq[:, 8 * UW:8 * UW + 8 * NC].rearrange("p (k n) -> p k n", k=8)
            if has_s:
                so = 8 * UW + 8 * NC
                usv = uq[:, so:so + 8 * 608].rearrange("p (k s r) -> p k s r", k=8, s=16)
            mv = uq[:, 0:MW].rearrange("p (k n) -> p k n", k=16)
            cv_ = cT[:].rearrange("p (k n) -> p k n", k=8)
            av = aT[:].rearrange("p (k n) -> p k n", k=8)
            kv_ = kT[:].rearrange("p (k n) -> p k n", k=2)
            vv = Vt[:].rearrange("p (b d) -> p b d", d=256)

            def splits(c0, c1):
                n = c1 - c0
                if n <= 512:
                    return [(c0, c1)]
                h = (n // 2 + 1) // 2 * 2
                return [(c0, c0 + h), (c0 + h, c1)]

            def slot_ap(slot, sp, n):
                return ps[:, (2 * slot + sp) * 512:(2 * slot + sp) * 512 + n]

            def slot_view(slot, spl):
                n = spl[0][1] - spl[0][0]
                if len(spl) == 1:
                    return ps[:, 2 * slot * 512:2 * slot * 512 + n]
                assert spl[1][1] - spl[1][0] == n
                return ps[:, 2 * slot * 512:(2 * slot + 2) * 512].rearrange("p (s n) -> p s n", s=2)[:, :, 0:n]

            def cols_view(ap2d, spl):
                if len(spl) == 1:
                    return ap2d
                return ap2d.rearrange("p (s n) -> p s n", s=2)

            def mm_chunk(slot, wt, bw, wcol, nk, src, bsrc, spl, kslot0=0):
                wv = wt[:].rearrange("p (k n) -> p k n", k=16)
                banks = [b_bank[2 * slot], b_bank[2 * slot + 1]][:len(spl)]
                for k in range(nk):
                    for si, (a, b) in enumerate(spl):
                        last = (k == nk - 1) and (si == len(spl) - 1)
                        PE.op(lambda k=k, si=si, a=a, b=b: nc.tensor.matmul(
                            out=slot_ap(slot, si, b - a), lhsT=wv[:, kslot0 + k, wcol:wcol + 128],
                            rhs=src[:, k, a:b], start=(k == 0), stop=(k == nk - 1)),
                            reads=[bw] + list(bsrc), writes=banks, inc=last)
                return banks

            SP.dma(ds_x, xv, xT_d[:, :, xoff:xoff + NC], writes=b_x)

            def rmsnorm(goff, r0, dst, bdst, is_final=False, ycol0=0, c_out0=0):
                spl = splits(r0, NC)
                slot = next_slot()
                banks = [b_bank[2 * slot], b_bank[2 * slot + 1]][:len(spl)]
                for k in range(16):
                    i = rr["sq"] % 2
                    rr["sq"] += 1
                    AC.op(lambda k=k, i=i: nc.scalar.activation(out=sqs[i][:, 0:NC - r0], in_=xv[:, k, r0:NC], func=AF.Square),
                          reads=[b_x[k]], writes=b_sq[i])
                    for si, (a, b) in enumerate(spl):
                        last = (k == 15) and (si == len(spl) - 1)
                        PE.op(lambda k=k, i=i, si=si, a=a, b=b: nc.tensor.matmul(
                            out=slot_ap(slot, si, b - a), lhsT=ones[:], rhs=sqs[i][:, a - r0:b - r0],
                            start=(k == 0), stop=(k == 15)), reads=b_sq[i] + [b_id], writes=banks, inc=True)
                rt, brt = nextT()
                n = NC - r0
                for si, (a, b) in enumerate(spl):
                    AC.op(lambda si=si, a=a, b=b: nc.scalar.activation(out=rt[:, a - r0:b - r0], in_=slot_ap(slot, si, b - a),
                                                                     func=AF.Sqrt, bias=epsb[:, 0:1], scale=1.0 / D),
                          reads=banks + [b_id], writes=[brt])
                DV.op(lambda: nc.vector.reciprocal(out=rt[:, 0:n], in_=rt[:, 0:n]), reads=[brt], writes=[brt])
                if not is_final:
                    for k in range(16):
                        DV.op(lambda k=k: nc.vector.scalar_tensor_tensor(
                            out=dst[:, k, r0:NC], in0=xv[:, k, r0:NC], scalar=pcol(goff + k), in1=rt[:, 0:n],
                            op0=ALU.mult, op1=ALU.mult), reads=[b_x[k], brt, b_par], writes=[bdst])
                else:
                    for k in range(16):
                        ot, bot = nextT()
                        if ot is rt:
                            ot, bot = nextT()
                        DV.op(lambda k=k, ot=ot: nc.vector.scalar_tensor_tensor(
                            out=ot[:, 0:n], in0=xv[:, k, r0:NC], scalar=pcol(goff + k), in1=rt[:, 0:n],
                            op0=ALU.mult, op1=ALU.mult), reads=[b_x[k], brt, b_par], writes=[bot])
                        SP.dma(ds_out, yT_o[:, k, ycol0:ycol0 + n], ot[:, 0:n], reads=[bot])

            def alias(dst, srcs):
                acc = {}
                for bb in srcs:
                    items = list(bb.r.items())
                    if bb.w is not None:
                        items.append(bb.w)
                    for dsr, vr in items:
                        if vr is None:
                            raise RuntimeError("alias on pending op")
                        acc[dsr] = max(acc.get(dsr, 0), vr)
                for bd in dst:
                    for dsr, vr in acc.items():
                        bd.r[dsr] = max(bd.r.get(dsr, 0) or 0, vr)

            for l in range(L):
                po = l * PL
                alias(b_u + b_q, b_m)
                if tile == 0:
                    p1_0, b_0, c0 = (0, 1, 248) if l == 0 else (128, 2, 376)
                else:
                    p1_0, b_0, c0 = 0, 0, 0
                if has_s:
                    GP.dma(ds_sampg, uq[:, so:so + 8 * 608].rearrange("p (k n) -> p k n", k=8), sconv_d[l], writes=b_u)
                    SP.dma(ds_samp, sffn[:], sffn_d[l], writes=[b_sffn])
                    GP.dma(ds_sampg, ckT[:].rearrange("p (k n) -> p k n", k=2), ckT_d[l], writes=[b_ckT])
                    GP.dma(ds_sampg, cvt[:].rearrange("p (s d) -> p s d", s=16), cv_d[l].rearrange("s k d -> k s d"), writes=[b_cv])
                if tile == 1:
                    DV.op(lambda: nc.vector.tensor_copy(out=kv_[:, :, 0:128], in_=kcar[l][:].rearrange("p (k n) -> p k n", k=2)),
                          reads=[b_kcar[l]], writes=b_k)
                    DV.op(lambda: nc.vector.tensor_copy(out=vv[:, 0, :], in_=vcar[l][:]), reads=[b_vcar[l]], writes=[b_v[0]])
                    DV.op(lambda: nc.vector.tensor_copy(out=uv[:, :, 0:30], in_=ucar[l][:].rearrange("p (k n) -> p k n", k=8)),
                          reads=[b_ucar[l]], writes=b_u)
                else:
                    DV.op(lambda: nc.vector.memset(uv[:, :, 0:30], 0.0), writes=b_u)

                rmsnorm(po + O_G1, p1_0, hv, b_h)

                if KSTOP == 1 + 10 * tile and l == 0:
                    raise _Stop()
                spl1 = splits(p1_0, NC)
                n1 = NC - p1_0
                for i in range(8):
                    wt, bw = w_next()
                    sa = next_slot()
                    ba = mm_chunk(sa, wt, bw, 0, 16, hv, [b_h], spl1)
                    sg = next_slot()
                    bg = mm_chunk(sg, wt, bw, 128, 16, hv, [b_h], spl1)
                    tt, btt = nextT()
                    AC.op(lambda: nc.scalar.activation(out=cols_view(tt[:, 0:n1], spl1), in_=slot_view(sg, spl1), func=AF.Sigmoid),
                          reads=bg, writes=[btt])
                    for si, (a, b) in enumerate(spl1):
                        pa, pb = a, min(b, NCP)
                        if pb > pa:
                            DV.op(lambda si=si, a=a, pa=pa, pb=pb: nc.vector.tensor_tensor(
                                out=uv[:, i, 30 + pa:30 + pb], in0=slot_ap(sa, si, b - a)[:, pa - a:pb - a],
                                in1=tt[:, pa - p1_0:pb - p1_0], op=ALU.mult), reads=ba + [btt], writes=[b_u[i]])
                        if has_s and b > NCP:
                            DV.op(lambda si=si, a=a, b=b: nc.vector.tensor_tensor(
                                out=usv[:, i, :, 30:38],
                                in0=slot_ap(sa, si, b - a)[:, NCP - a:NCP - a + 128].rearrange("p (s t) -> p s t", s=16),
                                in1=tt[:, NCP - p1_0:NCP - p1_0 + 128].rearrange("p (s t) -> p s t", s=16),
                                op=ALU.mult), reads=ba + [btt], writes=[b_u[i]])
                for i in range(4):
                    wt, bw = w_next()
                    for cj in range(2):
                        c = 2 * i + cj
                        s_ = next_slot()
                        bq_ = mm_chunk(s_, wt, bw, cj * 128, 16, hv, [b_h], spl1)
                        AC.op(lambda c=c, s_=s_: nc.scalar.activation(out=cols_view(qv[:, c, p1_0:NC], spl1), in_=slot_view(s_, spl1),
                                                                    func=AF.Copy, scale=0.125), reads=bq_, writes=[b_q[c]])
                wt, bw = w_next()
                wv_ = wt[:].rearrange("p (k n) -> p k n", k=16)
                for cj in range(2):
                    s_ = next_slot()
                    bk_ = mm_chunk(s_, wt, bw, cj * 128, 16, hv, [b_h], spl1)
                    AC.op(lambda cj=cj, s_=s_: nc.scalar.activation(out=cols_view(kv_[:, cj, 128 + p1_0:128 + NC], spl1),
                                                                  in_=slot_view(s_, spl1), func=AF.Copy), reads=bk_, writes=[b_k[cj]])

                def tokmajor(blk, wv_, bw, out_d, also_v=None):
                    for k in range(16):
                        PE.op(lambda k=k: nc.tensor.matmul(out=ps[:, 6 * 512:6 * 512 + 256], lhsT=hv[:, k, blk * 128:(blk + 1) * 128],
                                                           rhs=wv_[:, k, :], start=(k == 0), stop=(k == 15)),
                              reads=[b_h, bw], writes=[b_bank[6]], inc=(k == 15))
                    if out_d is None:
                        AC.op(lambda: nc.scalar.activation(out=vv[:, also_v, :], in_=ps[:, 6 * 512:6 * 512 + 256], func=AF.Copy),
                              reads=[b_bank[6]], writes=[b_v[also_v]])
                    else:
                        o_, bo_ = nextT()
                        AC.op(lambda: nc.scalar.activation(out=o_[:, 0:256], in_=ps[:, 6 * 512:6 * 512 + 256], func=AF.Copy),
                              reads=[b_bank[6]], writes=[bo_])
                        if also_v is not None:
                            DV.op(lambda: nc.vector.tensor_copy(out=vv[:, also_v, :], in_=o_[:, 0:256]), reads=[bo_], writes=[b_v[also_v]])
                        SP.dma(ds_out, out_d, o_[:, 0:256], reads=[bo_])

                if tile == 1 and not (SKIPO & 1):
                    tokmajor(3, wv_, bw, kvp_o[l, 0])
                    tokmajor(4, wv_, bw, kvsn_o[l, 0])
                wt, bw = w_next()
                wv_ = wt[:].rearrange("p (k n) -> p k n", k=16)
                for blk in range(p1_0 // 128, NC // 128):
                    od = None
                    if tile == 1 and blk == 3:
                        od = kvp_o[l, 1]
                    if tile == 1 and blk == 4:
                        od = kvsn_o[l, 1]
                    if SKIPO & 2:
                        od = None
                    tokmajor(blk, wv_, bw, od, also_v=blk + 1)

                if tile == 1 and (SKIPO & 4):
                    pass
                elif tile == 1:
                    t_, bt_ = nextT()
                    DV.op(lambda: nc.vector.tensor_copy(out=t_[:, 0:240].rearrange("p (k n) -> p k n", k=8), in_=uv[:, :, 30 + 482:30 + 512]),
                          reads=b_u, writes=[bt_])
                    SP.dma(ds_out, convp_o[l], t_[:, 0:240], reads=[bt_])
                    for hh in range(2):
                        t_, bt_ = nextT()
                        DV.op(lambda hh=hh, t_=t_: nc.vector.tensor_copy(
                            out=t_[:, 0:512].rearrange("p (k s t) -> p k s t", k=4, s=16), in_=usv[:, 4 * hh:4 * hh + 4, :, 30:38]),
                            reads=b_u, writes=[bt_])
                        SP.dma(ds_out, convsn_o[l][:, 512 * hh:512 * hh + 512], t_[:, 0:512], reads=[bt_])
                else:
                    DV.op(lambda: nc.vector.tensor_copy(out=ucar[l][:].rearrange("p (k n) -> p k n", k=8), in_=uv[:, :, 30 + 866:30 + 896]),
                          reads=b_u, writes=[b_ucar[l]])
                    DV.op(lambda: nc.vector.tensor_copy(out=kcar[l][:].rearrange("p (k n) -> p k n", k=2), in_=kv_[:, :, 128 + 768:128 + 896]),
                          reads=b_k, writes=[b_kcar[l]])
                    DV.op(lambda: nc.vector.tensor_copy(out=vcar[l][:], in_=vv[:, 7, :]), reads=[b_v[7]], writes=[b_vcar[l]])

                if KSTOP == 2 + 10 * tile and l == 0:
                    raise _Stop()
                cc0 = 128 * b_0
                ncv = NC - cc0
                npc = NCP - cc0
                splc = splits(cc0, NC)
                s1 = next_slot()
                s2 = next_slot()
                bk1 = [b_bank[2 * s1], b_bank[2 * s1 + 1]][:len(splc)]
                bk2 = [b_bank[2 * s2], b_bank[2 * s2 + 1]][:len(splc)]
                for cp in range(4):
                    chs = (2 * cp, 2 * cp + 1)
                    accl = [nextT(), nextT()]
                    for ai, ch in enumerate(chs):
                        (acc, bacc), cwo = accl[ai], po + O_CW + ch * 31
                        DV.op(lambda: nc.vector.tensor_scalar(out=acc[:, 0:npc], in0=uv[:, ch, cc0:cc0 + npc], scalar1=pcol(cwo),
                                                             scalar2=pcol(po + O_CB + ch), op0=ALU.mult, op1=ALU.add),
                              reads=[b_u[ch], b_par], writes=[bacc])
                    for j in range(1, 31):
                        for ai, ch in enumerate(chs):
                            (acc, bacc), cwo = accl[ai], po + O_CW + ch * 31
                            DV.op(lambda: nc.vector.scalar_tensor_tensor(out=acc[:, 0:npc], in0=uv[:, ch, cc0 + j:cc0 + j + npc], scalar=pcol(cwo + j),
                                                                         in1=acc[:, 0:npc], op0=ALU.mult, op1=ALU.add),
                                  reads=[b_u[ch], bacc], writes=[bacc])
                    if has_s:
                        for ai, ch in enumerate(chs):
                            (acc, bacc), cwo = accl[ai], po + O_CW + ch * 31
                            accs = acc[:, npc:npc + 128].rearrange("p (s t) -> p s t", s=16)
                            DV.op(lambda: nc.vector.tensor_scalar(out=accs, in0=usv[:, ch, :, 0:8], scalar1=pcol(cwo),
                                                                 scalar2=pcol(po + O_CB + ch), op0=ALU.mult, op1=ALU.add),
                                  reads=[b_u[ch], b_par], writes=[bacc])
                        for j in range(1, 31):
                            for ai, ch in enumerate(chs):
                                (acc, bacc), cwo = accl[ai], po + O_CW + ch * 31
                                accs = acc[:, npc:npc + 128].rearrange("p (s t) -> p s t", s=16)
                                DV.op(lambda: nc.vector.scalar_tensor_tensor(out=accs, in0=usv[:, ch, :, j:j + 8], scalar=pcol(cwo + j),
                                                                             in1=accs, op0=ALU.mult, op1=ALU.add),
                                      reads=[b_u[ch], bacc], writes=[bacc])
                    for ai, ch in enumerate(chs):
                        acc, bacc = accl[ai]
                        AC.op(lambda: nc.scalar.activation(out=cv_[:, ch, cc0:NC], in_=acc[:, 0:ncv], func=AF.Copy), reads=[bacc], writes=[b_c[ch]])
                        i = rr["sq"] % 2
                        rr["sq"] += 1
                        AC.op(lambda: nc.scalar.activation(out=sqs[i][:, 0:ncv], in_=acc[:, 0:ncv], func=AF.Square), reads=[bacc], writes=b_sq[i])
                        for si, (a_, b_) in enumerate(splc):
                            PE.op(lambda: nc.tensor.matmul(out=slot_ap(s1, si, b_ - a_), lhsT=ones[:], rhs=cv_[:, ch, a_:b_],
                                                           start=(ch == 0), stop=(ch == 7)), reads=[b_c[ch], b_id], writes=bk1, inc=True)
                            PE.op(lambda: nc.tensor.matmul(out=slot_ap(s2, si, b_ - a_), lhsT=ones[:], rhs=sqs[i][:, a_ - cc0:b_ - cc0],
                                                           start=(ch == 0), stop=(ch == 7)), reads=b_sq[i] + [b_id], writes=bk2, inc=True)
                mu, bmu = nextT()
                rs_, brs = nextT()
                for si, (a, b) in enumerate(splc):
                    AC.op(lambda si=si, a=a, b=b: nc.scalar.activation(out=mu[:, a - cc0:b - cc0], in_=slot_ap(s1, si, b - a), func=AF.Copy, scale=1.0 / 1024),
                          reads=bk1, writes=[bmu])
                    DV.op(lambda si=si, a=a, b=b: nc.vector.tensor_tensor(out=rs_[:, a - cc0:b - cc0], in0=mu[:, a - cc0:b - cc0], in1=mu[:, a - cc0:b - cc0], op=ALU.mult),
                          reads=[bmu], writes=[brs])
                    DV.op(lambda si=si, a=a, b=b: nc.vector.scalar_tensor_tensor(out=rs_[:, a - cc0:b - cc0], in0=slot_ap(s2, si, b - a), scalar=1.0 / 1024,
                                                                                 in1=rs_[:, a - cc0:b - cc0], op0=ALU.mult, op1=ALU.subtract),
                          reads=bk2 + [brs], writes=[brs])
                AC.op(lambda: nc.scalar.activation(out=rs_[:, 0:ncv], in_=rs_[:, 0:ncv], func=AF.Sqrt, bias=epsb[:, 0:1], scale=1.0), reads=[brs, b_id], writes=[brs])
                DV.op(lambda: nc.vector.reciprocal(out=rs_[:, 0:ncv], in_=rs_[:, 0:ncv]), reads=[brs], writes=[brs])
                for ch in range(8):
                    t_, bt_ = nextT()
                    while t_ is mu or t_ is rs_:
                        t_, bt_ = nextT()
                    DV.op(lambda: nc.vector.tensor_tensor(out=t_[:, 0:ncv], in0=cv_[:, ch, cc0:NC], in1=mu[:, 0:ncv], op=ALU.subtract),
                          reads=[b_c[ch], bmu], writes=[bt_])
                    DV.op(lambda: nc.vector.tensor_tensor(out=t_[:, 0:ncv], in0=t_[:, 0:ncv], in1=rs_[:, 0:ncv], op=ALU.mult),
                          reads=[bt_, brs], writes=[bt_])
                    AC.op(lambda: nc.scalar.activation(out=cv_[:, ch, cc0:NC], in_=t_[:, 0:ncv], func=AF.Silu,
                                                       bias=pcol(po + O_LB + ch), scale=pcol(po + O_LG + ch)),
                          reads=[bt_, b_par], writes=[b_c[ch]])

                if KSTOP == 3 + 10 * tile and l == 0:
                    raise _Stop()
                def attention(blk, is_samp):
                    q0 = blk * 128
                    tabi = 2 if is_samp else (1 if (tile == 0 and blk == 3) else 0)
                    dt_ = dtab[:, tabi * 256:(tabi + 1) * 256]

                    def pair(c):
                        kc = 0 if c < 4 else 1
                        par_ = rr["at"] % 2
                        rr["at"] += 1
                        sbk = [b_S[0][par_], b_S[1][par_]]
                        sofs = [0 * 512 + par_ * 256, 1 * 512 + par_ * 256]
                        for hf in range(2):
                            pr = slice(hf * 64, hf * 64 + 64)
                            if not is_samp:
                                PE.op(lambda hf=hf, pr=pr: nc.tensor.matmul(out=ps[:, sofs[hf]:sofs[hf] + 256], lhsT=qv[pr, c, q0:q0 + 128],
                                                                          rhs=kv_[pr, kc, q0:q0 + 256], start=True, stop=True),
                                      reads=[b_q[c], b_k[kc]], writes=[sbk[hf]])
                            else:
                                ckv = ckT[:].rearrange("p (k s n) -> p k s n", k=2, s=16)
                                for s in range(16):
                                    PE.op(lambda s=s, pr=pr: nc.tensor.matmul(out=ps[:, 6 * 512 + s * 8:6 * 512 + s * 8 + 8], lhsT=ckv[pr, kc, s, :],
                                                                            rhs=qv[pr, c, q0 + s * 8:q0 + s * 8 + 8], start=True, stop=True),
                                          reads=[b_q[c], b_ckT], writes=[b_bank[6]], inc=(s == 15))
                                DV.op(lambda: nc.vector.tensor_copy(out=stf[:], in_=ps[:, 6 * 512:6 * 512 + 128]), reads=[b_bank[6]], writes=[b_stf])
                                PE.op(lambda hf=hf: nc.tensor.transpose(out=ps[:, sofs[hf]:sofs[hf] + 128], in_=stf[:], identity=identf[:]),
                                      reads=[b_stf, b_id], writes=[sbk[hf]])
                                PE.op(lambda hf=hf, pr=pr: nc.tensor.matmul(out=ps[:, sofs[hf] + 128:sofs[hf] + 256], lhsT=qv[pr, c, q0:q0 + 128],
                                                                          rhs=kv_[pr, kc, 128 + q0:128 + q0 + 128], start=True, stop=True),
                                      reads=[b_q[c], b_k[kc]], writes=[sbk[hf]])
                        sbt, bsb = sbs[par_], b_sb[par_]
                        smt, bsm = sm[par_], b_sm[par_]
                        for hf in range(2):
                            h = PAIRS[c][hf]
                            DV.op(lambda hf=hf, h=h: nc.vector.scalar_tensor_tensor(out=sbt[:, hf * 256:hf * 256 + 256], in0=dt_, scalar=-SLOPES[h],
                                                                                    in1=ps[:, sofs[hf]:sofs[hf] + 256], op0=ALU.mult, op1=ALU.add),
                                  reads=[sbk[hf], b_dtab], writes=[bsb])
                        sk = pcol(po + O_SK + 2 * c, 2)
                        DV.op(lambda: nc.vector.tensor_reduce(out=smt[:, 0:2], in_=sbt[:].rearrange("p (h n) -> p h n", h=2), axis=AX.X, op=ALU.max),
                              reads=[bsb], writes=[bsm])
                        DV.op(lambda: nc.vector.tensor_tensor(out=smt[:, 0:2], in0=smt[:, 0:2], in1=sk, op=ALU.max), reads=[bsm, b_par], writes=[bsm])
                        DV.op(lambda: nc.vector.tensor_scalar(out=smt[:, 2:4], in0=smt[:, 0:2], scalar1=-1.0, scalar2=None, op0=ALU.mult), reads=[bsm], writes=[bsm])
                        DV.op(lambda: nc.vector.tensor_tensor(out=smt[:, 4:6], in0=sk, in1=smt[:, 0:2], op=ALU.subtract), reads=[bsm, b_par], writes=[bsm])
                        yield
                        pbt, bpb = pbs[par_], b_p[par_]
                        for hf in range(2):
                            AC.op(lambda hf=hf: nc.scalar.activation(out=sbt[:, hf * 256:hf * 256 + 256], in_=sbt[:, hf * 256:hf * 256 + 256], func=AF.Exp,
                                                                     bias=smt[:, 2 + hf:3 + hf], scale=1.0, accum_out=smt[:, 8 + hf:9 + hf]),
                                  reads=[bsb, bsm], writes=[bsb, bsm])
                        AC.op(lambda: nc.scalar.activation(out=smt[:, 6:8], in_=smt[:, 4:6], func=AF.Exp), reads=[bsm], writes=[bsm])
                        DV.op(lambda: nc.vector.tensor_tensor(out=smt[:, 10:12], in0=smt[:, 8:10], in1=smt[:, 6:8], op=ALU.add), reads=[bsm], writes=[bsm])
                        DV.op(lambda: nc.vector.reciprocal(out=smt[:, 12:14], in_=smt[:, 10:12]), reads=[bsm], writes=[bsm])
                        for hf in range(2):
                            DV.op(lambda hf=hf: nc.vector.tensor_scalar(out=pbt[:, hf * 256:hf * 256 + 256], in0=sbt[:, hf * 256:hf * 256 + 256],
                                                                        scalar1=smt[:, 12 + hf:13 + hf], scalar2=None, op0=ALU.mult),
                                  reads=[bsb, bsm], writes=[bpb])
                        for j in range(4):
                            PE.op(lambda j=j: nc.tensor.transpose(out=pt[:, j * 128:(j + 1) * 128], in_=pbt[:, j * 128:(j + 1) * 128], identity=identb[:]),
                                  reads=[bpb, b_id], writes=[b_pt], inc=(j == 3))
                        pTt, bpT = pTs[par_], b_pT[par_]
                        AC.op(lambda: nc.scalar.activation(out=pTt, in_=pt[:, 0:512], func=AF.Copy), reads=[b_pt], writes=[bpT])
                        abank = 2 + (c % 4)
                        aof = abank * 512
                        for hf in range(2):
                            kvh = (0 if c < 4 else 2) + hf
                            pr = slice(hf * 64, hf * 64 + 64)
                            if not is_samp:
                                PE.op(lambda hf=hf, pr=pr, kvh=kvh: nc.tensor.matmul(out=ps[pr, aof:aof + 128], lhsT=vv[:, blk, kvh * 64:kvh * 64 + 64],
                                                                                   rhs=pTt[:, (2 * hf) * 128:(2 * hf + 1) * 128], start=True, stop=False),
                                      reads=[b_v[blk], bpT], writes=[b_bank[abank]], inc=False)
                                PE.op(lambda hf=hf, pr=pr, kvh=kvh: nc.tensor.matmul(out=ps[pr, aof:aof + 128], lhsT=vv[:, blk + 1, kvh * 64:kvh * 64 + 64],
                                                                                   rhs=pTt[:, (2 * hf + 1) * 128:(2 * hf + 2) * 128], start=False, stop=True),
                                      reads=[b_v[blk + 1], bpT], writes=[b_bank[abank]], inc=True)
                            else:
                                cvv = cvt[:].rearrange("p (s d) -> p s d", s=16)
                                PE.op(lambda hf=hf, pr=pr, kvh=kvh: nc.tensor.matmul(out=ps[pr, aof:aof + 128], lhsT=vv[:, blk + 1, kvh * 64:kvh * 64 + 64],
                                                                                   rhs=pTt[:, (2 * hf + 1) * 128:(2 * hf + 2) * 128], start=True, stop=False),
                                      reads=[b_v[blk + 1], bpT], writes=[b_bank[abank]], inc=False)
                                for s in range(16):
                                    PE.op(lambda hf=hf, pr=pr, kvh=kvh, s=s: nc.tensor.matmul(
                                        out=ps[pr, aof + s * 8:aof + s * 8 + 8], lhsT=cvv[:, s, kvh * 64:kvh * 64 + 64],
                                        rhs=pTt[:, (2 * hf) * 128 + s * 8:(2 * hf) * 128 + s * 8 + 8], start=False, stop=(s == 15)),
                                        reads=[b_cv, bpT], writes=[b_bank[abank]], inc=(s == 15))
                        AC.op(lambda: nc.scalar.activation(out=av[:, c, q0:q0 + 128], in_=ps[:, aof:aof + 128], func=AF.Copy),
                              reads=[b_bank[abank]], writes=[b_a[c]])

                    gens = [pair(c) for c in range(8)]
                    next(gens[0])
                    for c in range(1, 8):
                        next(gens[c])
                        for _ in gens[c - 1]:
                            pass
                    for _ in gens[7]:
                        pass

                b_S = [[Buf("S00"), Buf("S01")], [Buf("S10"), Buf("S11")]]
                alias(b_S[0] + b_S[1], [b_bank[0], b_bank[1]])
                for blk in range(b_0, NBLK):
                    attention(blk, False)
                if has_s:
                    attention(4, True)
                alias([b_bank[0], b_bank[1]], b_S[0] + b_S[1])

                if KSTOP == 4 + 10 * tile and l == 0:
                    raise _Stop()
                if DEBUG and tile == 1 and l == 0:
                    GP.dma(ds_dbg, dbg_o["dbg_c"], cv_, reads=b_c)
                    GP.dma(ds_dbg, dbg_o["dbg_a"], av, reads=b_a)
                    GP.dma(ds_dbg, dbg_o["dbg_q"], qv, reads=b_q)
                    GP.dma(ds_dbg, dbg_o["dbg_k"], kv_[:, :, 128:128 + NC], reads=b_k)
                alias(b_m, b_u + b_q)

                splf = splits(c0, NC)
                nf = NC - c0
                for jj in range(8):
                    for cj in range(2):
                        j = 2 * jj + cj
                        wgc, bwgc = w_next(keep=cj)
                        wga, bwga = wgc, bwgc
                        s_ = next_slot()
                        bb_ = mm_chunk(s_, wgc, bwgc, 0, 16, hv, [b_h], splf)
                        t1, bt1 = nextT()
                        AC.op(lambda: nc.scalar.activation(out=cols_view(t1[:, 0:nf], splf), in_=slot_view(s_, splf), func=AF.Sigmoid), reads=bb_, writes=[bt1])
                        s_ = next_slot()
                        bb_ = mm_chunk(s_, wga, bwga, 128, 16, hv, [b_h], splf)
                        t2, bt2 = nextT()
                        AC.op(lambda: nc.scalar.activation(out=cols_view(t2[:, 0:nf], splf), in_=slot_view(s_, splf), func=AF.Sigmoid), reads=bb_, writes=[bt2])
                        if cj == 0:
                            wco, bwco = w_next()
                        s_ = next_slot()
                        bb_ = mm_chunk(s_, wco, bwco, cj * 128, 8, cv_, b_c, splf, kslot0=0)
                        DV.op(lambda: nc.vector.tensor_tensor(out=cols_view(t1[:, 0:nf], splf), in0=slot_view(s_, splf), in1=cols_view(t1[:, 0:nf], splf), op=ALU.mult),
                              reads=bb_ + [bt1], writes=[bt1])
                        s_ = next_slot()
                        bb_ = mm_chunk(s_, wco, bwco, cj * 128, 8, av, b_a, splf, kslot0=8)
                        DV.op(lambda: nc.vector.tensor_tensor(out=cols_view(t2[:, 0:nf], splf), in0=slot_view(s_, splf), in1=cols_view(t2[:, 0:nf], splf), op=ALU.mult),
                              reads=bb_ + [bt2], writes=[bt2])
                        DV.op(lambda: nc.vector.tensor_tensor(out=mv[:, j, c0:NC], in0=t1[:, 0:nf], in1=t2[:, 0:nf], op=ALU.add),
                              reads=[bt1, bt2], writes=[b_m[j]])

                def resid_add(j, s_, bb_):
                    DV.op(lambda: nc.vector.tensor_tensor(out=cols_view(xv[:, j, c0:NC], splf), in0=slot_view(s_, splf), in1=cols_view(xv[:, j, c0:NC], splf), op=ALU.add),
                          reads=bb_ + [b_x[j]], writes=[b_x[j]])

                def warm_mask(j):
                    if tile == 0 and c0 < 384:
                        DV.op(lambda: nc.vector.tensor_scalar(out=xv[:, j, c0:384], in0=xv[:, j, c0:384], scalar1=vmask[:, 0:1], scalar2=None, op0=ALU.mult),
                              reads=[b_x[j], b_vmask], writes=[b_x[j]])

                for jj in range(8):
                    wt, bw = w_next()
                    for cj in range(2):
                        j = 2 * jj + cj
                        s_ = next_slot()
                        bb_ = mm_chunk(s_, wt, bw, cj * 128, 16, mv, b_m, splf)
                        resid_add(j, s_, bb_)
                        warm_mask(j)

                if DEBUG and tile == 1 and l == 0:
                    GP.dma(ds_dbg, dbg_o["dbg_m"], mv, reads=b_m)
                    GP.dma(ds_dbg, dbg_o["dbg_x"], xv, reads=b_x)
                if KSTOP == 5 + 10 * tile and l == 0:
                    raise _Stop()
                rmsnorm(po + O_G2, c0, hv, b_h)
                if DEBUG and tile == 1 and l == 0:
                    GP.dma(ds_dbg, dbg_o["dbg_h2"], hv, reads=[b_h])

                npf = NCP - c0
                for gi in range(3):
                    for i in range(16):
                        wt, bw = w_next()
                        accs_ = []
                        for cj in range(2):
                            chn = (gi * 16 + i) + 48 * cj
                            s_ = next_slot()
                            bb_ = mm_chunk(s_, wt, bw, cj * 128, 16, hv, [b_h], splf)
                            acc, bacc = nextT()
                            fwo = po + O_FW + chn * 3
                            AC.op(lambda: nc.scalar.activation(out=cols_view(acc[:, 0:nf], splf), in_=slot_view(s_, splf), func=AF.Identity,
                                                               bias=pcol(po + O_FB + chn), scale=pcol(fwo + 2)), reads=bb_ + [b_par], writes=[bacc])

                            def pcols(a, b):
                                out = []
                                for si, (sa_, sb_) in enumerate(splf):
                                    lo, hi = max(a, sa_), min(b, sb_)
                                    if hi > lo:
                                        out.append((si, sa_, lo, hi))
                                return out
                            for tap, wi in ((1, 1), (2, 0)):
                                for (si, sa_, lo, hi) in pcols(c0, NCP - tap):
                                    DV.op(lambda si=si, sa_=sa_, lo=lo, hi=hi, tap=tap, wi=wi: nc.vector.scalar_tensor_tensor(
                                        out=acc[:, lo + tap - c0:hi + tap - c0], in0=slot_ap(s_, si, 512)[:, lo - sa_:hi - sa_], scalar=pcol(fwo + wi),
                                        in1=acc[:, lo + tap - c0:hi + tap - c0], op0=ALU.mult, op1=ALU.add), reads=bb_ + [bacc, b_par], writes=[bacc])
                            if tile == 1:
                                fc = fcar[l][:].rearrange("p (c t) -> p c t", t=2)
                                DV.op(lambda: nc.vector.scalar_tensor_tensor(out=acc[:, 0:1], in0=fc[:, chn, 1:2], scalar=pcol(fwo + 1), in1=acc[:, 0:1],
                                                                             op0=ALU.mult, op1=ALU.add), reads=[bacc, b_fcar[l], b_par], writes=[bacc])
                                DV.op(lambda: nc.vector.scalar_tensor_tensor(out=acc[:, 0:2], in0=fc[:, chn, 0:2], scalar=pcol(fwo + 0), in1=acc[:, 0:2],
                                                                             op0=ALU.mult, op1=ALU.add), reads=[bacc, b_fcar[l], b_par], writes=[bacc])
                            if has_s:
                                si_s = len(splf) - 1
                                sa_ = splf[si_s][0]
                                assert sa_ <= NCP
                                pss = slot_ap(s_, si_s, 512)[:, NCP - sa_:NCP - sa_ + 128].rearrange("p (s t) -> p s t", s=16)
                                acs = acc[:, NCP - c0:NCP - c0 + 128].rearrange("p (s t) -> p s t", s=16)
                                sfv = sffn[:].rearrange("p (c s t) -> p c s t", c=96, s=16)
                                DV.op(lambda: nc.vector.scalar_tensor_tensor(out=acs[:, :, 1:8], in0=pss[:, :, 0:7], scalar=pcol(fwo + 1), in1=acs[:, :, 1:8],
                                                                             op0=ALU.mult, op1=ALU.add), reads=bb_ + [bacc, b_par], writes=[bacc])
                                DV.op(lambda: nc.vector.scalar_tensor_tensor(out=acs[:, :, 2:8], in0=pss[:, :, 0:6], scalar=pcol(fwo + 0), in1=acs[:, :, 2:8],
                                                                             op0=ALU.mult, op1=ALU.add), reads=bb_ + [bacc, b_par], writes=[bacc])
                                DV.op(lambda: nc.vector.scalar_tensor_tensor(out=acs[:, :, 0:1], in0=sfv[:, chn, :, 1:2], scalar=pcol(fwo + 1), in1=acs[:, :, 0:1],
                                                                             op0=ALU.mult, op1=ALU.add), reads=[bacc, b_sffn, b_par], writes=[bacc])
                                DV.op(lambda: nc.vector.scalar_tensor_tensor(out=acs[:, :, 0:2], in0=sfv[:, chn, :, 0:2], scalar=pcol(fwo + 0), in1=acs[:, :, 0:2],
                                                                             op0=ALU.mult, op1=ALU.add), reads=[bacc, b_sffn, b_par], writes=[bacc])
                                AC.op(lambda: nc.scalar.activation(out=sfv[:, chn, :, :], in_=pss[:, :, 6:8], func=AF.Copy), reads=bb_ + [bacc], writes=[b_sffn])
                            si_p = [si for si, (sa2, sb2) in enumerate(splf) if sa2 < NCP][-1]
                            sa2 = splf[si_p][0]
                            fc = fcar[l][:].rearrange("p (c t) -> p c t", t=2)
                            AC.op(lambda: nc.scalar.activation(out=fc[:, chn, :], in_=slot_ap(s_, si_p, 512)[:, NCP - 2 - sa2:NCP - sa2], func=AF.Copy),
                                  reads=bb_ + [bacc], writes=[b_fcar[l]])
                            accs_.append((acc, bacc))
                        (ag, bag), (avl, bavl) = accs_
                        AC.op(lambda: nc.scalar.activation(out=ag[:, 0:nf], in_=ag[:, 0:nf], func=AF.Silu), reads=[bag], writes=[bag])
                        DV.op(lambda: nc.vector.tensor_tensor(out=mv[:, i, c0:NC], in0=ag[:, 0:nf], in1=avl[:, 0:nf], op=ALU.mult),
                              reads=[bag, bavl], writes=[b_m[i]])
                    for jj in range(8):
                        wt, bw = w_next()
                        for cj in range(2):
                            j = 2 * jj + cj
                            s_ = next_slot()
                            bb_ = mm_chunk(s_, wt, bw, cj * 128, 16, mv, b_m, splf)
                            resid_add(j, s_, bb_)
                            if gi == 2:
                                warm_mask(j)
                if KSTOP == 6 + 10 * tile and l == 0:
                    raise _Stop()
                if KSTOP == 7 + 10 * tile and l == 1:
                    raise _Stop()
                if tile == 1:
                    SP.dma(ds_out, ffnp_o[l], fcar[l][:], reads=[b_fcar[l]])
                    SP.dma(ds_out, ffns_o[l], sffn[:], reads=[b_sffn])

            if tile == 0:
                rmsnorm(2 * PL, 384, None, None, is_final=True, ycol0=0)
            else:
                rmsnorm(2 * PL, 0, None, None, is_final=True, ycol0=512)
            barrier()

    epsb = sb("epsb", [128, 1], F32)
    DV.op(lambda: nc.vector.memset(epsb[:], EPS), writes=[b_id])
    try:
        run_tile(0)
        if KSTOP == 8:
            raise _Stop()
        run_tile(1)
    except _Stop:
        barrier()
    for d in (ds_out, ds_copy):
        nc.sync.wait_ge(d.sem, d.val)
    return nc


def _unit(W, rows, cols):
    U = np.zeros((128, 16, 256), np.float32)
    for k, r in enumerate(rows):
        if r is None:
            continue
        if isinstance(r, np.ndarray):
            U[:, k, :] = W[r][:, cols]
        else:
            U[:, k, :] = W[r:r + 128][:, cols]
    return U.reshape(128, 4096)


def _pack_layer(w_in, w_co, w_ao, w_out, w_up, w_down):
    units = []
    r16 = [k * 128 for k in range(16)]
    ar = np.arange
    for i in range(8):
        units.append(_unit(w_in, r16, np.concatenate([ar(i * 128, i * 128 + 128), ar(1024 + i * 128, 1024 + i * 128 + 128)])))
    for i in range(4):
        cols = []
        for c in (2 * i, 2 * i + 1):
            for h in PAIRS[c]:
                cols.append(ar(2048 + h * 64, 2048 + h * 64 + 64))
        units.append(_unit(w_in, r16, np.concatenate(cols)))
    units.append(_unit(w_in, r16, ar(3072, 3328)))
    units.append(_unit(w_in, r16, ar(3328, 3584)))
    arow = []
    for c in range(8):
        arow.append(np.concatenate([ar(h * 64, h * 64 + 64) for h in PAIRS[c]]))
    for jj in range(8):
        j0, j1 = 2 * jj, 2 * jj + 1
        units.append(_unit(w_in, r16, np.concatenate([ar(3584 + j0 * 128, 3584 + j0 * 128 + 128), ar(5632 + j0 * 128, 5632 + j0 * 128 + 128)])))
        U = np.zeros((128, 16, 256), np.float32)
        cols = ar(jj * 256, jj * 256 + 256)
        for k in range(8):
            U[:, k, :] = w_co[k * 128:(k + 1) * 128][:, cols]
            U[:, 8 + k, :] = w_ao[arow[k]][:, cols]
        units.append(U.reshape(128, 4096))
        units.append(_unit(w_in, r16, np.concatenate([ar(3584 + j1 * 128, 3584 + j1 * 128 + 128), ar(5632 + j1 * 128, 5632 + j1 * 128 + 128)])))
    for j in range(8):
        units.append(_unit(w_out, r16, ar(j * 256, j * 256 + 256)))
    for gi in range(3):
        for i in range(16):
            c = gi * 16 + i
            units.append(_unit(w_up, r16, np.concatenate([ar(c * 128, c * 128 + 128), ar(6144 + c * 128, 6144 + c * 128 + 128)])))
        for j in range(8):
            units.append(_unit(w_down, [(gi * 16 + k) * 128 for k in range(16)], ar(j * 256, j * 256 + 256)))
    assert len(units) == NU
    return np.stack(units)


def _fm(v):
    return np.ascontiguousarray(v.reshape(-1, 128).T)


_NC_CACHE = {}
_PREP_ONLY = False


def kernel(x_prompt, x_sample, cache_k, cache_v, state_conv, state_ffn_conv,
           norm1_g, w_in, conv_w, conv_b, conv_ln_g, conv_ln_b, w_conv_out,
           attn_sinks, w_attn_out, w_out, norm2_g, w_up, ffn_conv_w, ffn_conv_b,
           w_down, final_norm_g):
    f = np.float32
    A = lambda a: np.asarray(a, dtype=f)
    x_prompt, x_sample = A(x_prompt), A(x_sample)
    cache_k, cache_v, state_conv, state_ffn_conv = A(cache_k), A(cache_v), A(state_conv), A(state_ffn_conv)
    wp = [_pack_layer(A(w_in[l]), A(w_conv_out[l]), A(w_attn_out[l]), A(w_out[l]), A(w_up[l]), A(w_down[l])) for l in range(L)]
    par = np.zeros((128, 2 * PL + 16), f)
    for l in range(L):
        o = l * PL
        par[:, o + O_G1:o + O_G1 + 16] = _fm(A(norm1_g[l]))
        par[:, o + O_G2:o + O_G2 + 16] = _fm(A(norm2_g[l]))
        cw = A(conv_w[l])
        par[:, o + O_CW:o + O_CW + 248] = cw.T.reshape(8, 128, 31).transpose(1, 0, 2).reshape(128, 248)
        par[:, o + O_CB:o + O_CB + 8] = _fm(A(conv_b[l]))
        par[:, o + O_LG:o + O_LG + 8] = _fm(A(conv_ln_g[l]))
        par[:, o + O_LB:o + O_LB + 8] = _fm(A(conv_ln_b[l]))
        sk = A(attn_sinks[l])
        par[:, o + O_SK:o + O_SK + 16] = np.array([sk[PAIRS[c][hf]] for c in range(8) for hf in range(2)], f)[None, :]
        fw = A(ffn_conv_w[l])
        par[:, o + O_FW:o + O_FW + 288] = fw.T.reshape(96, 128, 3).transpose(1, 0, 2).reshape(128, 288)
        par[:, o + O_FB:o + O_FB + 96] = _fm(A(ffn_conv_b[l]))
    par[:, 2 * PL:2 * PL + 16] = _fm(A(final_norm_g))
    ident = np.eye(128, dtype=f)
    a_ = np.arange(128)[:, None]
    j_ = np.arange(256)[None, :]
    dist = (128 + a_ - j_).astype(f)
    gen = np.where((j_ >= a_) & (j_ <= 128 + a_), dist, BIG).astype(f)
    s_r, t_r = np.arange(128)[:, None] // 8, np.arange(128)[:, None] % 8
    jc = np.arange(128)[None, :]
    dc = np.where(jc >= t_r, (128 + t_r - jc), BIG)
    s_c, t_c = jc // 8, jc % 8
    dn = np.where((s_c == s_r) & (t_c <= t_r), (t_r - t_c), BIG)
    samp = np.concatenate([dc, dn], axis=1).astype(f)

    in_maps = []
    for c in range(8):
        b, seg = c // 4, c % 4
        s0 = 16 * c
        xx = np.zeros((1536, D), f)
        p0 = seg * 1024 - 384
        lo = max(p0, 0)
        xx[lo - p0:1408] = x_prompt[b, lo:seg * 1024 + 1024]
        xx[1408:] = x_sample[s0:s0 + 16].reshape(128, D)
        xT = np.ascontiguousarray(xx.T.reshape(16, 128, 1536).transpose(1, 0, 2))
        first = gen.copy()
        if seg == 0:
            first[:, :128] = BIG
        dt = np.concatenate([gen, first, samp], axis=1)
        vm = np.full((128, 1), 0.0 if seg == 0 else 1.0, f)
        sc = state_conv[:, s0:s0 + 16]
        scT = np.zeros((L, 128, 8, 16, 38), f)
        scT[..., :30] = sc.reshape(L, 16, 30, 8, 128).transpose(0, 4, 3, 1, 2)
        sf = state_ffn_conv[:, s0:s0 + 16]
        sfT = sf.reshape(L, 16, 2, 96, 128).transpose(0, 4, 3, 1, 2)
        ck = cache_k[:, s0:s0 + 16].reshape(L, 16, 128, 256)
        cvv = cache_v[:, s0:s0 + 16].reshape(L, 16, 128, 256)
        ckT = ck.reshape(L, 16, 128, 2, 2, 64).transpose(0, 4, 5, 3, 1, 2).reshape(L, 128, 2, 2048)
        m = {
            "xT": xT, "vmask": vm, "dtab": np.ascontiguousarray(dt), "par": par, "wp0": wp[0], "wp1": wp[1],
            "sconvT": np.ascontiguousarray(scT.reshape(L, 128, 8, 608)),
            "sffnT": np.ascontiguousarray(sfT.reshape(L, 128, 96 * 32)),
            "ckT": np.ascontiguousarray(ckT), "cv": np.ascontiguousarray(cvv), "ck": np.ascontiguousarray(ck),
            "sconvn": np.ascontiguousarray(sc), "ident": ident,
        }
        in_maps.append(m)

    if _PREP_ONLY:
        return in_maps
    if "nc" not in _NC_CACHE:
        _NC_CACHE["nc"] = build_program()
    nc = _NC_CACHE["nc"]
    res = run_bass_kernel_spmd(nc, in_maps, core_ids=list(range(8)))
    return _post(res.results)


def _post(R):
    f = np.float32
    y_prompt = np.zeros((2, 4096, D), f)
    y_sample = np.zeros((128, 8, D), f)
    k_p = np.zeros((L, 2, 128, 4, 64), f)
    v_p = np.zeros((L, 2, 128, 4, 64), f)
    conv_p = np.zeros((L, 2, 30, 1024), f)
    ffn_p = np.zeros((L, 2, 2, 12288), f)
    k_s = np.zeros((L, 128, 128, 4, 64), f)
    v_s = np.zeros((L, 128, 128, 4, 64), f)
    conv_s = np.zeros((L, 128, 30, 1024), f)
    ffn_s = np.zeros((L, 128, 2, 12288), f)
    for c in range(8):
        b, seg = c // 4, c % 4
        s0 = 16 * c
        r = R[c]
        y = np.asarray(r["yT"]).transpose(1, 0, 2).reshape(D, 1152).T
        y_prompt[b, seg * 1024:(seg + 1) * 1024] = y[:1024]
        y_sample[s0:s0 + 16] = y[1024:].reshape(16, 8, D)
        kvsn = np.asarray(r["kvsn"])
        k_s[:, s0:s0 + 16, :120] = np.asarray(r["kso"]).reshape(L, 16, 120, 4, 64)
        v_s[:, s0:s0 + 16, :120] = np.asarray(r["vso"]).reshape(L, 16, 120, 4, 64)
        k_s[:, s0:s0 + 16, 120:] = kvsn[:, 0].reshape(L, 16, 8, 4, 64)
        v_s[:, s0:s0 + 16, 120:] = kvsn[:, 1].reshape(L, 16, 8, 4, 64)
        conv_s[:, s0:s0 + 16, :22] = np.asarray(r["convso"])
        csn = np.asarray(r["convsn"]).reshape(L, 128, 8, 16, 8)
        conv_s[:, s0:s0 + 16, 22:] = csn.transpose(0, 3, 4, 2, 1).reshape(L, 16, 8, 1024)
        fs = np.asarray(r["ffns"]).reshape(L, 128, 96, 16, 2)
        ffn_s[:, s0:s0 + 16] = fs.transpose(0, 3, 4, 2, 1).reshape(L, 16, 2, 12288)
        if seg == 3:
            kvp = np.asarray(r["kvp"])
            k_p[:, b] = kvp[:, 0].reshape(L, 128, 4, 64)
            v_p[:, b] = kvp[:, 1].reshape(L, 128, 4, 64)
            cp = np.asarray(r["convp"]).reshape(L, 128, 8, 30)
            conv_p[:, b] = cp.transpose(0, 3, 2, 1).reshape(L, 30, 1024)
            fp = np.asarray(r["ffnp"]).reshape(L, 128, 96, 2)
            ffn_p[:, b] = fp.transpose(0, 3, 2, 1).reshape(L, 2, 12288)
    return (y_prompt, y_sample, k_p, v_p, conv_p, ffn_p, k_s, v_s, conv_s, ffn_s)
```
